# Optimizing a Trainium2 kernel written in Bass

```python
import math
import jax, jax.numpy as jnp
from jax import lax
import numpy as np

D_MODEL = 1024
BATCH = 4
SEQ = 4096
DEPTH = 1
DEC_BATCH = 128
DEC_SEQ = 1
PAST_LEN = 2048
PAGE_SIZE = 128

N_MEM = 256
SSM_WIDTH = D_MODEL // 2
SSM_GROUP = 16
SSM_GROUPS = SSM_WIDTH // SSM_GROUP
SSM_STATE = 64
ATT_HEAD_DIM = 64
ATT_HEADS_PER_GROUP = 4
DIL_PAIRS = ((128, 1), (512, 4), (2048, 16))
N_DIL = len(DIL_PAIRS)
ATT_HEADS = N_DIL * ATT_HEADS_PER_GROUP
ATT_WIDTH = ATT_HEADS * ATT_HEAD_DIM
ATT_OUT_WIDTH = ATT_HEADS_PER_GROUP * ATT_HEAD_DIM
MEM_HEADS = 4
MEM_HEAD_DIM = 128
MEM_WIDTH = MEM_HEADS * MEM_HEAD_DIM
N_BRANCH = 3
IN_WIDTH = SSM_WIDTH + 3 * ATT_WIDTH + MEM_WIDTH + N_BRANCH * D_MODEL
D_FF = 2816
CONV_W = 3
EPS = 1e-6

kernel_name = 'hybrid_s5_dilated_alibi_memxattn_convffn_step'


def alibi_slopes():
    return jnp.asarray(2.0 ** (-8.0 * np.arange(1, ATT_HEADS + 1) / ATT_HEADS), dtype=jnp.float32)


def rmsnorm(x, g):
    xf = x.astype(jnp.float32)
    y = xf * lax.rsqrt(jnp.mean(xf * xf, axis=-1, keepdims=True) + EPS)
    return (y * g.astype(jnp.float32)).astype(x.dtype)


def split_in(h, w_in):
    n, l, _ = h.shape
    z = h @ w_in
    offs = [SSM_WIDTH, SSM_WIDTH + ATT_WIDTH, SSM_WIDTH + 2 * ATT_WIDTH,
            SSM_WIDTH + 3 * ATT_WIDTH, SSM_WIDTH + 3 * ATT_WIDTH + MEM_WIDTH]
    u, q, k, v, qm, gl = jnp.split(z, offs, axis=-1)
    q = q.reshape(n, l, ATT_HEADS, ATT_HEAD_DIM)
    k = k.reshape(n, l, ATT_HEADS, ATT_HEAD_DIM)
    v = v.reshape(n, l, ATT_HEADS, ATT_HEAD_DIM)
    qm = qm.reshape(n, l, MEM_HEADS, MEM_HEAD_DIM)
    gates = jax.nn.sigmoid(gl.reshape(n, l, N_BRANCH, D_MODEL))
    return u, q, k, v, qm, gates


def _cplx_affine_combine(e1, e2):
    a1r, a1i, b1r, b1i = e1
    a2r, a2i, b2r, b2i = e2
    return (a2r * a1r - a2i * a1i, a2r * a1i + a2i * a1r,
            a2r * b1r - a2i * b1i + b2r, a2r * b1i + a2i * b1r + b2i)


def s5_scan(u, s0_re, s0_im, a_re, a_im, log_dt, b_re, b_im, c_re, c_im, d_skip):
    f32 = jnp.float32
    n, l, _ = u.shape
    a_re, a_im = a_re.astype(f32), a_im.astype(f32)
    dt = jnp.exp(log_dt.astype(f32))[:, None]
    mag = jnp.exp(a_re * dt)
    ab_re, ab_im = mag * jnp.cos(a_im * dt), mag * jnp.sin(a_im * dt)
    den = a_re * a_re + a_im * a_im
    q_re = ((ab_re - 1.0) * a_re + ab_im * a_im) / den
    q_im = (ab_im * a_re - (ab_re - 1.0) * a_im) / den
    b_re, b_im = b_re.astype(f32), b_im.astype(f32)
    bb_re = q_re[..., None] * b_re - q_im[..., None] * b_im
    bb_im = q_re[..., None] * b_im + q_im[..., None] * b_re
    uf = u.astype(f32)
    ug = uf.reshape(n, l, SSM_GROUPS, SSM_GROUP)
    bu_re = jnp.einsum('gpc,nlgc->nlgp', bb_re, ug)
    bu_im = jnp.einsum('gpc,nlgc->nlgp', bb_im, ug)
    ar = jnp.broadcast_to(ab_re, bu_re.shape)
    ai = jnp.broadcast_to(ab_im, bu_im.shape)
    pr, pi, hr, hi = lax.associative_scan(_cplx_affine_combine, (ar, ai, bu_re, bu_im), axis=1)
    s0r = s0_re.astype(f32)[:, None]
    s0i = s0_im.astype(f32)[:, None]
    s_re = hr + pr * s0r - pi * s0i
    s_im = hi + pr * s0i + pi * s0r
    y = (jnp.einsum('gcp,nlgp->nlgc', c_re.astype(f32), s_re)
         - jnp.einsum('gcp,nlgp->nlgc', c_im.astype(f32), s_im))
    y = y.reshape(n, l, SSM_WIDTH) + d_skip.astype(f32) * uf
    return y.astype(u.dtype), s_re[:, -1].astype(u.dtype), s_im[:, -1].astype(u.dtype)


def ssm_branch(u, s0_re, s0_im, a_re, a_im, log_dt, b_re, b_im, c_re, c_im, d_skip, w_glu):
    y, sr, si = s5_scan(u, s0_re, s0_im, a_re, a_im, log_dt, b_re, b_im, c_re, c_im, d_skip)
    z = jax.nn.gelu(y)
    ga, gb = jnp.split(z @ w_glu, 2, axis=-1)
    return ga * jax.nn.sigmoid(gb), sr, si


def dilated_prompt(q, k, v, dil, steps, slopes):
    f32 = jnp.float32
    n, l, h, e = q.shape
    sub = l // dil
    nb = -(-sub // steps)
    pad = nb * steps - sub

    def to_blocks(t):
        t = t.reshape(n, sub, dil, h, e).transpose(0, 2, 1, 3, 4).reshape(n * dil, sub, h, e)
        t = jnp.pad(t, ((0, 0), (0, pad), (0, 0), (0, 0)))
        return t.reshape(n * dil, nb, steps, h, e)

    def with_prev(t):
        prev = jnp.pad(t, ((0, 0), (1, 0), (0, 0), (0, 0), (0, 0)))[:, :-1]
        return jnp.concatenate([prev, t], axis=2)

    qb = to_blocks(q)
    kk = with_prev(to_blocks(k))
    vv = with_prev(to_blocks(v))
    s = jnp.einsum('xbqhe,xbkhe->xbhqk', qb, kk).astype(f32) * (e ** -0.5)
    qi = jnp.arange(steps)
    kj = jnp.arange(2 * steps)
    dist = qi[:, None] + steps - kj[None, :]
    valid = (dist >= 0) & (dist <= steps)
    first_ok = (jnp.arange(nb)[:, None, None] > 0) | (kj[None, None, :] >= steps)
    mask = valid[None] & first_ok
    bias = -slopes[:, None, None] * (dist * dil).astype(f32)[None]
    s = jnp.where(mask[None, :, None], s + bias[None, None], -jnp.inf)
    m = jnp.max(s, axis=-1, keepdims=True)
    p = jnp.exp(s - m)
    den = jnp.sum(p, axis=-1)
    o = jnp.einsum('xbhqk,xbkhe->xbqhe', p, vv.astype(f32)) / den.transpose(0, 1, 3, 2)[..., None]
    lse = (m[..., 0] + jnp.log(den)).transpose(0, 1, 3, 2)
    o = o.reshape(n * dil, nb * steps, h, e)[:, :sub]
    o = o.reshape(n, dil, sub, h, e).transpose(0, 2, 1, 3, 4).reshape(n, l, h, e)
    lse = lse.reshape(n * dil, nb * steps, h)[:, :sub]
    lse = lse.reshape(n, dil, sub, h).transpose(0, 2, 1, 3).reshape(n, l, h)
    return o, lse


def dilated_sample(q, k_all, v_all, n_past, dil, steps, slopes):
    f32 = jnp.float32
    t, e = q.shape[1], q.shape[3]
    step = jnp.arange(steps + 1)
    idx = n_past + jnp.arange(t)[:, None] - dil * step[None, :]
    valid = idx >= 0
    idx = jnp.maximum(idx, 0)
    kg = k_all[:, idx]
    vg = v_all[:, idx]
    s = jnp.einsum('nthe,ntshe->nths', q, kg).astype(f32) * (e ** -0.5)
    s = s - slopes[:, None] * (step * dil).astype(f32)[None, :]
    s = jnp.where(valid[None, :, None, :], s, -jnp.inf)
    m = jnp.max(s, axis=-1, keepdims=True)
    p = jnp.exp(s - m)
    den = jnp.sum(p, axis=-1)
    o = jnp.einsum('nths,ntshe->nthe', p, vg.astype(f32)) / den[..., None]
    lse = m[..., 0] + jnp.log(den)
    return o, lse


def merge_dilated(outs, lses, w_att_o, dtype):
    w = jax.nn.softmax(jnp.stack(lses, axis=0), axis=0)
    o = jnp.einsum('gnlh,gnlhe->nlhe', w, jnp.stack(outs, axis=0))
    n, l = o.shape[0], o.shape[1]
    return o.reshape(n, l, ATT_OUT_WIDTH).astype(dtype) @ w_att_o


def mem_kv(mem, g_mem, w_mem_kv):
    n = mem.shape[0]
    mk, mv = jnp.split(rmsnorm(mem, g_mem) @ w_mem_kv, 2, axis=-1)
    return (mk.reshape(n, N_MEM, MEM_HEADS, MEM_HEAD_DIM),
            mv.reshape(n, N_MEM, MEM_HEADS, MEM_HEAD_DIM))


def mem_attend(qm, mk, mv, w_mem_o):
    n, l = qm.shape[0], qm.shape[1]
    s = jnp.einsum('nlhe,nmhe->nhlm', qm, mk.astype(qm.dtype)).astype(jnp.float32) * (MEM_HEAD_DIM ** -0.5)
    p = jax.nn.softmax(s, axis=-1)
    o = jnp.einsum('nhlm,nmhe->nlhe', p, mv.astype(jnp.float32))
    return o.reshape(n, l, MEM_WIDTH).astype(qm.dtype) @ w_mem_o


def merge_branches(gates, b_ssm, b_att, b_mem, w_out):
    m = gates[:, :, 0] * b_ssm + gates[:, :, 1] * b_att + gates[:, :, 2] * b_mem
    return m @ w_out


def conv_ffn(h, conv_buf, w_up, conv_w, conv_b, w_down):
    l = h.shape[1]
    a, v = jnp.split(h @ w_up, 2, axis=-1)
    full = jnp.concatenate([conv_buf.astype(a.dtype), a], axis=1)
    c = full[:, 0:l] * conv_w[0]
    for j in range(1, CONV_W):
        c = c + full[:, j:j + l] * conv_w[j]
    c = c + conv_b
    y = (jax.nn.gelu(c) * v) @ w_down
    return y, full[:, -(CONV_W - 1):]


def setup_inputs(seed: int = 0) -> dict:
    key = jax.random.key(seed)
    ks = iter(jax.random.split(key, 48))
    f32 = jnp.float32

    def nrm(shape, scale):
        return jax.random.normal(next(ks), shape, f32) * scale

    L = DEPTH
    wl = [min(w, PAST_LEN) for w, _ in DIL_PAIRS]
    hpg, hd = ATT_HEADS_PER_GROUP, ATT_HEAD_DIM
    n_idx = jnp.arange(SSM_STATE, dtype=f32)
    return {
        'x_prompt': nrm((BATCH, SEQ, D_MODEL), 1.0),
        'x_sample': nrm((DEC_BATCH, DEC_SEQ, D_MODEL), 1.0),
        'state_ssm_re': nrm((L, DEC_BATCH, SSM_GROUPS, SSM_STATE), 0.5),
        'state_ssm_im': nrm((L, DEC_BATCH, SSM_GROUPS, SSM_STATE), 0.5),
        'cache_w1_k': nrm((L, DEC_BATCH, wl[0], hpg, hd), 1.0),
        'cache_w1_v': nrm((L, DEC_BATCH, wl[0], hpg, hd), 1.0),
        'cache_w2_k': nrm((L, DEC_BATCH, wl[1], hpg, hd), 1.0),
        'cache_w2_v': nrm((L, DEC_BATCH, wl[1], hpg, hd), 1.0),
        'cache_w3_k': nrm((L, DEC_BATCH, wl[2], hpg, hd), 1.0),
        'cache_w3_v': nrm((L, DEC_BATCH, wl[2], hpg, hd), 1.0),
        'cache_mem_k': nrm((L, DEC_BATCH, N_MEM, MEM_HEADS, MEM_HEAD_DIM), 1.0),
        'cache_mem_v': nrm((L, DEC_BATCH, N_MEM, MEM_HEADS, MEM_HEAD_DIM), 1.0),
        'state_ffn_conv': nrm((L, DEC_BATCH, CONV_W - 1, D_FF), 1.0),
        'mem_prompt': nrm((BATCH, N_MEM, D_MODEL), 1.0),
        'norm1_g': 1.0 + nrm((L, D_MODEL), 0.01),
        'w_in': nrm((L, D_MODEL, IN_WIDTH), D_MODEL ** -0.5),
        'ssm_a_re': -0.5 + nrm((L, SSM_GROUPS, SSM_STATE), 0.01),
        'ssm_a_im': math.pi * n_idx + nrm((L, SSM_GROUPS, SSM_STATE), 0.01),
        'ssm_log_dt': jax.random.uniform(next(ks), (L, SSM_GROUPS), f32, math.log(1e-3), math.log(1e-1)),
        'ssm_b_re': nrm((L, SSM_GROUPS, SSM_STATE, SSM_GROUP), (2.0 * SSM_GROUP) ** -0.5),
        'ssm_b_im': nrm((L, SSM_GROUPS, SSM_STATE, SSM_GROUP), (2.0 * SSM_GROUP) ** -0.5),
        'ssm_c_re': nrm((L, SSM_GROUPS, SSM_GROUP, SSM_STATE), SSM_STATE ** -0.5),
        'ssm_c_im': nrm((L, SSM_GROUPS, SSM_GROUP, SSM_STATE), SSM_STATE ** -0.5),
        'ssm_d': 1.0 + nrm((L, SSM_WIDTH), 0.1),
        'w_ssm_glu': nrm((L, SSM_WIDTH, 2 * D_MODEL), SSM_WIDTH ** -0.5),
        'w_att_o': nrm((L, ATT_OUT_WIDTH, D_MODEL), ATT_OUT_WIDTH ** -0.5),
        'mem_norm_g': 1.0 + nrm((L, D_MODEL), 0.01),
        'w_mem_kv': nrm((L, D_MODEL, 2 * MEM_WIDTH), D_MODEL ** -0.5),
        'w_mem_o': nrm((L, MEM_WIDTH, D_MODEL), MEM_WIDTH ** -0.5),
        'w_out': nrm((L, D_MODEL, D_MODEL), D_MODEL ** -0.5),
        'norm2_g': 1.0 + nrm((L, D_MODEL), 0.01),
        'w_up': nrm((L, D_MODEL, 2 * D_FF), D_MODEL ** -0.5),
        'ffn_conv_w': nrm((L, CONV_W, D_FF), CONV_W ** -0.5),
        'ffn_conv_b': nrm((L, D_FF), 0.01),
        'w_down': nrm((L, D_FF, D_MODEL), D_FF ** -0.5),
        'final_norm_g': 1.0 + nrm((D_MODEL,), 0.01),
    }


def reference(x_prompt, x_sample, state_ssm_re, state_ssm_im, cache_w1_k, cache_w1_v, cache_w2_k,
              cache_w2_v, cache_w3_k, cache_w3_v, cache_mem_k, cache_mem_v, state_ffn_conv, mem_prompt,
              norm1_g, w_in, ssm_a_re, ssm_a_im, ssm_log_dt, ssm_b_re, ssm_b_im, ssm_c_re, ssm_c_im, ssm_d,
              w_ssm_glu, w_att_o, mem_norm_g, w_mem_kv, w_mem_o, w_out, norm2_g, w_up, ffn_conv_w,
              ffn_conv_b, w_down, final_norm_g):
    slopes = alibi_slopes()
    hpg = ATT_HEADS_PER_GROUP
    xp, xs = x_prompt, x_sample
    p_states, s_states = [], []
    for i in range(DEPTH):
        ssm_w = (ssm_a_re[i], ssm_a_im[i], ssm_log_dt[i], ssm_b_re[i], ssm_b_im[i],
                 ssm_c_re[i], ssm_c_im[i], ssm_d[i], w_ssm_glu[i])

        n_p, l_p = xp.shape[0], xp.shape[1]
        u, q, k, v, qm, gates = split_in(rmsnorm(xp, norm1_g[i]), w_in[i])
        z0 = jnp.zeros((n_p, SSM_GROUPS, SSM_STATE), jnp.float32)
        b_ssm, sr_p, si_p = ssm_branch(u, z0, z0, *ssm_w)
        outs, lses, win_p = [], [], []
        for g, (win, dil) in enumerate(DIL_PAIRS):
            sl = slice(g * hpg, (g + 1) * hpg)
            o, lse = dilated_prompt(q[:, :, sl], k[:, :, sl], v[:, :, sl], dil, win // dil, slopes[sl])
            outs.append(o)
            lses.append(lse)
            keep = min(win, l_p)
            win_p += [k[:, l_p - keep:, sl], v[:, l_p - keep:, sl]]
        b_att = merge_dilated(outs, lses, w_att_o[i], xp.dtype)
        mk_p, mv_p = mem_kv(mem_prompt, mem_norm_g[i], w_mem_kv[i])
        b_mem = mem_attend(qm, mk_p, mv_p, w_mem_o[i])
        xp = xp + merge_branches(gates, b_ssm, b_att, b_mem, w_out[i])
        zbuf = jnp.zeros((n_p, CONV_W - 1, D_FF), xp.dtype)
        f, conv_p = conv_ffn(rmsnorm(xp, norm2_g[i]), zbuf, w_up[i], ffn_conv_w[i], ffn_conv_b[i], w_down[i])
        xp = xp + f
        p_states.append((sr_p, si_p, win_p[0], win_p[1], win_p[2], win_p[3], win_p[4], win_p[5],
                         mk_p, mv_p, conv_p))

        u, q, k, v, qm, gates = split_in(rmsnorm(xs, norm1_g[i]), w_in[i])
        b_ssm, sr_s, si_s = ssm_branch(u, state_ssm_re[i], state_ssm_im[i], *ssm_w)
        k_bufs = (cache_w1_k[i], cache_w2_k[i], cache_w3_k[i])
        v_bufs = (cache_w1_v[i], cache_w2_v[i], cache_w3_v[i])
        outs, lses, win_s = [], [], []
        for g, (win, dil) in enumerate(DIL_PAIRS):
            sl = slice(g * hpg, (g + 1) * hpg)
            k_new, v_new = k[:, :, sl], v[:, :, sl]
            k_all = jnp.concatenate([k_bufs[g].astype(k_new.dtype), k_new], axis=1)
            v_all = jnp.concatenate([v_bufs[g].astype(v_new.dtype), v_new], axis=1)
            o, lse = dilated_sample(q[:, :, sl], k_all, v_all, k_bufs[g].shape[1], dil, win // dil, slopes[sl])
            outs.append(o)
            lses.append(lse)
            win_s += [k_new, v_new]
        b_att = merge_dilated(outs, lses, w_att_o[i], xs.dtype)
        b_mem = mem_attend(qm, cache_mem_k[i], cache_mem_v[i], w_mem_o[i])
        xs = xs + merge_branches(gates, b_ssm, b_att, b_mem, w_out[i])
        f, conv_s = conv_ffn(rmsnorm(xs, norm2_g[i]), state_ffn_conv[i], w_up[i], ffn_conv_w[i],
                             ffn_conv_b[i], w_down[i])
        xs = xs + f
        s_states.append((sr_s, si_s, win_s[0], win_s[1], win_s[2], win_s[3], win_s[4], win_s[5], conv_s))

    y_prompt = rmsnorm(xp, final_norm_g)
    y_sample = rmsnorm(xs, final_norm_g)
    (p_ssm_re, p_ssm_im, p_w1_k, p_w1_v, p_w2_k, p_w2_v, p_w3_k, p_w3_v,
     p_mem_k, p_mem_v, p_ffn_conv) = [jnp.stack(a, axis=0) for a in zip(*p_states)]
    (s_ssm_re, s_ssm_im, s_w1_k, s_w1_v, s_w2_k, s_w2_v, s_w3_k, s_w3_v,
     s_ffn_conv) = [jnp.stack(a, axis=0) for a in zip(*s_states)]
    return (y_prompt, y_sample,
            p_ssm_re, p_ssm_im, p_w1_k, p_w1_v, p_w2_k, p_w2_v, p_w3_k, p_w3_v, p_mem_k, p_mem_v, p_ffn_conv,
            s_ssm_re, s_ssm_im, s_w1_k, s_w1_v, s_w2_k, s_w2_v, s_w3_k, s_w3_v, s_ffn_conv)
```

```python
import math
from contextlib import ExitStack

import numpy as np
import concourse.bass as bass
import concourse.mybir as mybir
from concourse.bass_utils import run_bass_kernel_spmd

F32 = mybir.dt.float32
BF16 = mybir.dt.bfloat16
AF = mybir.ActivationFunctionType
ALU = mybir.AluOpType
AX = mybir.AxisListType


class PsBank:
    def __init__(self, key, ap):
        self.key = key
        self.ap = ap


class Sched:
    ENG = ("pe", "act", "dve", "pool", "sp")
    NDS = 8

    def __init__(self, nc):
        self.nc = nc
        self.es = ExitStack()
        self.q = {e: [] for e in self.ENG}
        self.cnt = {e: 0 for e in self.ENG}
        self.dcnt = {e: 0 for e in self.ENG}
        self.known = {e: {} for e in self.ENG}
        self.lastw = {}
        self.readers = {}
        self.sem = {e: self.es.enter_context(nc.semaphore("s_" + e)) for e in self.ENG}
        self.dsem = {e: [self.es.enter_context(nc.semaphore("d_%s%d" % (e, i))) for i in range(self.NDS)]
                     for e in ("sp", "pool", "act")}
        self.banks = []
        for i in range(8):
            t = self.es.enter_context(nc.psum_tensor("psb%d" % i, [128, 512], F32))
            self.banks.append(PsBank("psb%d" % i, t))
        self.bank_i = 0
        self.nrot = 8

    def sb(self, name, shape, dtype):
        return self.es.enter_context(self.nc.sbuf_tensor("sb_" + name, shape, dtype))

    def psum(self):
        b = self.banks[self.bank_i % self.nrot]
        self.bank_i += 1
        return b

    def _deps(self, reads, writes):
        deps = []
        for b in reads:
            deps.extend(self.lastw.get(b, ()))
        for b in writes:
            deps.extend(self.lastw.get(b, ()))
            deps.extend(self.readers.get(b, ()))
        return deps

    def _waits(self, eng, deps):
        waits = []
        kn = self.known[eng]
        for ev in deps:
            if ev[0] == "c":
                _, e2, idx = ev
                if e2 == eng and idx < self.cnt[eng] - 1:
                    continue
                if kn.get(("c", e2), 0) >= idx:
                    continue
                kn[("c", e2)] = idx
                waits.append(ev)
            else:
                _, qn, j = ev
                s, c = j % self.NDS, j // self.NDS + 1
                if kn.get(("d", qn, s), 0) >= c:
                    continue
                kn[("d", qn, s)] = c
                waits.append(ev)
        return waits

    def _record(self, ev, reads, writes):
        for b in writes:
            if self.readers.get(b) or b not in self.lastw:
                self.lastw[b] = [ev]
            else:
                self.lastw[b] = self.lastw[b] + [ev]
            self.readers[b] = []
        for b in reads:
            if b not in writes:
                self.readers.setdefault(b, []).append(ev)

    def op(self, eng, fn, reads=(), writes=()):
        deps = self._deps(reads, writes)
        waits = self._waits(eng, deps)
        self.cnt[eng] += 1
        ev = ("c", eng, self.cnt[eng])
        self.q[eng].append((fn, waits, ev))
        self._record(ev, reads, writes)

    def dma(self, qn, out, in_, reads=(), writes=(), **kw):
        deps = self._deps(reads, writes)
        j = self.dcnt[qn]
        if j >= self.NDS:
            deps.append(("d", qn, j - self.NDS))
        waits = self._waits(qn, deps)
        self.dcnt[qn] += 1
        ev = ("d", qn, j)
        self.q[qn].append((lambda e: e.dma_start(out=out, in_=in_, **kw), waits, ev))
        self._record(ev, reads, writes)

    def emit(self):
        nc = self.nc
        q = self.q
        needed = set()
        for name in self.ENG:
            for fn, waits, ev in q[name]:
                for w in waits:
                    if w[0] == "c":
                        needed.add(w)
        for e in ("pe", "act", "dve", "pool"):
            if self.cnt[e] > 0:
                needed.add(("c", e, self.cnt[e]))
        import os
        if os.environ.get("DENSE"):
            for e in ("pe", "act", "dve", "pool"):
                for idx in range(1, self.cnt[e] + 1):
                    needed.add(("c", e, idx))
        cum = {}
        for e in ("pe", "act", "dve", "pool"):
            c, arr = 0, [0] * (self.cnt[e] + 1)
            for idx in range(1, self.cnt[e] + 1):
                if ("c", e, idx) in needed:
                    c += 1
                arr[idx] = c
            cum[e] = arr

        def semval(w):
            if w[0] == "c":
                return self.sem[w[1]], cum[w[1]][w[2]]
            _, qn, j = w
            return self.dsem[qn][j % self.NDS], 16 * (j // self.NDS + 1)

        fin = []
        for qn in ("sp", "pool", "act"):
            for s in range(self.NDS):
                n = (self.dcnt[qn] - s + self.NDS - 1) // self.NDS
                if n > 0:
                    fin.append((self.dsem[qn][s], 16 * n))
        for e in ("pe", "act", "dve", "pool"):
            if self.cnt[e] > 0:
                fin.append((self.sem[e], cum[e][self.cnt[e]]))
        self.n_inc = {e: cum[e][-1] for e in cum}

        def run(e, name, final=()):
            for fn, waits, ev in q[name]:
                for w in waits:
                    s, v = semval(w)
                    e.wait_ge(s, v)
                ins = fn(e)
                if ev[0] == "d":
                    ins.then_inc(self.dsem[ev[1]][ev[2] % self.NDS], 16)
                elif ev in needed:
                    ins.then_inc(self.sem[ev[1]], 1)
            for s, v in final:
                e.wait_ge(s, v)

        with nc.Block() as block:
            @block.sync
            def _(e):
                run(e, "sp", fin)

            @block.tensor
            def _(e):
                run(e, "pe")

            @block.scalar
            def _(e):
                run(e, "act")

            @block.vector
            def _(e):
                run(e, "dve")

            @block.gpsimd
            def _(e):
                run(e, "pool")
        self.es.close()


D = 1024
SEQ = 4096
TOK = 512
NBLK = SEQ // TOK
NS = 16
DFF = 2816
NF = DFF // 128
INW = 6400
OFF_U, OFF_Q, OFF_K, OFF_V, OFF_QM, OFF_G = 0, 512, 1280, 2048, 2816, 3328
DILS = (1, 4, 16)
EPS = 1e-6
NEG = -30000.0
SLOPES = [2.0 ** (-8.0 * h / 12.0) for h in range(1, 13)]
TWO_PI = 2.0 * math.pi


def host_consts():
    import ml_dtypes
    c = {}
    c["ident_bf"] = np.eye(128, dtype=np.float32).astype(ml_dtypes.bfloat16)
    c["ident_f"] = np.eye(128, dtype=np.float32)
    a = np.arange(128)[:, None].astype(np.float64)
    b3_ = np.arange(32)[None, :].astype(np.float64)
    b = np.arange(128)[None, :].astype(np.float64)
    t12 = np.zeros((128, 2, 2, 4, 128), np.float32)
    for g in range(2):
        for h in range(4):
            sl = SLOPES[g * 4 + h] * DILS[g]
            dA = 128 + b - a
            t12[:, g, 0, h, :] = np.where(dA <= 128, -sl * dA, NEG)
            dB = b - a
            t12[:, g, 1, h, :] = np.where(dB >= 0, -sl * dB, NEG)
    c["tab12"] = t12.reshape(128, 16 * 128)
    bA = 128 + b - a
    bB = b - a
    c["base12"] = np.stack([np.where(bA <= 128, -bA, -1e9), np.where(bB >= 0, -bB, -1e9)], axis=1).astype(np.float32).reshape(128, 256)
    t3b = np.zeros((128, 4, 2, 32), np.float32)
    for v in range(4):
        dA = 128 + 32 * v + b3_ - a
        t3b[:, v, 0, :] = np.where(dA <= 128, -dA, -1e9)
        dB = 32 * v + b3_ - a
        t3b[:, v, 1, :] = np.where(dB >= 0, -dB, -1e9)
    c["base3"] = t3b.reshape(128, 256)
    b3 = np.arange(32)[None, :].astype(np.float64)
    t3 = np.zeros((128, 4, 2, 4, 32), np.float32)
    for v in range(4):
        for h in range(4):
            sl = SLOPES[8 + h] * 16
            dA = 128 + 32 * v + b3 - a
            t3[:, v, 0, h, :] = np.where(dA <= 128, -sl * dA, NEG)
            dB = 32 * v + b3 - a
            t3[:, v, 1, h, :] = np.where(dB >= 0, -sl * dB, NEG)
    c["tab3"] = t3.reshape(128, 32 * 32)
    ts = np.zeros((128, 12), np.float32)
    for g in range(3):
        for h in range(4):
            ts[:, g * 4 + h] = -SLOPES[g * 4 + h] * DILS[g] * (128 - np.arange(128))
    c["tabs_s"] = ts
    sel = np.zeros((16, 16, 128), np.float32)
    for bb in range(16):
        sel[bb, bb, :] = 1.0
    c["sel"] = sel.reshape(16, 16 * 128)
    c["twopi"] = np.full((128, 1), TWO_PI, np.float32)
    c["iota"] = np.tile(np.arange(514, dtype=np.float32)[None, :], (128, 1))
    return c


CONST_SHAPES = {"ident_bf": ([128, 128], BF16), "ident_f": ([128, 128], F32), "tab12": ([128, 2048], F32),
                "tab3": ([128, 1024], F32), "base12": ([128, 256], F32), "base3": ([128, 256], F32), "tabs_s": ([128, 12], F32), "sel": ([16, 2048], F32),
                "twopi": ([128, 1], F32), "iota": ([128, 514], F32)}

IN_SHAPES = {
    "xp": [SEQ, D], "memp": [256, D], "xs": [NS, D], "sre": [NS, 2048], "sim": [NS, 2048],
    "ck1": [NS, 128, 256], "cv1": [NS, 128, 256], "ck2": [NS, 512, 256], "cv2": [NS, 512, 256],
    "ck3": [NS, 2048, 256], "cv3": [NS, 2048, 256], "cmk": [NS, 256, 512], "cmv": [NS, 256, 512],
    "sconv": [NS, 2, DFF],
    "norm1_g": [1, D], "w_in": [D, INW], "a_re": [16, 128], "a_im": [16, 128], "log_dt": [16, 2],
    "b_re": [2048, 16], "b_im": [2048, 16], "c_re": [512, 64], "c_im": [512, 64], "ssm_d": [4, 128],
    "w_glu": [512, 2048], "w_atto": [256, D], "mem_g": [1, D], "w_memkv": [D, D], "w_memo": [512, D],
    "w_out": [D, D], "norm2_g": [1, D], "w_up": [D, 2 * DFF], "conv_w": [3, DFF], "conv_b": [1, DFF],
    "w_down": [DFF, D], "fin_g": [1, D],
}
OUT_SHAPES = {
    "y_p": [SEQ, D], "y_s": [NS, D], "p_sre": [16, 128], "p_sim": [16, 128],
    "p_k1": [128, 256], "p_v1": [128, 256], "p_k2": [512, 256], "p_v2": [512, 256],
    "p_k3": [2048, 256], "p_v3": [2048, 256], "p_mk": [256, 512], "p_mv": [256, 512], "p_conv": [2, DFF],
    "s_sre": [NS, 2048], "s_sim": [NS, 2048], "s_k1": [NS, 256], "s_v1": [NS, 256], "s_k2": [NS, 256],
    "s_v2": [NS, 256], "s_k3": [NS, 256], "s_v3": [NS, 256], "s_conv": [NS, 2, DFF],
}


def build_nc(stage=99, nblk=NBLK, debug=False):
    nc = bass.Bass("TRN2", target_bir_lowering=False)
    I = {k: nc.dram_tensor(k, s, F32, kind="ExternalInput").ap() for k, s in IN_SHAPES.items()}
    C = {k: nc.dram_tensor(k, s, dt, kind="ExternalInput").ap() for k, (s, dt) in CONST_SHAPES.items()}
    O = {k: nc.dram_tensor(k, s, F32, kind="ExternalOutput").ap() for k, s in OUT_SHAPES.items()}
    WB = {k: nc.dram_tensor(k + "_bf", IN_SHAPES[k], BF16, kind="Internal").ap()
          for k in ("w_in", "w_glu", "w_atto", "w_memkv", "w_memo", "w_out", "w_up", "w_down")}
    tabs_d = nc.dram_tensor("tabs_d", [16, 128, 1024], F32, kind="Internal").ap()
    S = Sched(nc)
    sb = S.sb
    dbg_n = {"i": 0}

    def dbg(name, ap, shape, keys, dt=F32):
        if not debug:
            return
        t = nc.dram_tensor("dbg_" + name, shape, dt, kind="ExternalOutput").ap()
        S.dma("sp", t, ap, reads=keys, writes=[])

    ident_bf = sb("ident_bf", [128, 128], BF16)
    ident_f = sb("ident_f", [128, 128], F32)
    ones_bf = sb("ones_bf", [128, 128], BF16)
    twopi = sb("twopi", [128, 1], F32)
    S.dma("sp", ident_bf[:], C["ident_bf"], writes=["ident_bf"])
    S.dma("sp", ident_f[:], C["ident_f"], writes=["ident_f"])
    S.dma("sp", twopi[:], C["twopi"], writes=["twopi"])
    S.op("dve", lambda e: e.memset(ones_bf[:], 1.0), writes=["ones_bf"])
    zer_bf = sb("zer_bf", [128, 64], BF16)
    S.op("dve", lambda e: e.memset(zer_bf[:], 0.0), writes=["zer_bf"])

    for k in ("w_in", "w_memkv", "w_glu", "w_atto", "w_memo", "w_out", "w_up", "w_down"):
        rows = IN_SHAPES[k][0]
        step = 256
        for r0 in range(0, rows, step):
            r1 = min(rows, r0 + step)
            S.dma("pool", WB[k][r0:r1, :], I[k][r0:r1, :], writes=[k + "_bf"])

    wbuf = [sb("wbuf%d" % i, [128, 8, 512], BF16) for i in range(2)]
    wstate = {"i": 0}
    arena = sb("arena", [128, 22, TOK], BF16)
    af = sb("af", [128, 4, TOK + 2], F32)
    xres = sb("xres", [128, 4, D], F32)
    xnb = sb("xnb", [128, D], BF16)
    gbc = sb("gbc", [128, D], F32)
    ytmp = sb("ytmp", [128, D], F32)
    hT = sb("hT", [128, 8, TOK], BF16)
    ubf = sb("ubf", [128, 4, TOK], BF16)
    qmT = sb("qmT", [128, 4, TOK], BF16)
    zT = sb("zT", [128, 4, TOK], BF16)
    kT12 = sb("kT12", [128, 2, 2, 2, TOK], BF16)
    kT3 = sb("kT3", [128, 2, SEQ], BF16)
    V1 = sb("V1", [128, 2, 4, 256], BF16)
    V2 = sb("V2", [128, 2, 4, 256], BF16)
    V3 = sb("V3", [128, 16, 2, 256], BF16)
    vst = [sb("vst%d" % i, [128, 256], BF16) for i in range(2)]
    stg = sb("stg", [128, 512], F32)
    pT = [sb("pT%d" % i, [128, 512], BF16) for i in range(2)]
    oT = sb("oT", [64, 4, TOK], BF16)
    omT = sb("omT", [128, 4, TOK], BF16)
    pTm = sb("pTm", [128, 2, TOK], BF16)
    mkT = sb("mkT", [128, 4, 256], BF16)
    mv = sb("mv", [128, 2, 512], BF16)
    base12 = sb("base12", [128, 2, 128], F32)
    base3 = sb("base3", [128, 8, 32], F32)
    small = sb("small", [128, 64], F32)
    tabt = [sb("tabt%d" % i, [128, 2, 512], F32) for i in range(2)]
    sbf = [sb("sbf%d" % i, [128, TOK], BF16) for i in range(2)]
    bbT = sb("bbT", [128, 16, 2, 128], BF16)
    cT = sb("cT", [128, 16, 2, 128], BF16)
    dD = sb("dD", [128, 4, 128], BF16)
    mag = sb("mag", [128, 16], F32)
    e512 = sb("e512", [128, 2, 16], F32)
    e511 = sb("e511", [128, 2, 16], F32)
    ab = sb("ab", [128, 2, 16], F32)
    rlast = sb("rlast", [128, 2, 16], F32)
    rinit = sb("rinit", [128, 2, 16], F32)
    carry = sb("carry", [128, NF, 2], F32)
    cwT = sb("cwT", [128, NF, 4], F32)
    S.dma("sp", base12[:], C["base12"].rearrange("p (a b) -> p a b", b=128), writes=["base12"])
    S.dma("sp", base3[:], C["base3"].rearrange("p (a b) -> p a b", b=32), writes=["base3"])

    P32 = kT3[:, :, :].bitcast(F32).rearrange("p a b -> p (a b)")
    pro_state = {"off": 0, "keys": []}

    def pro(name, shape):
        n = int(np.prod(shape[1:]))
        o = pro_state["off"]
        pro_state["off"] = o + n
        assert pro_state["off"] <= 4096, name
        pro_state["keys"].append(name)
        v = P32[:shape[0], o:o + n]
        if len(shape) == 3:
            v = v.rearrange("p (a b) -> p a b", b=shape[2])
        elif len(shape) == 4:
            v = v.rearrange("p (a b c) -> p a b c", b=shape[2], c=shape[3])
        return v

    AR = lambda j: ("ar", j)
    AFK = lambda j: ("af", j)
    evac_rr = {"i": 0}

    def evac(out_ap, ps, rows, n, reads, writes, scale=None, func=None):
        evac_rr["i"] += 1
        if func is not None or scale is not None or evac_rr["i"] % 2 == 0:
            f = func if func is not None else AF.Copy
            if scale is None:
                S.op("act", lambda e: e.activation(out_ap, ps.ap[:rows, :n], f), reads=[ps.key] + reads, writes=writes)
            else:
                S.op("act", lambda e: e.activation(out_ap, ps.ap[:rows, :n], f, scale=scale),
                     reads=[ps.key] + reads, writes=writes)
        else:
            S.op("dve", lambda e: e.tensor_copy(out_ap, ps.ap[:rows, :n]), reads=[ps.key] + reads, writes=writes)

    def wload(pieces, kc, krows):
        i = wstate["i"] % 2
        wstate["i"] += 1
        t, key = wbuf[i], "wbuf%d" % i
        for (W, wk, r0, c0, n, dc) in pieces:
            src = W[r0:r0 + kc * krows, c0:c0 + n].rearrange("(k p) c -> p k c", p=krows)
            S.dma("sp", t[:krows, :kc, dc:dc + n], src, reads=[wk], writes=[key])
        return t, key

    def mm(ps, prow, n, lhs_fn, rhs_fn, kc, reads, start=True, stop=True):
        def fn(e):
            ins = None
            for k in range(kc):
                ins = e.matmul(ps.ap[:prow, :n], lhs_fn(k), rhs_fn(k), start=(start and k == 0),
                               stop=(stop and k == kc - 1))
            return ins
        S.op("pe", fn, reads=reads, writes=[ps.key])

    def load_gbc(name):
        S.dma("sp", gbc[:], I[name].to_broadcast([128, D]), writes=["gbc", "gbc0", "gbc1"], allow_slow_non_contiguous=True)

    def rmsnorm_rows(x_ap, rows, xkeys, out_ap, outkeys):
        S.op("act", lambda e: e.activation(xnb[:rows, :], x_ap, AF.Square, accum_out=small[:rows, 0:1]),
             reads=xkeys, writes=["xnb", "small0"])
        S.op("dve", lambda e: e.tensor_scalar(small[:rows, 1:2], small[:rows, 0:1], 1.0 / D, EPS, ALU.mult, ALU.add),
             reads=["small0"], writes=["small1"])
        S.op("act", lambda e: e.activation(small[:rows, 3:4], small[:rows, 1:2], AF.Sqrt), reads=["small1"], writes=["small3"])
        S.op("dve", lambda e: e.reciprocal(small[:rows, 2:3], small[:rows, 3:4]), reads=["small3"], writes=["small2"])
        S.op("dve", lambda e: e.scalar_tensor_tensor(out_ap, x_ap, small[:rows, 2:3], gbc[:rows, :], ALU.mult, ALU.mult),
             reads=xkeys + ["small2", "gbc"], writes=outkeys)

    def transpose_to(dst_fn, src_bf, rows, nk, skeys, dkeys):
        ps = S.psum()
        pv = ps.ap[:, :].bitcast(BF16)

        def fn(e):
            ins = None
            for k in range(nk):
                ins = e.transpose(pv[:, k * 128:k * 128 + rows], src_bf[:rows, k * 128:(k + 1) * 128],
                                  ident_bf[:rows, :rows])
            return ins
        S.op("pe", fn, reads=skeys + ["ident_bf"], writes=[ps.key])
        evac_rr["t"] = evac_rr.get("t", 0) + 1
        use_dve = evac_rr["t"] % 2
        for k in range(nk):
            S.op("dve" if use_dve else "act",
                 (lambda e, k=k: e.tensor_copy(dst_fn(k), pv[:, k * 128:k * 128 + rows])) if use_dve else
                 (lambda e, k=k: e.activation(dst_fn(k), pv[:, k * 128:k * 128 + rows], AF.Copy)),
                 reads=[ps.key], writes=dkeys)

    ld16 = pro("ld16", [16, 4, 128])
    S.dma("sp", ld16[:, 0, :], I["a_re"], writes=["ld16"])
    S.dma("sp", ld16[:, 1, :], I["a_im"], writes=["ld16"])
    ldt = sb("ldt", [16, 2], F32)
    S.dma("sp", ldt[:], I["log_dt"], writes=["ldt"])
    S.op("dve", lambda e: e.tensor_copy(ld16[:, 2, :].rearrange("g (a p) -> g a p", p=64),
                                        ldt[:, :].unsqueeze(2).to_broadcast([16, 2, 64])), reads=["ldt"], writes=["ld16"])
    cst = sb("cst", [128, 12, 16], F32)
    ps = S.psum()

    def _tr3(e):
        ins = None
        for j in range(3):
            ins = e.transpose(ps.ap[:, j * 16:(j + 1) * 16], ld16[:, j, :], ident_f[:16, :16])
        return ins
    S.op("pe", _tr3, reads=["ld16", "ident_f"], writes=[ps.key])
    S.op("dve", lambda e: e.tensor_copy(cst[:, 0:3, :], ps.ap[:, 0:48].rearrange("p (a b) -> p a b", b=16)),
         reads=[ps.key], writes=["cst"])
    S.op("act", lambda e: e.activation(cst[:, 2, :], cst[:, 2, :], AF.Exp), reads=["cst"], writes=["cst"])
    S.op("dve", lambda e: e.tensor_mul(cst[:, 3, :], cst[:, 1, :], cst[:, 2, :]), reads=["cst"], writes=["cst"])
    S.op("dve", lambda e: e.tensor_mul(cst[:, 5, :], cst[:, 0, :], cst[:, 2, :]), reads=["cst"], writes=["cst"])
    S.op("dve", lambda e: e.tensor_scalar(mag[:], cst[:, 5, :], 1.0 / 12.0, 1.0, ALU.mult, ALU.add), reads=["cst"], writes=["mag"])
    for n_ in range(11, 0, -1):
        S.op("dve", lambda e: e.tensor_mul(mag[:], mag[:], cst[:, 5, :]), reads=["cst", "mag"], writes=["mag"])
        S.op("dve", lambda e, n_=n_: e.tensor_scalar(mag[:], mag[:], 1.0 / n_, 1.0, ALU.mult, ALU.add), reads=["mag"], writes=["mag"])
    MAGIC = 12582912.0

    def reduce_pi(dst, src_ap, tmp, rk, wk, shift=0.0):
        S.op("dve", lambda e: e.tensor_scalar(tmp, src_ap, shift, 1.0 / TWO_PI, ALU.add, ALU.mult), reads=rk, writes=wk)
        S.op("dve", lambda e: e.tensor_scalar_add(tmp, tmp, MAGIC), reads=wk, writes=wk)
        S.op("dve", lambda e: e.tensor_scalar(tmp, tmp, MAGIC, -TWO_PI, ALU.subtract, ALU.mult), reads=wk, writes=wk)
        S.op("dve", lambda e: e.scalar_tensor_tensor(dst, src_ap, shift, tmp, ALU.add, ALU.add), reads=rk + wk, writes=wk)
        S.op("dve", lambda e: e.tensor_scalar(dst, dst, 3.141592, -3.141592, ALU.min, ALU.max), reads=wk, writes=wk)

    reduce_pi(cst[:, 6, :], cst[:, 3, :], cst[:, 11, :], ["cst"], ["cst"])
    ang = pro("ang", [128, 514])
    iota = pro("iota", [128, 514])
    S.dma("sp", iota[:], C["iota"], writes=["iota"])
    for gp in range(16):
        S.op("dve", lambda e, gp=gp: e.tensor_scalar_mul(ang[:, 0:513], iota[:, 0:513], cst[:, 6, gp:gp + 1]),
             reads=["cst", "iota"], writes=["ang"])
        reduce_pi(af[:, 1, 0:513], ang[:, 0:513], af[:, 0, 0:513], ["ang"], [AFK(0), AFK(1)], shift=math.pi / 2)
        reduce_pi(af[:, 2, 0:513], ang[:, 0:513], af[:, 3, 0:513], ["ang"], [AFK(2), AFK(3)])
        S.op("act", lambda e: e.activation(af[:, 0, 0:513], af[:, 1, 0:513], AF.Sin), reads=[AFK(1)], writes=[AFK(0)])
        S.op("act", lambda e: e.activation(af[:, 3, 0:513], af[:, 2, 0:513], AF.Sin), reads=[AFK(2)], writes=[AFK(3)])
        S.dma("sp", tabs_d[gp, :, 0:512], af[:, 0, 0:512], reads=[AFK(0)], writes=["tabs_d"])
        S.dma("sp", tabs_d[gp, :, 512:1024], af[:, 3, 0:512], reads=[AFK(3)], writes=["tabs_d"])
        for (dst, col) in ((e512, 512), (e511, 511)):
            S.op("dve", lambda e, gp=gp, dst=dst, col=col: e.tensor_copy(dst[:, 0, gp:gp + 1], af[:, 0, col:col + 1]),
                 reads=[AFK(0)], writes=["ecst"])
            S.op("dve", lambda e, gp=gp, dst=dst, col=col: e.tensor_copy(dst[:, 1, gp:gp + 1], af[:, 3, col:col + 1]),
                 reads=[AFK(3)], writes=["ecst"])
        S.op("dve", lambda e, gp=gp: e.tensor_copy(small[:, 16 + gp:17 + gp], af[:, 0, 1:2]), reads=[AFK(0)], writes=["small16"])
        S.op("dve", lambda e, gp=gp: e.tensor_copy(small[:, 32 + gp:33 + gp], af[:, 3, 1:2]), reads=[AFK(3)], writes=["small16"])
    S.op("dve", lambda e: e.tensor_mul(ab[:, 0, :], mag[:], small[:, 16:32]), reads=["mag", "small16"], writes=["ab"])
    S.op("dve", lambda e: e.tensor_mul(ab[:, 1, :], mag[:], small[:, 32:48]), reads=["mag", "small16"], writes=["ab"])
    c_ = lambda j: cst[:, j, :]
    S.op("dve", lambda e: e.tensor_scalar_add(c_(5), ab[:, 0, :], -1.0), reads=["ab"], writes=["cst"])
    S.op("dve", lambda e: e.tensor_mul(c_(7), c_(0), c_(0)), reads=["cst"], writes=["cst"])
    S.op("dve", lambda e: e.tensor_mul(c_(8), c_(1), c_(1)), reads=["cst"], writes=["cst"])
    S.op("dve", lambda e: e.tensor_add(c_(4), c_(7), c_(8)), reads=["cst"], writes=["cst"])
    S.op("dve", lambda e: e.reciprocal(c_(4), c_(4)), reads=["cst"], writes=["cst"])
    S.op("dve", lambda e: e.tensor_mul(c_(7), c_(5), c_(0)), reads=["cst"], writes=["cst"])
    S.op("dve", lambda e: e.tensor_mul(c_(8), ab[:, 1, :], c_(1)), reads=["cst", "ab"], writes=["cst"])
    S.op("dve", lambda e: e.tensor_add(c_(7), c_(7), c_(8)), reads=["cst"], writes=["cst"])
    S.op("dve", lambda e: e.tensor_mul(c_(9), c_(7), c_(4)), reads=["cst"], writes=["cst"])
    S.op("dve", lambda e: e.tensor_mul(c_(7), ab[:, 1, :], c_(0)), reads=["cst", "ab"], writes=["cst"])
    S.op("dve", lambda e: e.tensor_mul(c_(8), c_(5), c_(1)), reads=["cst"], writes=["cst"])
    S.op("dve", lambda e: e.tensor_sub(c_(7), c_(7), c_(8)), reads=["cst"], writes=["cst"])
    S.op("dve", lambda e: e.tensor_mul(c_(10), c_(7), c_(4)), reads=["cst"], writes=["cst"])
    braw = pro("braw", [128, 2, 16, 16])
    S.dma("sp", braw[:, 0], I["b_re"].rearrange("(gp q) c -> q gp c", q=128), writes=["braw"])
    S.dma("sp", braw[:, 1], I["b_im"].rearrange("(gp q) c -> q gp c", q=128), writes=["braw"])
    bbf = pro("bbf", [128, 2, 16, 16])
    qre_b = cst[:, 9, :].unsqueeze(2).to_broadcast([128, 16, 16])
    qim_b = cst[:, 10, :].unsqueeze(2).to_broadcast([128, 16, 16])
    tmpb = pro("tmpb", [128, 16, 16])
    S.op("dve", lambda e: e.tensor_mul(bbf[:, 0], braw[:, 0], qre_b), reads=["braw", "cst"], writes=["bbf"])
    S.op("dve", lambda e: e.tensor_mul(tmpb[:], braw[:, 1], qim_b), reads=["braw", "cst"], writes=["tmpb"])
    S.op("dve", lambda e: e.tensor_sub(bbf[:, 0], bbf[:, 0], tmpb[:]), reads=["tmpb", "bbf"], writes=["bbf"])
    S.op("dve", lambda e: e.tensor_mul(bbf[:, 1], braw[:, 1], qre_b), reads=["braw", "cst"], writes=["bbf"])
    S.op("dve", lambda e: e.tensor_mul(tmpb[:], braw[:, 0], qim_b), reads=["braw", "cst", "bbf"], writes=["tmpb"])
    S.op("dve", lambda e: e.tensor_add(bbf[:, 1], bbf[:, 1], tmpb[:]), reads=["tmpb", "bbf"], writes=["bbf"])
    S.op("dve", lambda e: e.memset(bbT[:], 0.0), writes=["bbT"])
    S.op("dve", lambda e: e.memset(cT[:], 0.0), writes=["cT"])
    S.op("dve", lambda e: e.memset(dD[:], 0.0), writes=["dD"])
    bbz = pro("bbz", [128, 32])
    for gp in range(16):
        for ri in range(2):
            S.op("dve", lambda e: e.memset(bbz[:], 0.0), writes=["bbz"])
            S.op("dve", lambda e, gp=gp, ri=ri: e.tensor_copy(bbz[0:64, 0:16], bbf[0:64, ri, gp, :]), reads=["bbf"], writes=["bbz"])
            S.op("dve", lambda e, gp=gp, ri=ri: e.tensor_copy(bbz[64:128, 16:32], bbf[64:128, ri, gp, :]), reads=["bbf"], writes=["bbz"])
            ps = S.psum()
            S.op("pe", lambda e, ps=ps: e.transpose(ps.ap[:32, :128], bbz[:, :], ident_f[:, :]), reads=["bbz", "ident_f"], writes=[ps.key])
            r0 = (gp % 4) * 32
            S.op("dve", lambda e, ps=ps: e.tensor_copy(vst[0][:32, :128], ps.ap[:32, :128]), reads=[ps.key], writes=["vst0"])
            S.dma("sp", bbT[r0:r0 + 32, gp, ri, :], vst[0][:32, :128], reads=["vst0"], writes=["bbT"])
    craw = pro("craw", [128, 4, 2, 64])
    S.dma("sp", craw[:, :, 0, :], I["c_re"].rearrange("(t q) p -> q t p", q=128), writes=["craw"])
    S.dma("sp", craw[:, :, 1, :], I["c_im"].rearrange("(t q) p -> q t p", q=128), writes=["craw"])
    for gp in range(16):
        t, r0 = gp // 4, (gp % 4) * 32
        for ri in range(2):
            ps = S.psum()
            S.op("pe", lambda e, ps=ps, t=t, ri=ri: e.transpose(ps.ap[:64, :128], craw[:, t, ri, :], ident_f[:, :]),
                 reads=["craw", "ident_f"], writes=[ps.key])
            sc = 1.0 if ri == 0 else -1.0
            S.op("act", lambda e, ps=ps, gp=gp, ri=ri, r0=r0, sc=sc: e.activation(cT[0:64, gp, ri, r0:r0 + 16], ps.ap[0:64, r0:r0 + 16], AF.Copy, scale=sc),
                 reads=[ps.key], writes=["cT"])
            S.op("act", lambda e, ps=ps, r0=r0, sc=sc: e.activation(stg[0:64, 0:16], ps.ap[0:64, r0 + 16:r0 + 32], AF.Copy, scale=sc),
                 reads=[ps.key], writes=["stg"])
            S.op("dve", lambda e: e.tensor_copy(vst[0][0:64, 0:16], stg[0:64, 0:16]), reads=["stg"], writes=["vst0"])
            S.dma("sp", cT[64:128, gp, ri, r0 + 16:r0 + 32], vst[0][0:64, 0:16], reads=["vst0"], writes=["cT"])
    dcol = sb("dcol", [128, 4], F32)
    drow = pro("drow", [4, 128])
    S.dma("sp", drow[:], I["ssm_d"], writes=["drow"])
    ps = S.psum()
    S.op("pe", lambda e, ps=ps: e.transpose(ps.ap[:, 0:4], drow[:, :], ident_f[:4, :4]), reads=["drow", "ident_f"], writes=[ps.key])
    S.op("dve", lambda e, ps=ps: e.tensor_copy(dcol[:], ps.ap[:, 0:4]), reads=[ps.key], writes=["dcol"])
    for t in range(4):
        S.op("dve", lambda e, t=t: e.tensor_scalar_mul(dD[:, t, :], ident_f[:, :], dcol[:, t:t + 1]), reads=["ident_f", "dcol"], writes=["dD"])
    cwrow = pro("cwrow", [NF, 4, 128])
    for j in range(3):
        S.dma("sp", cwrow[:, j, :], I["conv_w"][j:j + 1, :].rearrange("j (f q) -> (j f) q", q=128), writes=["cwrow"])
    S.dma("sp", cwrow[:, 3, :], I["conv_b"].rearrange("j (f q) -> (j f) q", q=128), writes=["cwrow"])
    ps = S.psum()

    def _trc(e, ps=ps):
        ins = None
        for j in range(4):
            ins = e.transpose(ps.ap[:, j * NF:(j + 1) * NF], cwrow[:, j, :], ident_f[:NF, :NF])
        return ins
    S.op("pe", _trc, reads=["cwrow", "ident_f"], writes=[ps.key])
    S.op("dve", lambda e, ps=ps: e.tensor_copy(cwT[:, :, :], ps.ap[:, 0:4 * NF].rearrange("p (j f) -> p f j", f=NF)),
         reads=[ps.key], writes=["cwT"])
    S.op("dve", lambda e: e.memset(carry[:], 0.0), writes=["carry"])
    S.op("dve", lambda e: e.memset(rinit[:], 0.0), writes=["rinit"])

    S.op("dve", lambda e: e.memset(small[:, 63:64], 0.0), writes=pro_state["keys"] + ["kT3"])

    load_gbc("mem_g")
    for mt in range(2):
        S.dma("sp", xres[:, mt, :], I["memp"][mt * 128:(mt + 1) * 128, :], writes=[("x", mt)])
        rmsnorm_rows(xres[:, mt, :], 128, [("x", mt)], xnb[:, :], ["xnb"])
        transpose_to(lambda k, mt=mt: hT[:, k, mt * 128:(mt + 1) * 128], xnb, 128, 8, ["xnb"], ["hT"])
    for cg, oname in ((0, "p_mk"), (1, "p_mv")):
        wt, wk = wload([(WB["w_memkv"], "w_memkv_bf", 0, cg * 512, 512, 0)], 8, 128)
        for mt in range(2):
            ps = S.psum()
            mm(ps, 128, 512, lambda k, mt=mt: hT[:, k, mt * 128:(mt + 1) * 128], lambda k, wt=wt: wt[:, k, :], 8, [wk, "hT"])
            evac(stg[:, :], ps, 128, 512, [], ["stg"])
            S.dma("sp", O[oname][mt * 128:(mt + 1) * 128, :], stg[:, :], reads=["stg"], writes=[])
            if cg == 1:
                S.op("dve", lambda e, mt=mt: e.tensor_copy(mv[:, mt, :], stg[:, :]), reads=["stg"], writes=["mv"])
        if cg == 0:
            for h in range(4):
                ps = S.psum()
                mm(ps, 128, 256, lambda k, wt=wt, h=h: wt[:, k, h * 128:(h + 1) * 128], lambda k: hT[:, k, 0:256], 8, [wk, "hT"])
                evac(mkT[:, h, :], ps, 128, 256, [], ["mkT"])

    def window_outputs(blk):
        T0 = blk * TOK
        for g, win in enumerate((128, 512, 2048)):
            tts = [tt for tt in range(4) if T0 + tt * 128 >= SEQ - win]
            if not tts:
                continue
            wt, wk = wload([(WB["w_in"], "w_in_bf", 0, OFF_K + g * 256, 256, 0),
                            (WB["w_in"], "w_in_bf", 0, OFF_V + g * 256, 256, 256)], 8, 128)
            for tt in tts:
                ps = S.psum()
                mm(ps, 128, 512, lambda k, tt=tt: hT[:, k, tt * 128:(tt + 1) * 128], lambda k, wt=wt: wt[:, k, :], 8, [wk, "hT"])
                evac(stg[:, :], ps, 128, 512, [], ["stg"])
                r0 = T0 + tt * 128 - (SEQ - win)
                S.dma("sp", O["p_k%d" % (g + 1)][r0:r0 + 128, :], stg[:, 0:256], reads=["stg"], writes=[])
                S.dma("sp", O["p_v%d" % (g + 1)][r0:r0 + 128, :], stg[:, 256:512], reads=["stg"], writes=[])

    YB = [S.banks[6], S.banks[7]]
    S.nrot = 6
    SC_MEM = 128.0 ** -0.5

    def dense_fm(Wk, col0, ncols, kc, krows, rhs_fn, rkeys, n, cb):
        for c0 in range(0, ncols, 512):
            w = min(512, ncols - c0)
            wt, wk = wload([(WB[Wk], Wk + "_bf", 0, col0 + c0, w, 0)], kc, krows)
            for mi in range(w // 128):
                ps = S.psum()
                mm(ps, 128, n, lambda k, wt=wt, mi=mi: wt[:krows, k, mi * 128:(mi + 1) * 128], rhs_fn, kc, [wk] + rkeys)
                cb((c0 // 128) + mi, ps)

    def gates_to_arena(b, rhs_fn, rkeys, n):
        def cb(m, ps):
            S.op("act", lambda e: e.activation(arena[:, m, :n], ps.ap[:, :n], AF.Sigmoid), reads=[ps.key], writes=[AR(m)])
        dense_fm("w_in", OFF_G + b * D, D, 8, 128, rhs_fn, rkeys, n, cb)

    def merge_cb(first, n):
        def cb(m, ps):
            if first:
                S.op("dve", lambda e: e.tensor_tensor(arena[:, 8 + m, :n], ps.ap[:, :n], arena[:, m, :n], ALU.mult),
                     reads=[ps.key, AR(m)], writes=[AR(8 + m)])
            else:
                S.op("dve", lambda e: e.tensor_tensor(stg[:, :n], ps.ap[:, :n], arena[:, m, :n], ALU.mult),
                     reads=[ps.key, AR(m)], writes=["stg"])
                S.op("pool", lambda e: e.tensor_tensor(arena[:, 8 + m, :n], arena[:, 8 + m, :n], stg[:, :n], ALU.add),
                     reads=["stg", AR(8 + m)], writes=[AR(8 + m)])
        return cb

    def glu_branch(zt, n):
        for half in range(2):
            wA, kA = wload([(WB["w_glu"], "w_glu_bf", 0, half * 512, 512, 0)], 4, 128)
            wB, kB = wload([(WB["w_glu"], "w_glu_bf", 0, D + half * 512, 512, 0)], 4, 128)
            for mi in range(4):
                m = half * 4 + mi
                pa, pb_ = S.psum(), S.psum()
                mm(pa, 128, n, lambda k, mi=mi, wA=wA: wA[:, k, mi * 128:(mi + 1) * 128], lambda k: zt[:, k, :n], 4, [kA, "zT"])
                mm(pb_, 128, n, lambda k, mi=mi, wB=wB: wB[:, k, mi * 128:(mi + 1) * 128], lambda k: zt[:, k, :n], 4, [kB, "zT"])
                S.op("act", lambda e, pb_=pb_: e.activation(stg[:, :n], pb_.ap[:, :n], AF.Sigmoid), reads=[pb_.key], writes=["stg"])
                S.op("dve", lambda e, pa=pa: e.tensor_tensor(stg[:, :n], pa.ap[:, :n], stg[:, :n], ALU.mult), reads=[pa.key, "stg"], writes=["stg"])
                S.op("dve", lambda e, m=m: e.tensor_tensor(arena[:, 8 + m, :n], stg[:, :n], arena[:, m, :n], ALU.mult),
                     reads=["stg", AR(m)], writes=[AR(8 + m)])

    def out_proj_tm(Wk, kc_total, lhs_fn, lkeys, rows):
        pieces = [(k0, min(8, kc_total - k0)) for k0 in range(0, kc_total, 8)]
        ntt = 4 if rows == 128 else 1
        for cg in range(2):
            banks = [S.psum() for _ in range(ntt)]
            for pi, (k0, kc) in enumerate(pieces):
                wt, wk = wload([(WB[Wk], Wk + "_bf", k0 * 128, cg * 512, 512, 0)], kc, 128)
                for tt in range(ntt):
                    mm(banks[tt], rows, 512, lambda k, tt=tt, k0=k0: lhs_fn(tt, k0 + k), lambda k, wt=wt: wt[:, k, :], kc,
                       [wk] + lkeys, start=(pi == 0), stop=(pi == len(pieces) - 1))
            for tt in range(ntt):
                S.op("dve", lambda e, tt=tt, cg=cg, b=banks[tt]: e.tensor_tensor(
                    xres[:rows, tt, cg * 512:(cg + 1) * 512], xres[:rows, tt, cg * 512:(cg + 1) * 512], b.ap[:rows, :512], ALU.add),
                    reads=[banks[tt].key, ("x", tt)], writes=[("x", tt)])

    def norm_to_hT(gname, ntt, rows, dst):
        load_gbc(gname)
        for tt in range(ntt):
            rmsnorm_rows(xres[:rows, tt, :], rows, [("x", tt)], xnb[:rows, :], ["xnb"])
            transpose_to(lambda k, tt=tt: dst[:, k, tt * rows:(tt + 1) * rows], xnb, rows, 8, ["xnb"], ["hT"])

    def ffn(n, a_hook):
        for s in range(NF // 2):
            wt, wk = wload([(WB["w_up"], "w_up_bf", 0, 256 * s, 256, 0), (WB["w_up"], "w_up_bf", 0, DFF + 256 * s, 256, 256)], 8, 128)
            for jj in range(2):
                f = 2 * s + jj
                pa, pv = S.psum(), S.psum()
                mm(pa, 128, n, lambda k, wt=wt, jj=jj: wt[:, k, jj * 128:(jj + 1) * 128], lambda k: hT[:, k, :n], 8, [wk, "hT"])
                mm(pv, 128, n, lambda k, wt=wt, jj=jj: wt[:, k, 256 + jj * 128:256 + (jj + 1) * 128], lambda k: hT[:, k, :n], 8, [wk, "hT"])
                a_hook(f, pa)
                S.op("act", lambda e: e.activation(af[:, 2, :n], af[:, 1, :n], AF.Gelu_apprx_tanh), reads=[AFK(1)], writes=[AFK(2)])
                S.op("dve", lambda e, f=f, pv=pv: e.tensor_tensor(arena[:, f, :n], af[:, 2, :n], pv.ap[:, :n], ALU.mult),
                     reads=[AFK(2), pv.key], writes=[AR(f)])

    def final_norm_out(rows, ntt, out_fn):
        load_gbc("fin_g")
        for tt in range(ntt):
            rmsnorm_rows(xres[:rows, tt, :], rows, [("x", tt)], ytmp[:rows, :], ["ytmp"])
            S.dma("sp", out_fn(tt), ytmp[:rows, :], reads=["ytmp"], writes=[])

    for blk in range(nblk):
        T0, par = blk * TOK, blk % 2
        for tt in range(4):
            S.dma("sp", xres[:, tt, :], I["xp"][T0 + tt * 128:T0 + (tt + 1) * 128, :], writes=[("x", tt)])
        norm_to_hT("norm1_g", 4, 128, hT)
        hrhs = lambda k: hT[:, k, :]

        def inproj_cb(base_tile):
            def cb(m, ps):
                f = base_tile + m
                if f < 4:
                    evac(ubf[:, f, :], ps, 128, TOK, [], ["ubf"])
                elif f < 10:
                    evac(arena[:, 16 + f - 4, :], ps, 128, TOK, [], [AR(16 + f - 4)], scale=0.125)
                elif f < 16:
                    kt = f - 10
                    g, pair = kt // 2, kt % 2
                    if g < 2:
                        evac(kT12[:, g, pair, par, :], ps, 128, TOK, [], ["kT12"])
                    else:
                        evac(kT3[:, pair, T0:T0 + TOK], ps, 128, TOK, [], ["kT3"])
                else:
                    evac(qmT[:, f - 22, :], ps, 128, TOK, [], ["qmT"])
            return cb
        if stage < 1:
            window_outputs(blk)
            continue
        dense_fm("w_in", 0, 2048, 8, 128, hrhs, ["hT"], TOK, inproj_cb(0))
        dense_fm("w_in", OFF_QM, 512, 8, 128, hrhs, ["hT"], TOK, inproj_cb(22))
        if blk == 0:
            dbg("hT", hT[:, :, :], [128, 8, TOK], ["hT"], BF16)
            dbg("ubf", ubf[:, :, :], [128, 4, TOK], ["ubf"], BF16)
            dbg("qT", arena[:, 16:22, :], [128, 6, TOK], [AR(16 + i) for i in range(6)], BF16)
            dbg("qmT", qmT[:, :, :], [128, 4, TOK], ["qmT"], BF16)

        if stage < 1.5:
            window_outputs(blk)
            continue
        for g in range(3 if stage >= 1.7 else (2 if stage >= 1.6 else 1)):
            wt, wk = wload([(WB["w_in"], "w_in_bf", 0, OFF_V + g * 256, 256, 0)], 8, 128)
            if g < 2:
                for c in range(4):
                    ps = S.psum()
                    if g == 0:
                        lf = lambda k, c=c: hT[:, k, c * 128:(c + 1) * 128]
                    else:
                        lf = lambda k, c=c: hT[:, k, :].rearrange("p (i r) -> p r i", r=4)[:, c, :]
                    mm(ps, 128, 256, lf, lambda k, wt=wt: wt[:, k, 0:256], 8, [wk, "hT"])
                    evac((V1 if g == 0 else V2)[:, par, c, :], ps, 128, 256, [], ["V1" if g == 0 else "V2"])
            else:
                off, kt = (32 * blk) % 128, blk // 4
                for c2 in range(8):
                    ps = S.psum()

                    def vfn(e, ps=ps, c2=c2, wt=wt):
                        ins = None
                        for rr in range(2):
                            for k in range(8):
                                ins = e.matmul(ps.ap[:32, rr * 256:(rr + 1) * 256],
                                               hT[:, k, :].rearrange("p (i r) -> p r i", r=16)[:, 2 * c2 + rr, :],
                                               wt[:, k, 0:256], start=(k == 0), stop=(k == 7))
                        return ins
                    S.op("pe", vfn, reads=[wk, "hT"], writes=[ps.key])
                    for rr in range(2):
                        vb, vk = vst[rr], "vst%d" % rr
                        S.op("dve", lambda e, ps=ps, vb=vb, rr=rr: e.tensor_copy(vb[:32, :], ps.ap[:32, rr * 256:(rr + 1) * 256]),
                             reads=[ps.key], writes=[vk])
                        if stage != 1.75:
                            S.dma("sp", V3[off:off + 32, 2 * c2 + rr, kt, :], vb[:32, :], reads=[vk], writes=["V3"])
        window_outputs(blk)
        if stage < 2:
            continue

        gates_to_arena(0, hrhs, ["hT"], TOK)
        def ssm_gp(gp):
            t = gp // 4
            tb = tabt[gp % 2]
            tk = "tabt%d" % (gp % 2)
            S.dma("sp", tb[:, :, :], tabs_d[gp].rearrange("p (a b) -> p a b", b=512), reads=["tabs_d"], writes=[tk])
            cos, sin = tb[:, 0, :], tb[:, 1, :]
            pr, pi_ = S.psum(), S.psum()
            mm(pr, 128, TOK, lambda k, gp=gp: bbT[:, gp, 0, :], lambda k, t=t: ubf[:, t, :], 1, ["bbT", "ubf"])
            mm(pi_, 128, TOK, lambda k, gp=gp: bbT[:, gp, 1, :], lambda k, t=t: ubf[:, t, :], 1, ["bbT", "ubf"])
            T = [af[:, j, 0:TOK] for j in range(4)]
            tt_ = lambda o, a, b_, op, rk, wk_, eng="dve": S.op(eng, lambda e: e.tensor_tensor(o, a, b_, op), reads=rk, writes=wk_)
            tt_(T[0], pr.ap[:, :TOK], cos, ALU.mult, [pr.key, tk], [AFK(0)])
            tt_(T[1], pi_.ap[:, :TOK], sin, ALU.mult, [pi_.key, tk], [AFK(1)])
            tt_(T[0], T[0], T[1], ALU.add, [AFK(0), AFK(1)], [AFK(0)])
            tt_(T[1], pi_.ap[:, :TOK], cos, ALU.mult, [pi_.key, tk, AFK(0)], [AFK(1)])
            tt_(T[2], pr.ap[:, :TOK], sin, ALU.mult, [pr.key, tk], [AFK(2)])
            tt_(T[1], T[1], T[2], ALU.subtract, [AFK(1), AFK(2)], [AFK(1)])
            magb = mag[:, gp:gp + 1].to_broadcast([128, TOK])
            S.op("dve", lambda e, gp=gp, magb=magb: e.tensor_tensor_scan(T[2], magb, T[0], rinit[:, 0, gp:gp + 1], ALU.mult, ALU.add),
                 reads=["mag", AFK(0), "rinit", AFK(1)], writes=[AFK(2)])
            S.op("dve", lambda e, gp=gp, magb=magb: e.tensor_tensor_scan(T[3], magb, T[1], rinit[:, 1, gp:gp + 1], ALU.mult, ALU.add),
                 reads=["mag", AFK(1), "rinit"], writes=[AFK(3)])
            S.op("act", lambda e, gp=gp: e.activation(rlast[:, 0, gp:gp + 1], af[:, 2, TOK - 1:TOK], AF.Copy), reads=[AFK(2)], writes=["rlast"])
            S.op("act", lambda e, gp=gp: e.activation(rlast[:, 1, gp:gp + 1], af[:, 3, TOK - 1:TOK], AF.Copy), reads=[AFK(3)], writes=["rlast"])
            s0, s1 = sbf[0], sbf[1]
            tt_(T[0], T[2], cos, ALU.mult, [AFK(2), tk], [AFK(0)])
            tt_(T[1], T[3], sin, ALU.mult, [AFK(3), tk], [AFK(1)])
            tt_(s0[:, :], T[0], T[1], ALU.subtract, [AFK(0), AFK(1)], ["sbf0"])
            tt_(T[0], T[2], sin, ALU.mult, [AFK(2), tk, "sbf0"], [AFK(0)])
            tt_(T[1], T[3], cos, ALU.mult, [AFK(3), tk, "sbf0"], [AFK(1)])
            tt_(s1[:, :], T[0], T[1], ALU.add, [AFK(0), AFK(1)], ["sbf1"])
            yb = YB[t % 2]

            def yfn(e, gp=gp, t=t, yb=yb):
                e.matmul(yb.ap[:, :TOK], cT[:, gp, 0, :], sbf[0][:, :], start=(gp % 4 == 0), stop=False)
                ins = e.matmul(yb.ap[:, :TOK], cT[:, gp, 1, :], sbf[1][:, :], start=False, stop=False)
                if gp % 4 == 3:
                    ins = e.matmul(yb.ap[:, :TOK], dD[:, t, :], ubf[:, t, :], start=False, stop=True)
                return ins
            S.op("pe", yfn, reads=["cT", "sbf0", "sbf1", "dD", "ubf"], writes=[yb.key])
            if gp % 4 == 3:
                S.op("act", lambda e, t=t, yb=yb: e.activation(zT[:, t, :], yb.ap[:, :TOK], AF.Gelu_apprx_tanh), reads=[yb.key], writes=["zT"])
        accn, accd = ytmp[:64, 0:TOK], ytmp[:64, TOK:2 * TOK]

        def att_unit(j, g):
            pair, pb = j // 2, (j % 2) * 64
            if j == 0 and g == 0:
                S.op("dve", lambda e: e.memset(small[:, 62:63], 0.0), writes=["gbc"])
            if True:
                dil = DILS[g]
                sig = SLOPES[g * 4 + j] * dil
                if g < 2:
                    ncg, Q = 4, 128
                else:
                    ncg, Q = 16, 32
                kt3, v3 = blk // 4, blk % 4
                qfull = arena[pb:pb + 64, 16 + 2 * g + pair, :]
                qsl = (lambda c, qfull=qfull: qfull[:, c * 128:(c + 1) * 128]) if g == 0 else \
                      (lambda c, qfull=qfull, dil=dil: qfull.rearrange("p (i r) -> p r i", r=dil)[:, c, :])
                def kcur(c, g=g, pair=pair, pb=pb):
                    if g == 0:
                        return kT12[pb:pb + 64, 0, pair, par, c * 128:(c + 1) * 128]
                    if g == 1:
                        return kT12[pb:pb + 64, 1, pair, par, :].rearrange("p (i r) -> p r i", r=4)[:, c, :]
                    return kT3[pb:pb + 64, pair, kt3 * 2048:(kt3 + 1) * 2048].rearrange("p (i r) -> p r i", r=16)[:, c, 0:32 * v3 + 32]

                def kprev(c, g=g, pair=pair, pb=pb):
                    if g == 0:
                        if c > 0:
                            return kT12[pb:pb + 64, 0, pair, par, (c - 1) * 128:c * 128]
                        return kT12[pb:pb + 64, 0, pair, 1 - par, 384:512] if blk > 0 else None
                    if g == 1:
                        return kT12[pb:pb + 64, 1, pair, 1 - par, :].rearrange("p (i r) -> p r i", r=4)[:, c, :] if blk > 0 else None
                    if kt3 == 0:
                        return None
                    return kT3[pb:pb + 64, pair, (kt3 - 1) * 2048:kt3 * 2048].rearrange("p (i r) -> p r i", r=16)[:, c, :]

                def vcur(c, g=g, j=j):
                    if g == 0:
                        return V1[:, par, c, j * 64:(j + 1) * 64]
                    if g == 1:
                        return V2[:, par, c, j * 64:(j + 1) * 64]
                    return V3[0:32 * v3 + 32, c, kt3, j * 64:(j + 1) * 64]

                def vprev(c, g=g, j=j):
                    if g == 0:
                        return V1[:, par, c - 1, j * 64:(j + 1) * 64] if c > 0 else V1[:, 1 - par, 3, j * 64:(j + 1) * 64]
                    if g == 1:
                        return V2[:, 1 - par, c, j * 64:(j + 1) * 64]
                    return V3[:, c, kt3 - 1, j * 64:(j + 1) * 64]
                rows_cur = 128 if g < 2 else 32 * v3 + 32
                if g < 2:
                    baseA, baseB = base12[:, 0, :], base12[:rows_cur, 1, :]
                else:
                    baseA, baseB = base3[:, 2 * v3, :], base3[:rows_cur, 2 * v3 + 1, :]
                chunks = []
                for (rows, kf, vf, base, nm) in ((128, kprev, vprev, baseA, 0), (rows_cur, kcur, vcur, baseB, 1)):
                    cols = [c for c in range(ncg) if kf(c) is not None]
                    if not cols:
                        continue
                    ps = S.psum()
                    kaps = {c: kf(c) for c in cols}
                    qaps = {c: qsl(c) for c in cols}
                    vaps = {c: vf(c) for c in cols}

                    def qk(e, ps=ps, cols=cols, kaps=kaps, qaps=qaps, rows=rows, Q=Q):
                        ins = None
                        for c in cols:
                            ins = e.matmul(ps.ap[:rows, c * Q:(c + 1) * Q], kaps[c], qaps[c], start=True, stop=True)
                        return ins
                    S.op("pe", qk, reads=["kT12", "kT3", AR(16 + 2 * g + pair)], writes=[ps.key])
                    stmp = gbc[:rows, nm * TOK:(nm + 1) * TOK]
                    S.op("dve", lambda e, ps=ps, rows=rows, base=base, stmp=stmp, sig=sig, ncg=ncg, Q=Q: e.scalar_tensor_tensor(
                        stmp.rearrange("p (c b) -> p c b", b=Q), base.unsqueeze(1).to_broadcast([rows, ncg, Q]), sig,
                        ps.ap[:rows, :TOK].rearrange("p (c b) -> p c b", b=Q), ALU.mult, ALU.add),
                        reads=[ps.key, "base12", "base3"], writes=["gbc%d" % nm])
                    S.op("act", lambda e, rows=rows, stmp=stmp, nm=nm: e.activation(pT[nm][:rows, :], stmp, AF.Exp),
                         reads=["gbc%d" % nm], writes=["pT%d" % nm])
                    chunks.append((rows, vaps, nm, cols))
                pn, pd = S.psum(), S.psum()

                def pv(e, pn=pn, chunks=chunks, ncg=ncg, Q=Q, ones=False):
                    ins = None
                    for c in range(ncg):
                        cs = [ch for ch in chunks if c in ch[3]]
                        for ci, (rows, vf, nm, cols) in enumerate(cs):
                            lhs = ones_bf[:rows, 0:64] if ones else vf[c]
                            ins = e.matmul(pn.ap[:64, c * Q:(c + 1) * Q], lhs, pT[nm][:rows, c * Q:(c + 1) * Q],
                                           start=(ci == 0), stop=(ci == len(cs) - 1))
                    return ins
                S.op("pe", pv, reads=["V1", "V2", "V3", "pT0", "pT1"], writes=[pn.key])
                S.op("pe", lambda e, pd=pd, pv=pv: pv(e, pn=pd, ones=True), reads=["ones_bf", "pT0", "pT1"], writes=[pd.key])
                for (acc, pz, ak) in ((accn, pn, "ytmp"), (accd, pd, "ytmp")):
                    if g == 0:
                        av = acc.rearrange("p (c b) -> p c b", b=128)
                    else:
                        av = acc.rearrange("p (i r) -> p r i", r=dil)
                    pzv = pz.ap[:64, :TOK].rearrange("p (c b) -> p c b", b=Q)
                    if g == 0:
                        S.op("act", lambda e, av=av, pzv=pzv: e.activation(av, pzv, AF.Copy), reads=[pz.key], writes=[ak])
                    else:
                        S.op("dve", lambda e, av=av, pzv=pzv: e.tensor_tensor(av, av, pzv, ALU.add), reads=[pz.key, ak], writes=[ak])

        def att_fin(j):
            S.op("dve", lambda e: e.reciprocal(accd, accd), reads=["ytmp"], writes=["ytmp"])
            S.op("dve", lambda e, j=j: e.tensor_tensor(oT[:, j, :], accn, accd, ALU.mult), reads=["ytmp"], writes=["oT"])
        def mem_unit(h):
            for c in range(2):
                ps = S.psum()
                mm(ps, 128, TOK, lambda k, h=h, c=c: mkT[:, h, c * 128:(c + 1) * 128], lambda k, h=h: qmT[:, h, :], 1, ["mkT", "qmT"])
                S.op("act", lambda e, ps=ps, c=c: e.activation(pTm[:, c, :], ps.ap[:, :TOK], AF.Exp, scale=SC_MEM), reads=[ps.key], writes=["pTm"])
            pn, pd = S.psum(), S.psum()
            mm(pn, 128, TOK, lambda k, h=h: mv[:, k, h * 128:(h + 1) * 128], lambda k: pTm[:, k, :], 2, ["mv", "pTm"])
            mm(pd, 128, TOK, lambda k: ones_bf[:, :], lambda k: pTm[:, k, :], 2, ["ones_bf", "pTm"])
            S.op("dve", lambda e, pd=pd: e.reciprocal(stg[:, :], pd.ap[:, :TOK]), reads=[pd.key], writes=["stg"])
            S.op("dve", lambda e, pn=pn, h=h: e.tensor_tensor(omT[:, h, :], pn.ap[:, :TOK], stg[:, :], ALU.mult),
                 reads=[pn.key, "stg"], writes=["omT"])
        units = []
        for j in range(4):
            for g in range(3):
                units.append((j, g))
        for gp in range(16):
            ssm_gp(gp)
            if stage >= 4 and gp < 12:
                j, g = units[gp]
                att_unit(j, g)
                if g == 2:
                    att_fin(j)
            elif stage >= 5 and gp >= 12:
                mem_unit(gp - 12)
        def cmul(dst, E, rk, wk_):
            S.op("dve", lambda e: e.tensor_mul(cst[:, 7, :], E[:, 0, :], rlast[:, 0, :]), reads=rk + ["rlast"], writes=["cst"])
            S.op("dve", lambda e: e.tensor_mul(cst[:, 8, :], E[:, 1, :], rlast[:, 1, :]), reads=rk + ["rlast"], writes=["cst"])
            S.op("dve", lambda e: e.tensor_sub(dst[:, 0, :], cst[:, 7, :], cst[:, 8, :]), reads=["cst"], writes=wk_)
            S.op("dve", lambda e: e.tensor_mul(cst[:, 7, :], E[:, 1, :], rlast[:, 0, :]), reads=rk + ["rlast"] + wk_, writes=["cst"])
            S.op("dve", lambda e: e.tensor_mul(cst[:, 8, :], E[:, 0, :], rlast[:, 1, :]), reads=rk + ["rlast"], writes=["cst"])
            S.op("dve", lambda e: e.tensor_add(dst[:, 1, :], cst[:, 7, :], cst[:, 8, :]), reads=["cst"], writes=wk_)
        if blk < NBLK - 1:
            cmul(rinit, e512, ["ecst"], ["rinit"])
        else:
            cmul(rinit, e511, ["ecst"], ["rinit"])
            for ri, oname in ((0, "p_sre"), (1, "p_sim")):
                ps = S.psum()
                S.op("pe", lambda e, ps=ps, ri=ri: e.transpose(ps.ap[:16, :128], rinit[:, ri, :], ident_f[:, :]),
                     reads=["rinit", "ident_f"], writes=[ps.key])
                S.op("dve", lambda e, ps=ps: e.tensor_copy(stg[:16, :128], ps.ap[:16, :128]), reads=[ps.key], writes=["stg"])
                S.dma("sp", O[oname], stg[:16, :128], reads=["stg"], writes=[])

        glu_branch(zT, TOK)
        if blk == 0:
            dbg("zT", zT[:, :, :], [128, 4, TOK], ["zT"], BF16)
            dbg("m1", arena[:, 8:16, :], [128, 8, TOK], [AR(8 + i) for i in range(8)], BF16)

        gates_to_arena(1, hrhs, ["hT"], TOK)
        dense_fm("w_atto", 0, D, 4, 64, lambda k: oT[:, k, :], ["oT"], TOK, merge_cb(False, TOK))
        if blk == 0:
            dbg("oT", oT[:, :, :], [64, 4, TOK], ["oT"], BF16)
            dbg("m2", arena[:, 8:16, :], [128, 8, TOK], [AR(8 + i) for i in range(8)], BF16)

        if stage < 5:
            continue
        gates_to_arena(2, hrhs, ["hT"], TOK)
        dense_fm("w_memo", 0, D, 4, 128, lambda k: omT[:, k, :], ["omT"], TOK, merge_cb(False, TOK))
        if blk == 0:
            dbg("omT", omT[:, :, :], [128, 4, TOK], ["omT"], BF16)
            dbg("m3", arena[:, 8:16, :], [128, 8, TOK], [AR(8 + i) for i in range(8)], BF16)

        if stage < 6:
            continue
        out_proj_tm("w_out", 8, lambda tt, k: arena[:, 8 + k, tt * 128:(tt + 1) * 128], [AR(8 + k) for k in range(8)], 128)
        if blk == 0:
            dbg("xmid", xres[:, :, :], [128, 4, D], [("x", i) for i in range(4)])
        norm_to_hT("norm2_g", 4, 128, hT)

        def conv_hook(f, pa):
            at = af[:, 0, :]
            S.op("act", lambda e: e.activation(at[:, 2:TOK + 2], pa.ap[:, :TOK], AF.Copy), reads=[pa.key], writes=[AFK(0)])
            S.op("pool", lambda e, f=f: e.tensor_copy(at[:, 0:2], carry[:, f, :]), reads=["carry"], writes=[AFK(0)])
            S.op("pool", lambda e, f=f: e.tensor_copy(carry[:, f, :], at[:, TOK:TOK + 2]), reads=[AFK(0)], writes=["carry"])
            cc = af[:, 1, 0:TOK]
            S.op("dve", lambda e, f=f: e.tensor_scalar(cc, at[:, 2:TOK + 2], cwT[:, f, 2:3], cwT[:, f, 3:4], ALU.mult, ALU.add),
                 reads=[AFK(0), "cwT"], writes=[AFK(1)])
            S.op("dve", lambda e, f=f: e.scalar_tensor_tensor(cc, at[:, 1:TOK + 1], cwT[:, f, 1:2], cc, ALU.mult, ALU.add),
                 reads=[AFK(0), AFK(1), "cwT"], writes=[AFK(1)])
            S.op("dve", lambda e, f=f: e.scalar_tensor_tensor(cc, at[:, 0:TOK], cwT[:, f, 0:1], cc, ALU.mult, ALU.add),
                 reads=[AFK(0), AFK(1), "cwT"], writes=[AFK(1)])
        ffn(TOK, conv_hook)
        if blk == 0:
            dbg("gT", arena[:, :, :], [128, NF, TOK], [AR(i) for i in range(NF)], BF16)
        out_proj_tm("w_down", NF, lambda tt, k: arena[:, k, tt * 128:(tt + 1) * 128], [AR(k) for k in range(NF)], 128)
        if blk == 0:
            dbg("xfin", xres[:, :, :], [128, 4, D], [("x", i) for i in range(4)])
        final_norm_out(128, 4, lambda tt: O["y_p"][T0 + tt * 128:T0 + (tt + 1) * 128, :])
    for tcol in range(2):
        ps = S.psum()
        S.op("dve", lambda e, tcol=tcol: e.tensor_copy(stg[:, 0:NF], carry[:, :, tcol]), reads=["carry"], writes=["stg"])
        S.op("pe", lambda e, ps=ps: e.transpose(ps.ap[:NF, :128], stg[:, 0:NF], ident_f[:, :]), reads=["stg", "ident_f"], writes=[ps.key])
        S.op("dve", lambda e, ps=ps: e.tensor_copy(ytmp[:NF, 0:128], ps.ap[:NF, :128]), reads=[ps.key], writes=["ytmp"])
        S.dma("sp", O["p_conv"][tcol:tcol + 1, :].rearrange("t (f q) -> (t f) q", q=128), ytmp[:NF, 0:128], reads=["ytmp"], writes=[])

    n = NS
    S.op("dve", lambda e: e.memset(small[:, 63:64], 0.0), writes=["kT3", "V3", "kT12", "V1", "V2"])
    PA = kT3[:, :, :].bitcast(F32).rearrange("p a b -> p (a b)")
    PB = V3[:, :, :, :].bitcast(F32).rearrange("p a b c -> p (a b c)")
    z_tok = PA[:n, 0:2816]
    ZQ, ZK, ZV, ZM = 0, 768, 1536, 2304
    s0raw = PA[:n, 2816:2816 + 1024].rearrange("p (a b) -> p a b", b=512)
    PAb = PA[:, 3840:4096]
    zq_bf = PB[:n, 0:640].bitcast(BF16)
    vt_bf = PB[:n, 640:1024].bitcast(BF16)
    s0T = PB[:, 1024:1536].rearrange("p (r g b) -> p r g b", r=2, g=16)
    snw = PB[:, 1536:2048].rearrange("p (r g b) -> p r g b", r=2, g=16)
    snb = PB[:, 2048:2304].bitcast(BF16).rearrange("p (r g b) -> p r g b", r=2, g=16)
    scT = PB[:, 2304:2496].rearrange("p (b c) -> p b c", c=12)
    pTs = PB[:, 2496:2592].bitcast(BF16).rearrange("p (b c) -> p b c", c=12)
    scm = PB[:, 2592:2720].rearrange("p (c b h) -> p c b h", c=2, b=16)
    pmb = PB[:, 2720:2784].bitcast(BF16).rearrange("p (c b h) -> p c b h", c=2, b=16)
    kbuf = [PB[:, 2784 + i * 512:2784 + (i + 1) * 512] for i in range(2)]
    SK = ["skey%d" % i for i in range(12)]
    selb = sb("selb", [16, 16, 128], BF16)
    tabs_s = sb("tabs_s", [128, 12], F32)
    vbb = [sb("vbb%d" % i, [128, 512], BF16) for i in range(2)]
    prd = sb("prd", [128, 512], F32)
    cbT = sb("cbT", [128, NF, 2, NS], F32)
    anT = sb("anT", [128, NF, NS], F32)
    S.dma("pool", selb[:, :, :], C["sel"].rearrange("p (a b) -> p a b", b=128), writes=["selb"])
    S.dma("sp", tabs_s[:], C["tabs_s"], writes=["tabs_s"])

    S.dma("sp", xres[:n, 0, :], I["xs"], writes=[("x", 0)])
    norm_to_hT("norm1_g", 1, n, hT)
    hr = lambda k: hT[:, k, :n]
    def ucb(m, ps):
        evac(ubf[:, m, :n], ps, 128, n, [], ["ubf"])
    dense_fm("w_in", 0, 512, 8, 128, hr, ["hT"], n, ucb)
    for ci, c0 in enumerate(range(OFF_Q, OFF_G, 512)):
        w = min(512, OFF_G - c0)
        wt, wk = wload([(WB["w_in"], "w_in_bf", 0, c0, w, 0)], 8, 128)
        ps = S.psum()
        mm(ps, n, w, lambda k: hT[:, k, :n], lambda k, wt=wt, w=w: wt[:, k, :w], 8, [wk, "hT"])
        S.op("dve", lambda e, ps=ps, ci=ci, w=w: e.tensor_copy(z_tok[:, ci * 512:ci * 512 + w], ps.ap[:n, :w]), reads=[ps.key], writes=["z_tok"])
    for g in range(3):
        S.dma("sp", O["s_k%d" % (g + 1)], z_tok[:, ZK + g * 256:ZK + (g + 1) * 256], reads=["z_tok"], writes=[])
        S.dma("sp", O["s_v%d" % (g + 1)], z_tok[:, ZV + g * 256:ZV + (g + 1) * 256], reads=["z_tok"], writes=[])
    S.op("dve", lambda e: e.tensor_copy(zq_bf[:, 0:768], z_tok[:, ZQ:ZQ + 768]), reads=["z_tok"], writes=["zq_bf"])
    S.op("dve", lambda e: e.tensor_copy(zq_bf[:, 768:1280], z_tok[:, ZM:ZM + 512]), reads=["z_tok"], writes=["zq_bf"])
    S.op("dve", lambda e: e.tensor_copy(vt_bf[:, :], z_tok[:, ZV:ZV + 768]), reads=["z_tok"], writes=["vt_bf"])

    for ri, nm in ((0, "sre"), (1, "sim")):
        for q4 in range(4):
            S.dma("sp", s0raw[:, q4 % 2, :], I[nm][:, q4 * 512:(q4 + 1) * 512], writes=["s0raw%d" % (q4 % 2)])
            ps = S.psum()

            def trs(e, ps=ps, q4=q4):
                ins = None
                for gg in range(4):
                    ins = e.transpose(ps.ap[:, gg * 16:(gg + 1) * 16], s0raw[:, q4 % 2, gg * 128:(gg + 1) * 128], ident_f[:n, :n])
                return ins
            S.op("pe", trs, reads=["s0raw%d" % (q4 % 2), "ident_f"], writes=[ps.key])
            S.op("dve", lambda e, ps=ps, ri=ri, q4=q4: e.tensor_copy(s0T[:, ri, 4 * q4:4 * q4 + 4, :],
                                                                       ps.ap[:, 0:64].rearrange("p (g b) -> p g b", b=16)),
                 reads=[ps.key], writes=["s0T"])
    pbr, pbi = S.psum(), S.psum()
    for ri, pz in ((0, pbr), (1, pbi)):
        def bufn(e, ri=ri, pz=pz):
            ins = None
            for gp in range(16):
                ins = e.matmul(pz.ap[:, gp * 16:(gp + 1) * 16], bbT[:, gp, ri, :], ubf[:, gp // 4, :n], start=True, stop=True)
            return ins
        S.op("pe", bufn, reads=["bbT", "ubf"], writes=[pz.key])
    abr = ab[:, 0, :].unsqueeze(2).to_broadcast([128, 16, 16])
    abi = ab[:, 1, :].unsqueeze(2).to_broadcast([128, 16, 16])
    t1 = af[:, 0, 0:256].rearrange("p (g b) -> p g b", b=16)
    t2_ = af[:, 1, 0:256].rearrange("p (g b) -> p g b", b=16)
    pv3 = lambda pz: pz.ap[:, 0:256].rearrange("p (g b) -> p g b", b=16)
    dv = lambda fn, rk, wk_: S.op("dve", fn, reads=rk, writes=wk_)
    dv(lambda e: e.tensor_tensor(t1, s0T[:, 0], abr, ALU.mult), ["s0T", "ab"], [AFK(0)])
    dv(lambda e: e.tensor_tensor(t2_, s0T[:, 1], abi, ALU.mult), ["s0T", "ab"], [AFK(1)])
    dv(lambda e: e.tensor_tensor(t1, t1, t2_, ALU.subtract), [AFK(0), AFK(1)], [AFK(0)])
    dv(lambda e: e.tensor_tensor(snw[:, 0], t1, pv3(pbr), ALU.add), [AFK(0), pbr.key], ["snw"])
    dv(lambda e: e.tensor_tensor(t1, s0T[:, 1], abr, ALU.mult), ["s0T", "ab", "snw"], [AFK(0)])
    dv(lambda e: e.tensor_tensor(t2_, s0T[:, 0], abi, ALU.mult), ["s0T", "ab", "snw"], [AFK(1)])
    dv(lambda e: e.tensor_tensor(t1, t1, t2_, ALU.add), [AFK(0), AFK(1)], [AFK(0)])
    dv(lambda e: e.tensor_tensor(snw[:, 1], t1, pv3(pbi), ALU.add), [AFK(0), pbi.key], ["snw"])
    dv(lambda e: e.tensor_copy(snb[:, :, :, :], snw[:, :, :, :]), ["snw"], ["snb"])
    for ri, oname in ((0, "s_sre"), (1, "s_sim")):
        for q4 in range(4):
            ps = S.psum()

            def trb(e, ps=ps, ri=ri, q4=q4):
                ins = None
                for gg in range(4):
                    ins = e.transpose(ps.ap[:n, gg * 128:(gg + 1) * 128], snw[:, ri, 4 * q4 + gg, :], ident_f[:, :])
                return ins
            S.op("pe", trb, reads=["snw", "ident_f"], writes=[ps.key])
            S.op("dve", lambda e, ps=ps: e.tensor_copy(stg[:n, :], ps.ap[:n, :512]), reads=[ps.key], writes=["stg"])
            S.dma("sp", O[oname][:, q4 * 512:(q4 + 1) * 512], stg[:n, :], reads=["stg"], writes=[])
    for t in range(4):
        yb = YB[t % 2]

        def ysf(e, t=t, yb=yb):
            ins = None
            for gi in range(4):
                gp = 4 * t + gi
                e.matmul(yb.ap[:, :n], cT[:, gp, 0, :], snb[:, 0, gp, :], start=(gi == 0), stop=False)
                e.matmul(yb.ap[:, :n], cT[:, gp, 1, :], snb[:, 1, gp, :], start=False, stop=False)
            return e.matmul(yb.ap[:, :n], dD[:, t, :], ubf[:, t, :n], start=False, stop=True)
        S.op("pe", ysf, reads=["cT", "snb", "dD", "ubf"], writes=[yb.key])
        S.op("act", lambda e, t=t, yb=yb: e.activation(zT[:, t, :n], yb.ap[:, :n], AF.Gelu_apprx_tanh), reads=[yb.key], writes=["zT"])
    gates_to_arena(0, hr, ["hT"], n)
    glu_branch(zT, n)
    dbg("s_zT", zT[:, :, :n], [128, 4, n], ["zT"], BF16)
    dbg("s_m1", arena[:, 8:16, :n], [128, 8, n], [AR(8 + i) for i in range(8)], BF16)

    caches = [("ck1", "cv1"), ("ck2", "cv2"), ("ck3", "cv3")]
    for b in range(n):
        pq0, pq1 = S.psum(), S.psum()
        S.op("pe", lambda e, b=b, pq0=pq0: e.matmul(pq0.ap[:, :512], selb[:, b, :], zq_bf[:, 0:512], start=True, stop=True),
             reads=["selb", "zq_bf"], writes=[pq0.key])
        S.op("pe", lambda e, b=b, pq1=pq1: e.matmul(pq1.ap[:, :256], selb[:, b, :], zq_bf[:, 512:768], start=True, stop=True),
             reads=["selb", "zq_bf"], writes=[pq1.key])
        for g in range(3):
            kb, kk = kbuf[g % 2], "kbuf%d" % (g % 2)
            S.dma("sp", kb[:, 0:256], I[caches[g][0]][b].rearrange("(i d) c -> i d c", d=DILS[g])[:, 0, :], writes=[kk])
            qsrc = pq0.ap[:, g * 256:(g + 1) * 256] if g < 2 else pq1.ap[:, 0:256]
            S.op("dve", lambda e, kb=kb, qsrc=qsrc: e.tensor_tensor(prd[:, 0:256], kb[:, 0:256], qsrc, ALU.mult),
                 reads=[kk, pq0.key, pq1.key], writes=["prd"])
            S.op("dve", lambda e, b=b, g=g: e.tensor_reduce(scT[:, b, g * 4:(g + 1) * 4], prd[:, 0:256].rearrange("p (h e) -> p h e", e=64),
                                                           AX.X, ALU.add), reads=["prd"], writes=["scT"])
    S.op("dve", lambda e: e.scalar_tensor_tensor(scT[:, :, :], scT[:, :, :], 0.125, tabs_s[:, :].unsqueeze(1).to_broadcast([128, 16, 12]),
                                                  ALU.mult, ALU.add), reads=["scT", "tabs_s"], writes=["scT"])
    dbg("s_scT", scT[:, :, :], [128, 16, 12], ["scT"])
    S.op("act", lambda e: e.activation(pTs[:, :, :], scT[:, :, :], AF.Exp), reads=["scT"], writes=["pTs"])
    pnew = small[:n, 40:52]
    S.op("dve", lambda e: e.tensor_tensor(prd[:n, 0:384].rearrange("p (a b) -> p a b", b=1)[:, :, 0] if False else PAb[:n, 0:1], PAb[:n, 0:1], PAb[:n, 0:1], ALU.mult) if False else
         e.tensor_tensor(stg[:n, 0:512], z_tok[:, ZQ:ZQ + 512], z_tok[:, ZK:ZK + 512], ALU.mult), reads=["z_tok"], writes=["stg"])
    S.op("dve", lambda e: e.tensor_reduce(pnew[:, 0:8], stg[:n, 0:512].rearrange("p (h e) -> p h e", e=64), AX.X, ALU.add),
         reads=["stg"], writes=["pnew"])
    S.op("dve", lambda e: e.tensor_tensor(stg[:n, 0:256], z_tok[:, ZQ + 512:ZQ + 768], z_tok[:, ZK + 512:ZK + 768], ALU.mult),
         reads=["z_tok", "pnew"], writes=["stg"])
    S.op("dve", lambda e: e.tensor_reduce(pnew[:, 8:12], stg[:n, 0:256].rearrange("p (h e) -> p h e", e=64), AX.X, ALU.add),
         reads=["stg"], writes=["pnew"])
    S.op("act", lambda e: e.activation(pnew, pnew, AF.Exp, scale=0.125), reads=["pnew"], writes=["pnew"])
    Dm = sb("Dm", [16, 12, 16], BF16)
    for c in range(12):
        S.op("dve", lambda e, c=c: e.tensor_scalar_mul(Dm[:, c, :], ident_f[:n, :n], pnew[:, c:c + 1]), reads=["pnew", "ident_f"], writes=["Dm"])
    dbg("s_pnew", pnew, [n, 12], ["pnew"])
    pnum, pden = S.psum(), S.psum()
    S.op("pe", lambda e: e.matmul(pnum.ap[:64, 0:64], ones_bf[:, 0:64], zer_bf[:, 0:64], start=True, stop=False),
         reads=["ones_bf", "zer_bf"], writes=[pnum.key])
    for b in range(n):
        for g in range(3):
            kb, kk = kbuf[g % 2], "kbuf%d" % (g % 2)
            vb, vk = vbb[g % 2], "vbb%d" % (g % 2)
            S.dma("sp", kb[:, 0:256], I[caches[g][1]][b].rearrange("(i d) c -> i d c", d=DILS[g])[:, 0, :], writes=[kk])
            S.op("act", lambda e, kb=kb, vb=vb: e.activation(vb[:, 0:256], kb[:, 0:256], AF.Copy), reads=[kk], writes=[vk])

            def pvs(e, b=b, g=g, vb=vb):
                ins = None
                for j in range(4):
                    ins = e.matmul(pnum.ap[:64, j * 16 + b:j * 16 + b + 1], vb[:, j * 64:(j + 1) * 64], pTs[:, b, g * 4 + j:g * 4 + j + 1],
                                   start=False, stop=False)
                return ins
            S.op("pe", pvs, reads=[vk, "pTs"], writes=[pnum.key])

    def pvnew(e):
        ins = None
        for j in range(4):
            for g in range(3):
                ins = e.matmul(pnum.ap[:64, j * 16:(j + 1) * 16], vt_bf[:, g * 256 + j * 64:g * 256 + (j + 1) * 64], Dm[:, g * 4 + j, :],
                               start=False, stop=(g == 2 and j == 3))
        return ins
    S.op("pe", pvnew, reads=["vt_bf", "Dm"], writes=[pnum.key])

    def dens(e):
        ins = None
        for j in range(4):
            for g in range(3):
                e.matmul(pden.ap[:64, j * 16:(j + 1) * 16], ones_bf[:, 0:64], pTs[:, :, g * 4 + j], start=(g == 0), stop=False)
            for g in range(3):
                ins = e.matmul(pden.ap[:64, j * 16:(j + 1) * 16], ones_bf[:n, 0:64], Dm[:, g * 4 + j, :], start=False, stop=(g == 2))
        return ins
    S.op("pe", dens, reads=["pTs", "Dm", "ones_bf"], writes=[pden.key])
    S.op("dve", lambda e: e.tensor_copy(stg[:64, 0:64], pden.ap[:64, 0:64]), reads=[pden.key], writes=["stg"])
    S.op("dve", lambda e: e.tensor_copy(stg[:64, 64:128], pnum.ap[:64, 0:64]), reads=[pnum.key, "stg"], writes=["stg"])
    dbg("s_dn", stg[:64, 0:128], [64, 128], ["stg"])
    S.op("dve", lambda e: e.reciprocal(af[:64, 0, 0:64], pden.ap[:64, 0:64]), reads=[pden.key], writes=[AFK(0)])
    S.op("dve", lambda e: e.tensor_tensor(oT[:, :, :n], pnum.ap[:64, 0:64].rearrange("p (j b) -> p j b", b=16),
                                          af[:64, 0, 0:64].rearrange("p (j b) -> p j b", b=16), ALU.mult),
         reads=[pnum.key, AFK(0)], writes=["oT"])
    gates_to_arena(1, hr, ["hT"], n)
    dense_fm("w_atto", 0, D, 4, 64, lambda k: oT[:, k, :n], ["oT"], n, merge_cb(False, n))
    dbg("s_oT", oT[:, :, :n], [64, 4, n], ["oT"], BF16)
    dbg("s_m2", arena[:, 8:16, :n], [128, 8, n], [AR(8 + i) for i in range(8)], BF16)

    for b in range(n):
        pq = S.psum()
        S.op("pe", lambda e, b=b, pq=pq: e.matmul(pq.ap[:, :512], selb[:, b, :], zq_bf[:, 768:1280], start=True, stop=True),
             reads=["selb", "zq_bf"], writes=[pq.key])
        for c in range(2):
            kb, kk = kbuf[c], "kbuf%d" % c
            S.dma("sp", kb[:, :], I["cmk"][b, c * 128:(c + 1) * 128, :], writes=[kk])
            S.op("dve", lambda e, kb=kb, pq=pq: e.tensor_tensor(prd[:, :], kb[:, :], pq.ap[:, :512], ALU.mult), reads=[kk, pq.key], writes=["prd"])
            S.op("dve", lambda e, b=b, c=c: e.tensor_reduce(scm[:, c, b, :], prd[:, :].rearrange("p (h e) -> p h e", e=128), AX.X, ALU.add),
                 reads=["prd"], writes=["scm"])
    S.op("act", lambda e: e.activation(pmb[:, :, :, :], scm[:, :, :, :], AF.Exp, scale=SC_MEM), reads=["scm"], writes=["pmb"])
    pnm, pdm = S.psum(), S.psum()
    S.op("pe", lambda e: e.matmul(pnm.ap[:, 0:64], ones_bf[:, :], zer_bf[:, 0:64], start=True, stop=False),
         reads=["ones_bf", "zer_bf"], writes=[pnm.key])
    for b in range(n):
        for c in range(2):
            kb, kk = kbuf[c], "kbuf%d" % c
            vb, vk = vbb[c], "vbb%d" % c
            S.dma("sp", kb[:, :], I["cmv"][b, c * 128:(c + 1) * 128, :], writes=[kk])
            S.op("act", lambda e, kb=kb, vb=vb: e.activation(vb[:, :], kb[:, :], AF.Copy), reads=[kk], writes=[vk])

            def pvm(e, b=b, c=c, vb=vb):
                ins = None
                for h in range(4):
                    ins = e.matmul(pnm.ap[:, h * 16 + b:h * 16 + b + 1], vb[:, h * 128:(h + 1) * 128], pmb[:, c, b, h:h + 1],
                                   start=False, stop=(c == 1 and b == n - 1 and h == 3))
                return ins
            S.op("pe", pvm, reads=[vk, "pmb"], writes=[pnm.key])

    def denm(e):
        ins = None
        for h in range(4):
            for c in range(2):
                ins = e.matmul(pdm.ap[:, h * 16:(h + 1) * 16], ones_bf[:, :], pmb[:, c, :, h], start=(c == 0), stop=(c == 1))
        return ins
    S.op("pe", denm, reads=["pmb", "ones_bf"], writes=[pdm.key])
    S.op("dve", lambda e: e.reciprocal(af[:, 0, 0:64], pdm.ap[:, 0:64]), reads=[pdm.key], writes=[AFK(0)])
    S.op("dve", lambda e: e.tensor_tensor(omT[:, :, :n], pnm.ap[:, 0:64].rearrange("p (h b) -> p h b", b=16),
                                          af[:, 0, 0:64].rearrange("p (h b) -> p h b", b=16), ALU.mult),
         reads=[pnm.key, AFK(0)], writes=["omT"])
    gates_to_arena(2, hr, ["hT"], n)
    dense_fm("w_memo", 0, D, 4, 128, lambda k: omT[:, k, :n], ["omT"], n, merge_cb(False, n))
    dbg("s_omT", omT[:, :, :n], [128, 4, n], ["omT"], BF16)
    dbg("s_m3", arena[:, 8:16, :n], [128, 8, n], [AR(8 + i) for i in range(8)], BF16)

    out_proj_tm("w_out", 8, lambda tt, k: arena[:, 8 + k, 0:n], [AR(8 + k) for k in range(8)], n)
    norm_to_hT("norm2_g", 1, n, hT)
    for tcol in range(2):
        crow = PA[:n, 0:2816]
        S.dma("sp", crow, I["sconv"][:, tcol, :], reads=[], writes=["z_tok"])
        for f0 in range(0, NF, 8):
            nf = min(8, NF - f0)
            ps = S.psum()

            def trc2(e, ps=ps, f0=f0, nf=nf, crow=crow):
                ins = None
                for fi in range(nf):
                    ins = e.transpose(ps.ap[:, fi * 16:(fi + 1) * 16], crow[:, (f0 + fi) * 128:(f0 + fi + 1) * 128], ident_f[:n, :n])
                return ins
            S.op("pe", trc2, reads=["z_tok", "ident_f"], writes=[ps.key])
            S.op("dve", lambda e, ps=ps, f0=f0, nf=nf, tcol=tcol: e.tensor_copy(
                cbT[:, f0:f0 + nf, tcol, :], ps.ap[:, 0:nf * 16].rearrange("p (f b) -> p f b", b=16)), reads=[ps.key], writes=["cbT"])
        if tcol == 1:
            S.dma("sp", O["s_conv"][:, 0, :], crow, reads=["z_tok"], writes=[])

    def conv_hook_s(f, pa):
        cc = af[:, 1, 0:n]
        S.op("dve", lambda e, f=f, pa=pa: e.tensor_copy(anT[:, f, :], pa.ap[:, :n]), reads=[pa.key], writes=["anT"])
        S.op("dve", lambda e, f=f: e.tensor_scalar(cc, anT[:, f, :], cwT[:, f, 2:3], cwT[:, f, 3:4], ALU.mult, ALU.add),
             reads=["anT", "cwT"], writes=[AFK(1)])
        S.op("dve", lambda e, f=f: e.scalar_tensor_tensor(cc, cbT[:, f, 1, :], cwT[:, f, 1:2], cc, ALU.mult, ALU.add),
             reads=["cbT", AFK(1), "cwT"], writes=[AFK(1)])
        S.op("dve", lambda e, f=f: e.scalar_tensor_tensor(cc, cbT[:, f, 0, :], cwT[:, f, 0:1], cc, ALU.mult, ALU.add),
             reads=["cbT", AFK(1), "cwT"], writes=[AFK(1)])
    ffn(n, conv_hook_s)
    for f0 in range(0, NF, 4):
        nf = min(4, NF - f0)
        ps = S.psum()

        def tra(e, ps=ps, f0=f0, nf=nf):
            ins = None
            for fi in range(nf):
                ins = e.transpose(ps.ap[:n, fi * 128:(fi + 1) * 128], anT[:, f0 + fi, :], ident_f[:, :])
            return ins
        S.op("pe", tra, reads=["anT", "ident_f"], writes=[ps.key])
        S.op("dve", lambda e, ps=ps, nf=nf: e.tensor_copy(stg[:n, 0:nf * 128], ps.ap[:n, 0:nf * 128]), reads=[ps.key], writes=["stg"])
        S.dma("sp", O["s_conv"][:, 1, f0 * 128:(f0 + nf) * 128], stg[:n, 0:nf * 128], reads=["stg"], writes=[])
    out_proj_tm("w_down", NF, lambda tt, k: arena[:, k, 0:n], [AR(k) for k in range(NF)], n)
    final_norm_out(n, 1, lambda tt: O["y_s"])

    S.emit()
    return nc


def kernel(**inp):
    f = lambda a: np.ascontiguousarray(np.asarray(a, dtype=np.float32))
    consts = host_consts()
    shared = {
        "norm1_g": f(inp["norm1_g"]), "w_in": f(inp["w_in"][0]),
        "a_re": f(inp["ssm_a_re"][0]).reshape(16, 128), "a_im": f(inp["ssm_a_im"][0]).reshape(16, 128),
        "log_dt": f(inp["ssm_log_dt"][0]).reshape(16, 2),
        "b_re": f(inp["ssm_b_re"][0]).reshape(2048, 16), "b_im": f(inp["ssm_b_im"][0]).reshape(2048, 16),
        "c_re": f(inp["ssm_c_re"][0]).reshape(512, 64), "c_im": f(inp["ssm_c_im"][0]).reshape(512, 64),
        "ssm_d": f(inp["ssm_d"][0]).reshape(4, 128), "w_glu": f(inp["w_ssm_glu"][0]), "w_atto": f(inp["w_att_o"][0]),
        "mem_g": f(inp["mem_norm_g"]), "w_memkv": f(inp["w_mem_kv"][0]), "w_memo": f(inp["w_mem_o"][0]),
        "w_out": f(inp["w_out"][0]), "norm2_g": f(inp["norm2_g"]), "w_up": f(inp["w_up"][0]),
        "conv_w": f(inp["ffn_conv_w"][0]), "conv_b": f(inp["ffn_conv_b"]), "w_down": f(inp["w_down"][0]),
        "fin_g": f(inp["final_norm_g"]).reshape(1, D),
    }
    shared.update(consts)
    in_maps = []
    for c in range(8):
        s, b0 = c % 4, c * NS
        m = dict(shared)
        m["xp"] = f(inp["x_prompt"][s])
        m["memp"] = f(inp["mem_prompt"][s])
        m["xs"] = f(inp["x_sample"][b0:b0 + NS, 0])
        m["sre"] = f(inp["state_ssm_re"][0, b0:b0 + NS]).reshape(NS, 2048)
        m["sim"] = f(inp["state_ssm_im"][0, b0:b0 + NS]).reshape(NS, 2048)
        caches = {"ck1": inp["cache_w1_k"], "cv1": inp["cache_w1_v"], "ck2": inp["cache_w2_k"], "cv2": inp["cache_w2_v"],
                  "ck3": inp["cache_w3_k"], "cv3": inp["cache_w3_v"]}
        for nm, arr in caches.items():
            a = np.asarray(arr)[0, b0:b0 + NS]
            m[nm] = f(a).reshape(NS, a.shape[1], 256)
        m["cmk"] = f(inp["cache_mem_k"][0, b0:b0 + NS]).reshape(NS, 256, 512)
        m["cmv"] = f(inp["cache_mem_v"][0, b0:b0 + NS]).reshape(NS, 256, 512)
        m["sconv"] = f(inp["state_ffn_conv"][0, b0:b0 + NS])
        in_maps.append(m)
    nc = build_nc()
    res = run_bass_kernel_spmd(nc, in_maps, core_ids=list(range(8)))
    R = res.results
    cat = lambda name, cores: np.stack([np.asarray(R[c][name], dtype=np.float32) for c in cores], axis=0)
    P = range(4)
    A = range(8)
    sm = lambda name, shape: np.concatenate([np.asarray(R[c][name], np.float32) for c in A], axis=0).reshape(shape)[None]
    outs = (
        cat("y_p", P), sm("y_s", (128, 1, D)),
        cat("p_sre", P).reshape(4, 32, 64)[None], cat("p_sim", P).reshape(4, 32, 64)[None],
        cat("p_k1", P).reshape(4, 128, 4, 64)[None], cat("p_v1", P).reshape(4, 128, 4, 64)[None],
        cat("p_k2", P).reshape(4, 512, 4, 64)[None], cat("p_v2", P).reshape(4, 512, 4, 64)[None],
        cat("p_k3", P).reshape(4, 2048, 4, 64)[None], cat("p_v3", P).reshape(4, 2048, 4, 64)[None],
        cat("p_mk", P).reshape(4, 256, 4, 128)[None], cat("p_mv", P).reshape(4, 256, 4, 128)[None],
        cat("p_conv", P)[None],
        sm("s_sre", (128, 32, 64)), sm("s_sim", (128, 32, 64)),
        sm("s_k1", (128, 1, 4, 64)), sm("s_v1", (128, 1, 4, 64)), sm("s_k2", (128, 1, 4, 64)), sm("s_v2", (128, 1, 4, 64)),
        sm("s_k3", (128, 1, 4, 64)), sm("s_v3", (128, 1, 4, 64)), sm("s_conv", (128, 2, DFF)),
    )
    outs = list(outs)
    outs[1] = outs[1][0]
    return tuple(np.ascontiguousarray(o) for o in outs)
```

```python
import math
from contextlib import ExitStack

import numpy as np
import concourse.bass as bass
import concourse.mybir as mybir
from concourse.bass_utils import run_bass_kernel_spmd

F32 = mybir.dt.float32
BF16 = mybir.dt.bfloat16
AF = mybir.ActivationFunctionType
ALU = mybir.AluOpType
AX = mybir.AxisListType


class PsBank:
    def __init__(self, key, ap):
        self.key = key
        self.ap = ap


class Sched:
    ENG = ("pe", "act", "dve", "pool", "sp")
    NDS = 8

    def __init__(self, nc):
        self.nc = nc
        self.es = ExitStack()
        self.q = {e: [] for e in self.ENG}
        self.cnt = {e: 0 for e in self.ENG}
        self.dcnt = {e: 0 for e in self.ENG}
        self.known = {e: {} for e in self.ENG}
        self.lastw = {}
        self.readers = {}
        self.sem = {e: self.es.enter_context(nc.semaphore("s_" + e)) for e in self.ENG}
        self.dsem = {e: [self.es.enter_context(nc.semaphore("d_%s%d" % (e, i))) for i in range(self.NDS)]
                     for e in ("sp", "pool", "act")}
        self.banks = []
        for i in range(8):
            t = self.es.enter_context(nc.psum_tensor("psb%d" % i, [128, 512], F32))
            self.banks.append(PsBank("psb%d" % i, t))
        self.bank_i = 0
        self.nrot = 8

    def sb(self, name, shape, dtype):
        return self.es.enter_context(self.nc.sbuf_tensor("sb_" + name, shape, dtype))

    def psum(self):
        b = self.banks[self.bank_i % self.nrot]
        self.bank_i += 1
        return b

    def _deps(self, reads, writes):
        deps = []
        for b in reads:
            deps.extend(self.lastw.get(b, ()))
        for b in writes:
            deps.extend(self.lastw.get(b, ()))
            deps.extend(self.readers.get(b, ()))
        return deps

    def _waits(self, eng, deps):
        waits = []
        kn = self.known[eng]
        for ev in deps:
            if ev[0] == "c":
                _, e2, idx = ev
                if e2 == eng and idx < self.cnt[eng] - 1:
                    continue
                if kn.get(("c", e2), 0) >= idx:
                    continue
                kn[("c", e2)] = idx
                waits.append(ev)
            else:
                _, qn, j = ev
                s, c = j % self.NDS, j // self.NDS + 1
                if kn.get(("d", qn, s), 0) >= c:
                    continue
                kn[("d", qn, s)] = c
                waits.append(ev)
        return waits

    def _record(self, ev, reads, writes):
        for b in writes:
            if self.readers.get(b) or b not in self.lastw:
                self.lastw[b] = [ev]
            else:
                self.lastw[b] = self.lastw[b] + [ev]
            self.readers[b] = []
        for b in reads:
            if b not in writes:
                self.readers.setdefault(b, []).append(ev)

    def op(self, eng, fn, reads=(), writes=()):
        deps = self._deps(reads, writes)
        waits = self._waits(eng, deps)
        self.cnt[eng] += 1
        ev = ("c", eng, self.cnt[eng])
        self.q[eng].append((fn, waits, ev))
        self._record(ev, reads, writes)

    def dma(self, qn, out, in_, reads=(), writes=(), **kw):
        deps = self._deps(reads, writes)
        j = self.dcnt[qn]
        if j >= self.NDS:
            deps.append(("d", qn, j - self.NDS))
        waits = self._waits(qn, deps)
        self.dcnt[qn] += 1
        ev = ("d", qn, j)
        self.q[qn].append((lambda e: e.dma_start(out=out, in_=in_, **kw), waits, ev))
        self._record(ev, reads, writes)

    def emit(self):
        nc = self.nc
        q = self.q
        needed = set()
        for name in self.ENG:
            for fn, waits, ev in q[name]:
                for w in waits:
                    if w[0] == "c":
                        needed.add(w)
        for e in ("pe", "act", "dve", "pool"):
            if self.cnt[e] > 0:
                needed.add(("c", e, self.cnt[e]))
        import os
        if os.environ.get("DENSE"):
            for e in ("pe", "act", "dve", "pool"):
                for idx in range(1, self.cnt[e] + 1):
                    needed.add(("c", e, idx))
        cum = {}
        for e in ("pe", "act", "dve", "pool"):
            c, arr = 0, [0] * (self.cnt[e] + 1)
            for idx in range(1, self.cnt[e] + 1):
                if ("c", e, idx) in needed:
                    c += 1
                arr[idx] = c
            cum[e] = arr

        def semval(w):
            if w[0] == "c":
                return self.sem[w[1]], cum[w[1]][w[2]]
            _, qn, j = w
            return self.dsem[qn][j % self.NDS], 16 * (j // self.NDS + 1)

        fin = []
        for qn in ("sp", "pool", "act"):
            for s in range(self.NDS):
                n = (self.dcnt[qn] - s + self.NDS - 1) // self.NDS
                if n > 0:
                    fin.append((self.dsem[qn][s], 16 * n))
        for e in ("pe", "act", "dve", "pool"):
            if self.cnt[e] > 0:
                fin.append((self.sem[e], cum[e][self.cnt[e]]))
        self.n_inc = {e: cum[e][-1] for e in cum}

        def run(e, name, final=()):
            for fn, waits, ev in q[name]:
                for w in waits:
                    s, v = semval(w)
                    e.wait_ge(s, v)
                ins = fn(e)
                if ev[0] == "d":
                    ins.then_inc(self.dsem[ev[1]][ev[2] % self.NDS], 16)
                elif ev in needed:
                    ins.then_inc(self.sem[ev[1]], 1)
            for s, v in final:
                e.wait_ge(s, v)

        with nc.Block() as block:
            @block.sync
            def _(e):
                run(e, "sp", fin)

            @block.tensor
            def _(e):
                run(e, "pe")

            @block.scalar
            def _(e):
                run(e, "act")

            @block.vector
            def _(e):
                run(e, "dve")

            @block.gpsimd
            def _(e):
                run(e, "pool")
        self.es.close()


D = 1024
SEQ = 4096
TOK = 512
NBLK = SEQ // TOK
NS = 16
DFF = 2816
NF = DFF // 128
INW = 6400
OFF_U, OFF_Q, OFF_K, OFF_V, OFF_QM, OFF_G = 0, 512, 1280, 2048, 2816, 3328
DILS = (1, 4, 16)
EPS = 1e-6
NEG = -30000.0
SLOPES = [2.0 ** (-8.0 * h / 12.0) for h in range(1, 13)]
TWO_PI = 2.0 * math.pi


def host_consts():
    import ml_dtypes
    c = {}
    c["ident_bf"] = np.eye(128, dtype=np.float32).astype(ml_dtypes.bfloat16)
    c["ident_f"] = np.eye(128, dtype=np.float32)
    a = np.arange(128)[:, None].astype(np.float64)
    b3_ = np.arange(32)[None, :].astype(np.float64)
    b = np.arange(128)[None, :].astype(np.float64)
    t12 = np.zeros((128, 2, 2, 4, 128), np.float32)
    for g in range(2):
        for h in range(4):
            sl = SLOPES[g * 4 + h] * DILS[g]
            dA = 128 + b - a
            t12[:, g, 0, h, :] = np.where(dA <= 128, -sl * dA, NEG)
            dB = b - a
            t12[:, g, 1, h, :] = np.where(dB >= 0, -sl * dB, NEG)
    c["tab12"] = t12.reshape(128, 16 * 128)
    bA = 128 + b - a
    bB = b - a
    c["base12"] = np.stack([np.where(bA <= 128, -bA, -1e9), np.where(bB >= 0, -bB, -1e9)], axis=1).astype(np.float32).reshape(128, 256)
    t3b = np.zeros((128, 4, 2, 32), np.float32)
    for v in range(4):
        dA = 128 + 32 * v + b3_ - a
        t3b[:, v, 0, :] = np.where(dA <= 128, -dA, -1e9)
        dB = 32 * v + b3_ - a
        t3b[:, v, 1, :] = np.where(dB >= 0, -dB, -1e9)
    c["base3"] = t3b.reshape(128, 256)
    b3 = np.arange(32)[None, :].astype(np.float64)
    t3 = np.zeros((128, 4, 2, 4, 32), np.float32)
    for v in range(4):
        for h in range(4):
            sl = SLOPES[8 + h] * 16
            dA = 128 + 32 * v + b3 - a
            t3[:, v, 0, h, :] = np.where(dA <= 128, -sl * dA, NEG)
            dB = 32 * v + b3 - a
            t3[:, v, 1, h, :] = np.where(dB >= 0, -sl * dB, NEG)
    c["tab3"] = t3.reshape(128, 32 * 32)
    ts = np.zeros((128, 12), np.float32)
    for g in range(3):
        for h in range(4):
            ts[:, g * 4 + h] = -SLOPES[g * 4 + h] * DILS[g] * (128 - np.arange(128))
    c["tabs_s"] = ts
    sel = np.zeros((16, 16, 128), np.float32)
    for bb in range(16):
        sel[bb, bb, :] = 1.0
    c["sel"] = sel.reshape(16, 16 * 128)
    c["twopi"] = np.full((128, 1), TWO_PI, np.float32)
    c["iota"] = np.tile(np.arange(514, dtype=np.float32)[None, :], (128, 1))
    return c


CONST_SHAPES = {"ident_bf": ([128, 128], BF16), "ident_f": ([128, 128], F32), "tab12": ([128, 2048], F32),
                "tab3": ([128, 1024], F32), "base12": ([128, 256], F32), "base3": ([128, 256], F32), "tabs_s": ([128, 12], F32), "sel": ([16, 2048], F32),
                "twopi": ([128, 1], F32), "iota": ([128, 514], F32)}

IN_SHAPES = {
    "xp": [SEQ, D], "memp": [256, D], "xs": [NS, D], "sre": [NS, 2048], "sim": [NS, 2048],
    "ck1": [NS, 128, 256], "cv1": [NS, 128, 256], "ck2": [NS, 512, 256], "cv2": [NS, 512, 256],
    "ck3": [NS, 2048, 256], "cv3": [NS, 2048, 256], "cmk": [NS, 256, 512], "cmv": [NS, 256, 512],
    "sconv": [NS, 2, DFF],
    "norm1_g": [1, D], "w_in": [D, INW], "a_re": [16, 128], "a_im": [16, 128], "log_dt": [16, 2],
    "b_re": [2048, 16], "b_im": [2048, 16], "c_re": [512, 64], "c_im": [512, 64], "ssm_d": [4, 128],
    "w_glu": [512, 2048], "w_atto": [256, D], "mem_g": [1, D], "w_memkv": [D, D], "w_memo": [512, D],
    "w_out": [D, D], "norm2_g": [1, D], "w_up": [D, 2 * DFF], "conv_w": [3, DFF], "conv_b": [1, DFF],
    "w_down": [DFF, D], "fin_g": [1, D],
}
OUT_SHAPES = {
    "y_p": [SEQ, D], "y_s": [NS, D], "p_sre": [16, 128], "p_sim": [16, 128],
    "p_k1": [128, 256], "p_v1": [128, 256], "p_k2": [512, 256], "p_v2": [512, 256],
    "p_k3": [2048, 256], "p_v3": [2048, 256], "p_mk": [256, 512], "p_mv": [256, 512], "p_conv": [2, DFF],
    "s_sre": [NS, 2048], "s_sim": [NS, 2048], "s_k1": [NS, 256], "s_v1": [NS, 256], "s_k2": [NS, 256],
    "s_v2": [NS, 256], "s_k3": [NS, 256], "s_v3": [NS, 256], "s_conv": [NS, 2, DFF],
}


def build_nc(stage=99, nblk=NBLK, debug=False):
    nc = bass.Bass("TRN2", target_bir_lowering=False)
    I = {k: nc.dram_tensor(k, s, F32, kind="ExternalInput").ap() for k, s in IN_SHAPES.items()}
    C = {k: nc.dram_tensor(k, s, dt, kind="ExternalInput").ap() for k, (s, dt) in CONST_SHAPES.items()}
    O = {k: nc.dram_tensor(k, s, F32, kind="ExternalOutput").ap() for k, s in OUT_SHAPES.items()}
    WB = {k: nc.dram_tensor(k + "_bf", IN_SHAPES[k], BF16, kind="Internal").ap()
          for k in ("w_in", "w_glu", "w_atto", "w_memkv", "w_memo", "w_out", "w_up", "w_down")}
    tabs_d = nc.dram_tensor("tabs_d", [16, 128, 1024], F32, kind="Internal").ap()
    S = Sched(nc)
    sb = S.sb
    dbg_n = {"i": 0}

    def dbg(name, ap, shape, keys, dt=F32):
        if not debug:
            return
        t = nc.dram_tensor("dbg_" + name, shape, dt, kind="ExternalOutput").ap()
        S.dma("sp", t, ap, reads=keys, writes=[])

    ident_bf = sb("ident_bf", [128, 128], BF16)
    ident_f = sb("ident_f", [128, 128], F32)
    ones_bf = sb("ones_bf", [128, 128], BF16)
    twopi = sb("twopi", [128, 1], F32)
    S.dma("sp", ident_bf[:], C["ident_bf"], writes=["ident_bf"])
    S.dma("sp", ident_f[:], C["ident_f"], writes=["ident_f"])
    S.dma("sp", twopi[:], C["twopi"], writes=["twopi"])
    S.op("dve", lambda e: e.memset(ones_bf[:], 1.0), writes=["ones_bf"])
    zer_bf = sb("zer_bf", [128, 64], BF16)
    S.op("dve", lambda e: e.memset(zer_bf[:], 0.0), writes=["zer_bf"])

    for k in ("w_in", "w_memkv", "w_glu", "w_atto", "w_memo", "w_out", "w_up", "w_down"):
        rows = IN_SHAPES[k][0]
        step = 256
        for r0 in range(0, rows, step):
            r1 = min(rows, r0 + step)
            S.dma("pool", WB[k][r0:r1, :], I[k][r0:r1, :], writes=[k + "_bf"])

    wbuf = [sb("wbuf%d" % i, [128, 8, 512], BF16) for i in range(2)]
    wstate = {"i": 0}
    arena = sb("arena", [128, 22, TOK], BF16)
    af = sb("af", [128, 4, TOK + 2], F32)
    xres = sb("xres", [128, 4, D], F32)
    xnb = sb("xnb", [128, D], BF16)
    gbc = sb("gbc", [128, D], F32)
    ytmp = sb("ytmp", [128, D], F32)
    hT = sb("hT", [128, 8, TOK], BF16)
    ubf = sb("ubf", [128, 4, TOK], BF16)
    qmT = sb("qmT", [128, 4, TOK], BF16)
    zT = sb("zT", [128, 4, TOK], BF16)
    kT12 = sb("kT12", [128, 2, 2, 2, TOK], BF16)
    kT3 = sb("kT3", [128, 2, SEQ], BF16)
    V1 = sb("V1", [128, 2, 4, 256], BF16)
    V2 = sb("V2", [128, 2, 4, 256], BF16)
    V3 = sb("V3", [128, 16, 2, 256], BF16)
    vst = [sb("vst%d" % i, [128, 256], BF16) for i in range(2)]
    stg = sb("stg", [128, 512], F32)
    pT = [sb("pT%d" % i, [128, 512], BF16) for i in range(2)]
    oT = sb("oT", [64, 4, TOK], BF16)
    omT = sb("omT", [128, 4, TOK], BF16)
    pTm = sb("pTm", [128, 2, TOK], BF16)
    mkT = sb("mkT", [128, 4, 256], BF16)
    mv = sb("mv", [128, 2, 512], BF16)
    base12 = sb("base12", [128, 2, 128], F32)
    base3 = sb("base3", [128, 8, 32], F32)
    small = sb("small", [128, 64], F32)
    tabt = [sb("tabt%d" % i, [128, 2, 512], F32) for i in range(2)]
    sbf = [sb("sbf%d" % i, [128, TOK], BF16) for i in range(2)]
    bbT = sb("bbT", [128, 16, 2, 128], BF16)
    cT = sb("cT", [128, 16, 2, 128], BF16)
    dD = sb("dD", [128, 4, 128], BF16)
    mag = sb("mag", [128, 16], F32)
    e512 = sb("e512", [128, 2, 16], F32)
    e511 = sb("e511", [128, 2, 16], F32)
    ab = sb("ab", [128, 2, 16], F32)
    rlast = sb("rlast", [128, 2, 16], F32)
    rinit = sb("rinit", [128, 2, 16], F32)
    carry = sb("carry", [128, NF, 2], F32)
    cwT = sb("cwT", [128, NF, 4], F32)
    S.dma("sp", base12[:], C["base12"].rearrange("p (a b) -> p a b", b=128), writes=["base12"])
    S.dma("sp", base3[:], C["base3"].rearrange("p (a b) -> p a b", b=32), writes=["base3"])

    P32 = kT3[:, :, :].bitcast(F32).rearrange("p a b -> p (a b)")
    pro_state = {"off": 0, "keys": []}

    def pro(name, shape):
        n = int(np.prod(shape[1:]))
        o = pro_state["off"]
        pro_state["off"] = o + n
        assert pro_state["off"] <= 4096, name
        pro_state["keys"].append(name)
        v = P32[:shape[0], o:o + n]
        if len(shape) == 3:
            v = v.rearrange("p (a b) -> p a b", b=shape[2])
        elif len(shape) == 4:
            v = v.rearrange("p (a b c) -> p a b c", b=shape[2], c=shape[3])
        return v

    AR = lambda j: ("ar", j)
    AFK = lambda j: ("af", j)
    evac_rr = {"i": 0}

    def evac(out_ap, ps, rows, n, reads, writes, scale=None, func=None):
        evac_rr["i"] += 1
        if func is not None or scale is not None or evac_rr["i"] % 2 == 0:
            f = func if func is not None else AF.Copy
            if scale is None:
                S.op("act", lambda e: e.activation(out_ap, ps.ap[:rows, :n], f), reads=[ps.key] + reads, writes=writes)
            else:
                S.op("act", lambda e: e.activation(out_ap, ps.ap[:rows, :n], f, scale=scale),
                     reads=[ps.key] + reads, writes=writes)
        else:
            S.op("dve", lambda e: e.tensor_copy(out_ap, ps.ap[:rows, :n]), reads=[ps.key] + reads, writes=writes)

    def wload(pieces, kc, krows):
        i = wstate["i"] % 2
        wstate["i"] += 1
        t, key = wbuf[i], "wbuf%d" % i
        for (W, wk, r0, c0, n, dc) in pieces:
            src = W[r0:r0 + kc * krows, c0:c0 + n].rearrange("(k p) c -> p k c", p=krows)
            S.dma("sp", t[:krows, :kc, dc:dc + n], src, reads=[wk], writes=[key])
        return t, key

    def mm(ps, prow, n, lhs_fn, rhs_fn, kc, reads, start=True, stop=True):
        def fn(e):
            ins = None
            for k in range(kc):
                ins = e.matmul(ps.ap[:prow, :n], lhs_fn(k), rhs_fn(k), start=(start and k == 0),
                               stop=(stop and k == kc - 1))
            return ins
        S.op("pe", fn, reads=reads, writes=[ps.key])

    def load_gbc(name):
        S.dma("sp", gbc[:], I[name].to_broadcast([128, D]), writes=["gbc"], allow_slow_non_contiguous=True)

    def rmsnorm_rows(x_ap, rows, xkeys, out_ap, outkeys):
        S.op("act", lambda e: e.activation(xnb[:rows, :], x_ap, AF.Square, accum_out=small[:rows, 0:1]),
             reads=xkeys, writes=["xnb", "small0"])
        S.op("dve", lambda e: e.tensor_scalar(small[:rows, 1:2], small[:rows, 0:1], 1.0 / D, EPS, ALU.mult, ALU.add),
             reads=["small0"], writes=["small1"])
        S.op("act", lambda e: e.activation(small[:rows, 3:4], small[:rows, 1:2], AF.Sqrt), reads=["small1"], writes=["small3"])
        S.op("dve", lambda e: e.reciprocal(small[:rows, 2:3], small[:rows, 3:4]), reads=["small3"], writes=["small2"])
        S.op("dve", lambda e: e.scalar_tensor_tensor(out_ap, x_ap, small[:rows, 2:3], gbc[:rows, :], ALU.mult, ALU.mult),
             reads=xkeys + ["small2", "gbc"], writes=outkeys)

    def transpose_to(dst_fn, src_bf, rows, nk, skeys, dkeys):
        ps = S.psum()
        pv = ps.ap[:, :].bitcast(BF16)

        def fn(e):
            ins = None
            for k in range(nk):
                ins = e.transpose(pv[:, k * 128:k * 128 + rows], src_bf[:rows, k * 128:(k + 1) * 128],
                                  ident_bf[:rows, :rows])
            return ins
        S.op("pe", fn, reads=skeys + ["ident_bf"], writes=[ps.key])
        evac_rr["t"] = evac_rr.get("t", 0) + 1
        use_dve = evac_rr["t"] % 2
        for k in range(nk):
            S.op("dve" if use_dve else "act",
                 (lambda e, k=k: e.tensor_copy(dst_fn(k), pv[:, k * 128:k * 128 + rows])) if use_dve else
                 (lambda e, k=k: e.activation(dst_fn(k), pv[:, k * 128:k * 128 + rows], AF.Copy)),
                 reads=[ps.key], writes=dkeys)

    ld16 = pro("ld16", [16, 4, 128])
    S.dma("sp", ld16[:, 0, :], I["a_re"], writes=["ld16"])
    S.dma("sp", ld16[:, 1, :], I["a_im"], writes=["ld16"])
    ldt = sb("ldt", [16, 2], F32)
    S.dma("sp", ldt[:], I["log_dt"], writes=["ldt"])
    S.op("dve", lambda e: e.tensor_copy(ld16[:, 2, :].rearrange("g (a p) -> g a p", p=64),
                                        ldt[:, :].unsqueeze(2).to_broadcast([16, 2, 64])), reads=["ldt"], writes=["ld16"])
    cst = sb("cst", [128, 12, 16], F32)
    ps = S.psum()

    def _tr3(e):
        ins = None
        for j in range(3):
            ins = e.transpose(ps.ap[:, j * 16:(j + 1) * 16], ld16[:, j, :], ident_f[:16, :16])
        return ins
    S.op("pe", _tr3, reads=["ld16", "ident_f"], writes=[ps.key])
    S.op("dve", lambda e: e.tensor_copy(cst[:, 0:3, :], ps.ap[:, 0:48].rearrange("p (a b) -> p a b", b=16)),
         reads=[ps.key], writes=["cst"])
    S.op("act", lambda e: e.activation(cst[:, 2, :], cst[:, 2, :], AF.Exp), reads=["cst"], writes=["cst"])
    S.op("dve", lambda e: e.tensor_mul(cst[:, 3, :], cst[:, 1, :], cst[:, 2, :]), reads=["cst"], writes=["cst"])
    S.op("dve", lambda e: e.tensor_mul(cst[:, 5, :], cst[:, 0, :], cst[:, 2, :]), reads=["cst"], writes=["cst"])
    S.op("dve", lambda e: e.tensor_scalar(mag[:], cst[:, 5, :], 1.0 / 12.0, 1.0, ALU.mult, ALU.add), reads=["cst"], writes=["mag"])
    for n_ in range(11, 0, -1):
        S.op("dve", lambda e: e.tensor_mul(mag[:], mag[:], cst[:, 5, :]), reads=["cst", "mag"], writes=["mag"])
        S.op("dve", lambda e, n_=n_: e.tensor_scalar(mag[:], mag[:], 1.0 / n_, 1.0, ALU.mult, ALU.add), reads=["mag"], writes=["mag"])
    MAGIC = 12582912.0

    def reduce_pi(dst, src_ap, tmp, rk, wk, shift=0.0):
        S.op("dve", lambda e: e.tensor_scalar(tmp, src_ap, shift, 1.0 / TWO_PI, ALU.add, ALU.mult), reads=rk, writes=wk)
        S.op("dve", lambda e: e.tensor_scalar_add(tmp, tmp, MAGIC), reads=wk, writes=wk)
        S.op("dve", lambda e: e.tensor_scalar(tmp, tmp, MAGIC, -TWO_PI, ALU.subtract, ALU.mult), reads=wk, writes=wk)
        S.op("dve", lambda e: e.scalar_tensor_tensor(dst, src_ap, shift, tmp, ALU.add, ALU.add), reads=rk + wk, writes=wk)
        S.op("dve", lambda e: e.tensor_scalar(dst, dst, 3.141592, -3.141592, ALU.min, ALU.max), reads=wk, writes=wk)

    reduce_pi(cst[:, 6, :], cst[:, 3, :], cst[:, 11, :], ["cst"], ["cst"])
    ang = pro("ang", [128, 514])
    iota = pro("iota", [128, 514])
    S.dma("sp", iota[:], C["iota"], writes=["iota"])
    for gp in range(16):
        S.op("dve", lambda e, gp=gp: e.tensor_scalar_mul(ang[:, 0:513], iota[:, 0:513], cst[:, 6, gp:gp + 1]),
             reads=["cst", "iota"], writes=["ang"])
        reduce_pi(af[:, 1, 0:513], ang[:, 0:513], af[:, 0, 0:513], ["ang"], [AFK(0), AFK(1)], shift=math.pi / 2)
        reduce_pi(af[:, 2, 0:513], ang[:, 0:513], af[:, 3, 0:513], ["ang"], [AFK(2), AFK(3)])
        S.op("act", lambda e: e.activation(af[:, 0, 0:513], af[:, 1, 0:513], AF.Sin), reads=[AFK(1)], writes=[AFK(0)])
        S.op("act", lambda e: e.activation(af[:, 3, 0:513], af[:, 2, 0:513], AF.Sin), reads=[AFK(2)], writes=[AFK(3)])
        S.dma("sp", tabs_d[gp, :, 0:512], af[:, 0, 0:512], reads=[AFK(0)], writes=["tabs_d"])
        S.dma("sp", tabs_d[gp, :, 512:1024], af[:, 3, 0:512], reads=[AFK(3)], writes=["tabs_d"])
        for (dst, col) in ((e512, 512), (e511, 511)):
            S.op("dve", lambda e, gp=gp, dst=dst, col=col: e.tensor_copy(dst[:, 0, gp:gp + 1], af[:, 0, col:col + 1]),
                 reads=[AFK(0)], writes=["ecst"])
            S.op("dve", lambda e, gp=gp, dst=dst, col=col: e.tensor_copy(dst[:, 1, gp:gp + 1], af[:, 3, col:col + 1]),
                 reads=[AFK(3)], writes=["ecst"])
        S.op("dve", lambda e, gp=gp: e.tensor_copy(small[:, 16 + gp:17 + gp], af[:, 0, 1:2]), reads=[AFK(0)], writes=["small16"])
        S.op("dve", lambda e, gp=gp: e.tensor_copy(small[:, 32 + gp:33 + gp], af[:, 3, 1:2]), reads=[AFK(3)], writes=["small16"])
    S.op("dve", lambda e: e.tensor_mul(ab[:, 0, :], mag[:], small[:, 16:32]), reads=["mag", "small16"], writes=["ab"])
    S.op("dve", lambda e: e.tensor_mul(ab[:, 1, :], mag[:], small[:, 32:48]), reads=["mag", "small16"], writes=["ab"])
    c_ = lambda j: cst[:, j, :]
    S.op("dve", lambda e: e.tensor_scalar_add(c_(5), ab[:, 0, :], -1.0), reads=["ab"], writes=["cst"])
    S.op("dve", lambda e: e.tensor_mul(c_(7), c_(0), c_(0)), reads=["cst"], writes=["cst"])
    S.op("dve", lambda e: e.tensor_mul(c_(8), c_(1), c_(1)), reads=["cst"], writes=["cst"])
    S.op("dve", lambda e: e.tensor_add(c_(4), c_(7), c_(8)), reads=["cst"], writes=["cst"])
    S.op("dve", lambda e: e.reciprocal(c_(4), c_(4)), reads=["cst"], writes=["cst"])
    S.op("dve", lambda e: e.tensor_mul(c_(7), c_(5), c_(0)), reads=["cst"], writes=["cst"])
    S.op("dve", lambda e: e.tensor_mul(c_(8), ab[:, 1, :], c_(1)), reads=["cst", "ab"], writes=["cst"])
    S.op("dve", lambda e: e.tensor_add(c_(7), c_(7), c_(8)), reads=["cst"], writes=["cst"])
    S.op("dve", lambda e: e.tensor_mul(c_(9), c_(7), c_(4)), reads=["cst"], writes=["cst"])
    S.op("dve", lambda e: e.tensor_mul(c_(7), ab[:, 1, :], c_(0)), reads=["cst", "ab"], writes=["cst"])
    S.op("dve", lambda e: e.tensor_mul(c_(8), c_(5), c_(1)), reads=["cst"], writes=["cst"])
    S.op("dve", lambda e: e.tensor_sub(c_(7), c_(7), c_(8)), reads=["cst"], writes=["cst"])
    S.op("dve", lambda e: e.tensor_mul(c_(10), c_(7), c_(4)), reads=["cst"], writes=["cst"])
    braw = pro("braw", [128, 2, 16, 16])
    S.dma("sp", braw[:, 0], I["b_re"].rearrange("(gp q) c -> q gp c", q=128), writes=["braw"])
    S.dma("sp", braw[:, 1], I["b_im"].rearrange("(gp q) c -> q gp c", q=128), writes=["braw"])
    bbf = pro("bbf", [128, 2, 16, 16])
    qre_b = cst[:, 9, :].unsqueeze(2).to_broadcast([128, 16, 16])
    qim_b = cst[:, 10, :].unsqueeze(2).to_broadcast([128, 16, 16])
    tmpb = pro("tmpb", [128, 16, 16])
    S.op("dve", lambda e: e.tensor_mul(bbf[:, 0], braw[:, 0], qre_b), reads=["braw", "cst"], writes=["bbf"])
    S.op("dve", lambda e: e.tensor_mul(tmpb[:], braw[:, 1], qim_b), reads=["braw", "cst"], writes=["tmpb"])
    S.op("dve", lambda e: e.tensor_sub(bbf[:, 0], bbf[:, 0], tmpb[:]), reads=["tmpb", "bbf"], writes=["bbf"])
    S.op("dve", lambda e: e.tensor_mul(bbf[:, 1], braw[:, 1], qre_b), reads=["braw", "cst"], writes=["bbf"])
    S.op("dve", lambda e: e.tensor_mul(tmpb[:], braw[:, 0], qim_b), reads=["braw", "cst", "bbf"], writes=["tmpb"])
    S.op("dve", lambda e: e.tensor_add(bbf[:, 1], bbf[:, 1], tmpb[:]), reads=["tmpb", "bbf"], writes=["bbf"])
    S.op("dve", lambda e: e.memset(bbT[:], 0.0), writes=["bbT"])
    S.op("dve", lambda e: e.memset(cT[:], 0.0), writes=["cT"])
    S.op("dve", lambda e: e.memset(dD[:], 0.0), writes=["dD"])
    bbz = pro("bbz", [128, 32])
    for gp in range(16):
        for ri in range(2):
            S.op("dve", lambda e: e.memset(bbz[:], 0.0), writes=["bbz"])
            S.op("dve", lambda e, gp=gp, ri=ri: e.tensor_copy(bbz[0:64, 0:16], bbf[0:64, ri, gp, :]), reads=["bbf"], writes=["bbz"])
            S.op("dve", lambda e, gp=gp, ri=ri: e.tensor_copy(bbz[64:128, 16:32], bbf[64:128, ri, gp, :]), reads=["bbf"], writes=["bbz"])
            ps = S.psum()
            S.op("pe", lambda e, ps=ps: e.transpose(ps.ap[:32, :128], bbz[:, :], ident_f[:, :]), reads=["bbz", "ident_f"], writes=[ps.key])
            r0 = (gp % 4) * 32
            S.op("dve", lambda e, ps=ps: e.tensor_copy(vst[0][:32, :128], ps.ap[:32, :128]), reads=[ps.key], writes=["vst0"])
            S.dma("sp", bbT[r0:r0 + 32, gp, ri, :], vst[0][:32, :128], reads=["vst0"], writes=["bbT"])
    craw = pro("craw", [128, 4, 2, 64])
    S.dma("sp", craw[:, :, 0, :], I["c_re"].rearrange("(t q) p -> q t p", q=128), writes=["craw"])
    S.dma("sp", craw[:, :, 1, :], I["c_im"].rearrange("(t q) p -> q t p", q=128), writes=["craw"])
    for gp in range(16):
        t, r0 = gp // 4, (gp % 4) * 32
        for ri in range(2):
            ps = S.psum()
            S.op("pe", lambda e, ps=ps, t=t, ri=ri: e.transpose(ps.ap[:64, :128], craw[:, t, ri, :], ident_f[:, :]),
                 reads=["craw", "ident_f"], writes=[ps.key])
            sc = 1.0 if ri == 0 else -1.0
            S.op("act", lambda e, ps=ps, gp=gp, ri=ri, r0=r0, sc=sc: e.activation(cT[0:64, gp, ri, r0:r0 + 16], ps.ap[0:64, r0:r0 + 16], AF.Copy, scale=sc),
                 reads=[ps.key], writes=["cT"])
            S.op("act", lambda e, ps=ps, r0=r0, sc=sc: e.activation(stg[0:64, 0:16], ps.ap[0:64, r0 + 16:r0 + 32], AF.Copy, scale=sc),
                 reads=[ps.key], writes=["stg"])
            S.op("dve", lambda e: e.tensor_copy(vst[0][0:64, 0:16], stg[0:64, 0:16]), reads=["stg"], writes=["vst0"])
            S.dma("sp", cT[64:128, gp, ri, r0 + 16:r0 + 32], vst[0][0:64, 0:16], reads=["vst0"], writes=["cT"])
    dcol = sb("dcol", [128, 4], F32)
    drow = pro("drow", [4, 128])
    S.dma("sp", drow[:], I["ssm_d"], writes=["drow"])
    ps = S.psum()
    S.op("pe", lambda e, ps=ps: e.transpose(ps.ap[:, 0:4], drow[:, :], ident_f[:4, :4]), reads=["drow", "ident_f"], writes=[ps.key])
    S.op("dve", lambda e, ps=ps: e.tensor_copy(dcol[:], ps.ap[:, 0:4]), reads=[ps.key], writes=["dcol"])
    for t in range(4):
        S.op("dve", lambda e, t=t: e.tensor_scalar_mul(dD[:, t, :], ident_f[:, :], dcol[:, t:t + 1]), reads=["ident_f", "dcol"], writes=["dD"])
    cwrow = pro("cwrow", [NF, 4, 128])
    for j in range(3):
        S.dma("sp", cwrow[:, j, :], I["conv_w"][j:j + 1, :].rearrange("j (f q) -> (j f) q", q=128), writes=["cwrow"])
    S.dma("sp", cwrow[:, 3, :], I["conv_b"].rearrange("j (f q) -> (j f) q", q=128), writes=["cwrow"])
    ps = S.psum()

    def _trc(e, ps=ps):
        ins = None
        for j in range(4):
            ins = e.transpose(ps.ap[:, j * NF:(j + 1) * NF], cwrow[:, j, :], ident_f[:NF, :NF])
        return ins
    S.op("pe", _trc, reads=["cwrow", "ident_f"], writes=[ps.key])
    S.op("dve", lambda e, ps=ps: e.tensor_copy(cwT[:, :, :], ps.ap[:, 0:4 * NF].rearrange("p (j f) -> p f j", f=NF)),
         reads=[ps.key], writes=["cwT"])
    S.op("dve", lambda e: e.memset(carry[:], 0.0), writes=["carry"])
    S.op("dve", lambda e: e.memset(rinit[:], 0.0), writes=["rinit"])

    S.op("dve", lambda e: e.memset(small[:, 63:64], 0.0), writes=pro_state["keys"] + ["kT3"])

    load_gbc("mem_g")
    for mt in range(2):
        S.dma("sp", xres[:, mt, :], I["memp"][mt * 128:(mt + 1) * 128, :], writes=[("x", mt)])
        rmsnorm_rows(xres[:, mt, :], 128, [("x", mt)], xnb[:, :], ["xnb"])
        transpose_to(lambda k, mt=mt: hT[:, k, mt * 128:(mt + 1) * 128], xnb, 128, 8, ["xnb"], ["hT"])
    for cg, oname in ((0, "p_mk"), (1, "p_mv")):
        wt, wk = wload([(WB["w_memkv"], "w_memkv_bf", 0, cg * 512, 512, 0)], 8, 128)
        for mt in range(2):
            ps = S.psum()
            mm(ps, 128, 512, lambda k, mt=mt: hT[:, k, mt * 128:(mt + 1) * 128], lambda k, wt=wt: wt[:, k, :], 8, [wk, "hT"])
            evac(stg[:, :], ps, 128, 512, [], ["stg"])
            S.dma("sp", O[oname][mt * 128:(mt + 1) * 128, :], stg[:, :], reads=["stg"], writes=[])
            if cg == 1:
                S.op("dve", lambda e, mt=mt: e.tensor_copy(mv[:, mt, :], stg[:, :]), reads=["stg"], writes=["mv"])
        if cg == 0:
            for h in range(4):
                ps = S.psum()
                mm(ps, 128, 256, lambda k, wt=wt, h=h: wt[:, k, h * 128:(h + 1) * 128], lambda k: hT[:, k, 0:256], 8, [wk, "hT"])
                evac(mkT[:, h, :], ps, 128, 256, [], ["mkT"])

    def window_outputs(blk):
        T0 = blk * TOK
        for g, win in enumerate((128, 512, 2048)):
            tts = [tt for tt in range(4) if T0 + tt * 128 >= SEQ - win]
            if not tts:
                continue
            wt, wk = wload([(WB["w_in"], "w_in_bf", 0, OFF_K + g * 256, 256, 0),
                            (WB["w_in"], "w_in_bf", 0, OFF_V + g * 256, 256, 256)], 8, 128)
            for tt in tts:
                ps = S.psum()
                mm(ps, 128, 512, lambda k, tt=tt: hT[:, k, tt * 128:(tt + 1) * 128], lambda k, wt=wt: wt[:, k, :], 8, [wk, "hT"])
                evac(stg[:, :], ps, 128, 512, [], ["stg"])
                r0 = T0 + tt * 128 - (SEQ - win)
                S.dma("sp", O["p_k%d" % (g + 1)][r0:r0 + 128, :], stg[:, 0:256], reads=["stg"], writes=[])
                S.dma("sp", O["p_v%d" % (g + 1)][r0:r0 + 128, :], stg[:, 256:512], reads=["stg"], writes=[])

    YB = [S.banks[6], S.banks[7]]
    S.nrot = 6
    SC_MEM = 128.0 ** -0.5

    def dense_fm(Wk, col0, ncols, kc, krows, rhs_fn, rkeys, n, cb):
        for c0 in range(0, ncols, 512):
            w = min(512, ncols - c0)
            wt, wk = wload([(WB[Wk], Wk + "_bf", 0, col0 + c0, w, 0)], kc, krows)
            for mi in range(w // 128):
                ps = S.psum()
                mm(ps, 128, n, lambda k, wt=wt, mi=mi: wt[:krows, k, mi * 128:(mi + 1) * 128], rhs_fn, kc, [wk] + rkeys)
                cb((c0 // 128) + mi, ps)

    def gates_to_arena(b, rhs_fn, rkeys, n):
        def cb(m, ps):
            S.op("act", lambda e: e.activation(arena[:, m, :n], ps.ap[:, :n], AF.Sigmoid), reads=[ps.key], writes=[AR(m)])
        dense_fm("w_in", OFF_G + b * D, D, 8, 128, rhs_fn, rkeys, n, cb)

    def merge_cb(first, n):
        def cb(m, ps):
            if first:
                S.op("dve", lambda e: e.tensor_tensor(arena[:, 8 + m, :n], ps.ap[:, :n], arena[:, m, :n], ALU.mult),
                     reads=[ps.key, AR(m)], writes=[AR(8 + m)])
            else:
                S.op("dve", lambda e: e.tensor_tensor(stg[:, :n], ps.ap[:, :n], arena[:, m, :n], ALU.mult),
                     reads=[ps.key, AR(m)], writes=["stg"])
                S.op("dve", lambda e: e.tensor_tensor(arena[:, 8 + m, :n], arena[:, 8 + m, :n], stg[:, :n], ALU.add),
                     reads=["stg", AR(8 + m)], writes=[AR(8 + m)])
        return cb

    def glu_branch(zt, n):
        for half in range(2):
            wA, kA = wload([(WB["w_glu"], "w_glu_bf", 0, half * 512, 512, 0)], 4, 128)
            wB, kB = wload([(WB["w_glu"], "w_glu_bf", 0, D + half * 512, 512, 0)], 4, 128)
            for mi in range(4):
                m = half * 4 + mi
                pa, pb_ = S.psum(), S.psum()
                mm(pa, 128, n, lambda k, mi=mi, wA=wA: wA[:, k, mi * 128:(mi + 1) * 128], lambda k: zt[:, k, :n], 4, [kA, "zT"])
                mm(pb_, 128, n, lambda k, mi=mi, wB=wB: wB[:, k, mi * 128:(mi + 1) * 128], lambda k: zt[:, k, :n], 4, [kB, "zT"])
                S.op("act", lambda e, pb_=pb_: e.activation(stg[:, :n], pb_.ap[:, :n], AF.Sigmoid), reads=[pb_.key], writes=["stg"])
                S.op("dve", lambda e, pa=pa: e.tensor_tensor(stg[:, :n], pa.ap[:, :n], stg[:, :n], ALU.mult), reads=[pa.key, "stg"], writes=["stg"])
                S.op("dve", lambda e, m=m: e.tensor_tensor(arena[:, 8 + m, :n], stg[:, :n], arena[:, m, :n], ALU.mult),
                     reads=["stg", AR(m)], writes=[AR(8 + m)])

    def out_proj_tm(Wk, kc_total, lhs_fn, lkeys, rows):
        pieces = [(k0, min(8, kc_total - k0)) for k0 in range(0, kc_total, 8)]
        ntt = 4 if rows == 128 else 1
        for cg in range(2):
            banks = [S.psum() for _ in range(ntt)]
            for pi, (k0, kc) in enumerate(pieces):
                wt, wk = wload([(WB[Wk], Wk + "_bf", k0 * 128, cg * 512, 512, 0)], kc, 128)
                for tt in range(ntt):
                    mm(banks[tt], rows, 512, lambda k, tt=tt, k0=k0: lhs_fn(tt, k0 + k), lambda k, wt=wt: wt[:, k, :], kc,
                       [wk] + lkeys, start=(pi == 0), stop=(pi == len(pieces) - 1))
            for tt in range(ntt):
                S.op("dve", lambda e, tt=tt, cg=cg, b=banks[tt]: e.tensor_tensor(
                    xres[:rows, tt, cg * 512:(cg + 1) * 512], xres[:rows, tt, cg * 512:(cg + 1) * 512], b.ap[:rows, :512], ALU.add),
                    reads=[banks[tt].key, ("x", tt)], writes=[("x", tt)])

    def norm_to_hT(gname, ntt, rows, dst):
        load_gbc(gname)
        for tt in range(ntt):
            rmsnorm_rows(xres[:rows, tt, :], rows, [("x", tt)], xnb[:rows, :], ["xnb"])
            transpose_to(lambda k, tt=tt: dst[:, k, tt * rows:(tt + 1) * rows], xnb, rows, 8, ["xnb"], ["hT"])

    def ffn(n, a_hook):
        for s in range(NF // 2):
            wt, wk = wload([(WB["w_up"], "w_up_bf", 0, 256 * s, 256, 0), (WB["w_up"], "w_up_bf", 0, DFF + 256 * s, 256, 256)], 8, 128)
            for jj in range(2):
                f = 2 * s + jj
                pa, pv = S.psum(), S.psum()
                mm(pa, 128, n, lambda k, wt=wt, jj=jj: wt[:, k, jj * 128:(jj + 1) * 128], lambda k: hT[:, k, :n], 8, [wk, "hT"])
                mm(pv, 128, n, lambda k, wt=wt, jj=jj: wt[:, k, 256 + jj * 128:256 + (jj + 1) * 128], lambda k: hT[:, k, :n], 8, [wk, "hT"])
                a_hook(f, pa)
                S.op("act", lambda e: e.activation(af[:, 2, :n], af[:, 1, :n], AF.Gelu_apprx_tanh), reads=[AFK(1)], writes=[AFK(2)])
                S.op("dve", lambda e, f=f, pv=pv: e.tensor_tensor(arena[:, f, :n], af[:, 2, :n], pv.ap[:, :n], ALU.mult),
                     reads=[AFK(2), pv.key], writes=[AR(f)])

    def final_norm_out(rows, ntt, out_fn):
        load_gbc("fin_g")
        for tt in range(ntt):
            rmsnorm_rows(xres[:rows, tt, :], rows, [("x", tt)], ytmp[:rows, :], ["ytmp"])
            S.dma("sp", out_fn(tt), ytmp[:rows, :], reads=["ytmp"], writes=[])

    for blk in range(nblk):
        T0, par = blk * TOK, blk % 2
        for tt in range(4):
            S.dma("sp", xres[:, tt, :], I["xp"][T0 + tt * 128:T0 + (tt + 1) * 128, :], writes=[("x", tt)])
        norm_to_hT("norm1_g", 4, 128, hT)
        hrhs = lambda k: hT[:, k, :]

        def inproj_cb(base_tile):
            def cb(m, ps):
                f = base_tile + m
                if f < 4:
                    evac(ubf[:, f, :], ps, 128, TOK, [], ["ubf"])
                elif f < 10:
                    evac(arena[:, 16 + f - 4, :], ps, 128, TOK, [], [AR(16 + f - 4)], scale=0.125)
                elif f < 16:
                    kt = f - 10
                    g, pair = kt // 2, kt % 2
                    if g < 2:
                        evac(kT12[:, g, pair, par, :], ps, 128, TOK, [], ["kT12"])
                    else:
                        evac(kT3[:, pair, T0:T0 + TOK], ps, 128, TOK, [], ["kT3"])
                else:
                    evac(qmT[:, f - 22, :], ps, 128, TOK, [], ["qmT"])
            return cb
        if stage < 1:
            window_outputs(blk)
            continue
        dense_fm("w_in", 0, 2048, 8, 128, hrhs, ["hT"], TOK, inproj_cb(0))
        dense_fm("w_in", OFF_QM, 512, 8, 128, hrhs, ["hT"], TOK, inproj_cb(22))
        if blk == 0:
            dbg("hT", hT[:, :, :], [128, 8, TOK], ["hT"], BF16)
            dbg("ubf", ubf[:, :, :], [128, 4, TOK], ["ubf"], BF16)
            dbg("qT", arena[:, 16:22, :], [128, 6, TOK], [AR(16 + i) for i in range(6)], BF16)
            dbg("qmT", qmT[:, :, :], [128, 4, TOK], ["qmT"], BF16)

        if stage < 1.5:
            window_outputs(blk)
            continue
        for g in range(3 if stage >= 1.7 else (2 if stage >= 1.6 else 1)):
            wt, wk = wload([(WB["w_in"], "w_in_bf", 0, OFF_V + g * 256, 256, 0)], 8, 128)
            if g < 2:
                for c in range(4):
                    ps = S.psum()
                    if g == 0:
                        lf = lambda k, c=c: hT[:, k, c * 128:(c + 1) * 128]
                    else:
                        lf = lambda k, c=c: hT[:, k, :].rearrange("p (i r) -> p r i", r=4)[:, c, :]
                    mm(ps, 128, 256, lf, lambda k, wt=wt: wt[:, k, 0:256], 8, [wk, "hT"])
                    evac((V1 if g == 0 else V2)[:, par, c, :], ps, 128, 256, [], ["V1" if g == 0 else "V2"])
            else:
                off, kt = (32 * blk) % 128, blk // 4
                for c2 in range(8):
                    ps = S.psum()

                    def vfn(e, ps=ps, c2=c2, wt=wt):
                        ins = None
                        for rr in range(2):
                            for k in range(8):
                                ins = e.matmul(ps.ap[:32, rr * 256:(rr + 1) * 256],
                                               hT[:, k, :].rearrange("p (i r) -> p r i", r=16)[:, 2 * c2 + rr, :],
                                               wt[:, k, 0:256], start=(k == 0), stop=(k == 7))
                        return ins
                    S.op("pe", vfn, reads=[wk, "hT"], writes=[ps.key])
                    for rr in range(2):
                        vb, vk = vst[rr], "vst%d" % rr
                        S.op("dve", lambda e, ps=ps, vb=vb, rr=rr: e.tensor_copy(vb[:32, :], ps.ap[:32, rr * 256:(rr + 1) * 256]),
                             reads=[ps.key], writes=[vk])
                        if stage != 1.75:
                            S.dma("sp", V3[off:off + 32, 2 * c2 + rr, kt, :], vb[:32, :], reads=[vk], writes=["V3"])
        window_outputs(blk)
        if stage < 2:
            continue

        gates_to_arena(0, hrhs, ["hT"], TOK)
        for gp in range(16):
            t = gp // 4
            tb = tabt[gp % 2]
            tk = "tabt%d" % (gp % 2)
            S.dma("sp", tb[:, :, :], tabs_d[gp].rearrange("p (a b) -> p a b", b=512), reads=["tabs_d"], writes=[tk])
            cos, sin = tb[:, 0, :], tb[:, 1, :]
            pr, pi_ = S.psum(), S.psum()
            mm(pr, 128, TOK, lambda k, gp=gp: bbT[:, gp, 0, :], lambda k, t=t: ubf[:, t, :], 1, ["bbT", "ubf"])
            mm(pi_, 128, TOK, lambda k, gp=gp: bbT[:, gp, 1, :], lambda k, t=t: ubf[:, t, :], 1, ["bbT", "ubf"])
            T = [af[:, j, 0:TOK] for j in range(4)]
            tt_ = lambda o, a, b_, op, rk, wk_, eng="dve": S.op(eng, lambda e: e.tensor_tensor(o, a, b_, op), reads=rk, writes=wk_)
            tt_(T[0], pr.ap[:, :TOK], cos, ALU.mult, [pr.key, tk], [AFK(0)])
            tt_(T[1], pi_.ap[:, :TOK], sin, ALU.mult, [pi_.key, tk], [AFK(1)])
            tt_(T[0], T[0], T[1], ALU.add, [AFK(0), AFK(1)], [AFK(0)])
            tt_(T[1], pi_.ap[:, :TOK], cos, ALU.mult, [pi_.key, tk, AFK(0)], [AFK(1)])
            tt_(T[2], pr.ap[:, :TOK], sin, ALU.mult, [pr.key, tk], [AFK(2)])
            tt_(T[1], T[1], T[2], ALU.subtract, [AFK(1), AFK(2)], [AFK(1)])
            magb = mag[:, gp:gp + 1].to_broadcast([128, TOK])
            S.op("dve", lambda e, gp=gp, magb=magb: e.tensor_tensor_scan(T[2], magb, T[0], rinit[:, 0, gp:gp + 1], ALU.mult, ALU.add),
                 reads=["mag", AFK(0), "rinit", AFK(1)], writes=[AFK(2)])
            S.op("dve", lambda e, gp=gp, magb=magb: e.tensor_tensor_scan(T[3], magb, T[1], rinit[:, 1, gp:gp + 1], ALU.mult, ALU.add),
                 reads=["mag", AFK(1), "rinit"], writes=[AFK(3)])
            S.op("act", lambda e, gp=gp: e.activation(rlast[:, 0, gp:gp + 1], af[:, 2, TOK - 1:TOK], AF.Copy), reads=[AFK(2)], writes=["rlast"])
            S.op("act", lambda e, gp=gp: e.activation(rlast[:, 1, gp:gp + 1], af[:, 3, TOK - 1:TOK], AF.Copy), reads=[AFK(3)], writes=["rlast"])
            s0, s1 = sbf[0], sbf[1]
            tt_(T[0], T[2], cos, ALU.mult, [AFK(2), tk], [AFK(0)])
            tt_(T[1], T[3], sin, ALU.mult, [AFK(3), tk], [AFK(1)])
            tt_(s0[:, :], T[0], T[1], ALU.subtract, [AFK(0), AFK(1)], ["sbf0"])
            tt_(T[0], T[2], sin, ALU.mult, [AFK(2), tk, "sbf0"], [AFK(0)])
            tt_(T[1], T[3], cos, ALU.mult, [AFK(3), tk, "sbf0"], [AFK(1)])
            tt_(s1[:, :], T[0], T[1], ALU.add, [AFK(0), AFK(1)], ["sbf1"])
            yb = YB[t % 2]

            def yfn(e, gp=gp, t=t, yb=yb):
                e.matmul(yb.ap[:, :TOK], cT[:, gp, 0, :], sbf[0][:, :], start=(gp % 4 == 0), stop=False)
                ins = e.matmul(yb.ap[:, :TOK], cT[:, gp, 1, :], sbf[1][:, :], start=False, stop=False)
                if gp % 4 == 3:
                    ins = e.matmul(yb.ap[:, :TOK], dD[:, t, :], ubf[:, t, :], start=False, stop=True)
                return ins
            S.op("pe", yfn, reads=["cT", "sbf0", "sbf1", "dD", "ubf"], writes=[yb.key])
            if gp % 4 == 3:
                S.op("act", lambda e, t=t, yb=yb: e.activation(zT[:, t, :], yb.ap[:, :TOK], AF.Gelu_apprx_tanh), reads=[yb.key], writes=["zT"])
        def cmul(dst, E, rk, wk_):
            S.op("dve", lambda e: e.tensor_mul(cst[:, 7, :], E[:, 0, :], rlast[:, 0, :]), reads=rk + ["rlast"], writes=["cst"])
            S.op("dve", lambda e: e.tensor_mul(cst[:, 8, :], E[:, 1, :], rlast[:, 1, :]), reads=rk + ["rlast"], writes=["cst"])
            S.op("dve", lambda e: e.tensor_sub(dst[:, 0, :], cst[:, 7, :], cst[:, 8, :]), reads=["cst"], writes=wk_)
            S.op("dve", lambda e: e.tensor_mul(cst[:, 7, :], E[:, 1, :], rlast[:, 0, :]), reads=rk + ["rlast"] + wk_, writes=["cst"])
            S.op("dve", lambda e: e.tensor_mul(cst[:, 8, :], E[:, 0, :], rlast[:, 1, :]), reads=rk + ["rlast"], writes=["cst"])
            S.op("dve", lambda e: e.tensor_add(dst[:, 1, :], cst[:, 7, :], cst[:, 8, :]), reads=["cst"], writes=wk_)
        if blk < NBLK - 1:
            cmul(rinit, e512, ["ecst"], ["rinit"])
        else:
            cmul(rinit, e511, ["ecst"], ["rinit"])
            for ri, oname in ((0, "p_sre"), (1, "p_sim")):
                ps = S.psum()
                S.op("pe", lambda e, ps=ps, ri=ri: e.transpose(ps.ap[:16, :128], rinit[:, ri, :], ident_f[:, :]),
                     reads=["rinit", "ident_f"], writes=[ps.key])
                S.op("dve", lambda e, ps=ps: e.tensor_copy(stg[:16, :128], ps.ap[:16, :128]), reads=[ps.key], writes=["stg"])
                S.dma("sp", O[oname], stg[:16, :128], reads=["stg"], writes=[])

        if stage < 3:
            continue
        glu_branch(zT, TOK)
        if blk == 0:
            dbg("zT", zT[:, :, :], [128, 4, TOK], ["zT"], BF16)
            dbg("m1", arena[:, 8:16, :], [128, 8, TOK], [AR(8 + i) for i in range(8)], BF16)

        if stage < 4:
            continue
        accn, accd = af[:64, 0, 0:TOK], af[:64, 1, 0:TOK]
        for j in range(4):
            pair, pb = j // 2, (j % 2) * 64
            for g in range(3):
                dil = DILS[g]
                sig = SLOPES[g * 4 + j] * dil
                if g < 2:
                    ncg, Q = 4, 128
                else:
                    ncg, Q = 16, 32
                kt3, v3 = blk // 4, blk % 4
                qfull = arena[pb:pb + 64, 16 + 2 * g + pair, :]
                qsl = (lambda c, qfull=qfull: qfull[:, c * 128:(c + 1) * 128]) if g == 0 else \
                      (lambda c, qfull=qfull, dil=dil: qfull.rearrange("p (i r) -> p r i", r=dil)[:, c, :])
                def kcur(c, g=g, pair=pair, pb=pb):
                    if g == 0:
                        return kT12[pb:pb + 64, 0, pair, par, c * 128:(c + 1) * 128]
                    if g == 1:
                        return kT12[pb:pb + 64, 1, pair, par, :].rearrange("p (i r) -> p r i", r=4)[:, c, :]
                    return kT3[pb:pb + 64, pair, kt3 * 2048:(kt3 + 1) * 2048].rearrange("p (i r) -> p r i", r=16)[:, c, 0:32 * v3 + 32]

                def kprev(c, g=g, pair=pair, pb=pb):
                    if g == 0:
                        if c > 0:
                            return kT12[pb:pb + 64, 0, pair, par, (c - 1) * 128:c * 128]
                        return kT12[pb:pb + 64, 0, pair, 1 - par, 384:512] if blk > 0 else None
                    if g == 1:
                        return kT12[pb:pb + 64, 1, pair, 1 - par, :].rearrange("p (i r) -> p r i", r=4)[:, c, :] if blk > 0 else None
                    if kt3 == 0:
                        return None
                    return kT3[pb:pb + 64, pair, (kt3 - 1) * 2048:kt3 * 2048].rearrange("p (i r) -> p r i", r=16)[:, c, :]

                def vcur(c, g=g, j=j):
                    if g == 0:
                        return V1[:, par, c, j * 64:(j + 1) * 64]
                    if g == 1:
                        return V2[:, par, c, j * 64:(j + 1) * 64]
                    return V3[0:32 * v3 + 32, c, kt3, j * 64:(j + 1) * 64]

                def vprev(c, g=g, j=j):
                    if g == 0:
                        return V1[:, par, c - 1, j * 64:(j + 1) * 64] if c > 0 else V1[:, 1 - par, 3, j * 64:(j + 1) * 64]
                    if g == 1:
                        return V2[:, 1 - par, c, j * 64:(j + 1) * 64]
                    return V3[:, c, kt3 - 1, j * 64:(j + 1) * 64]
                rows_cur = 128 if g < 2 else 32 * v3 + 32
                if g < 2:
                    baseA, baseB = base12[:, 0, :], base12[:rows_cur, 1, :]
                else:
                    baseA, baseB = base3[:, 2 * v3, :], base3[:rows_cur, 2 * v3 + 1, :]
                chunks = []
                for (rows, kf, vf, base, nm) in ((128, kprev, vprev, baseA, 0), (rows_cur, kcur, vcur, baseB, 1)):
                    cols = [c for c in range(ncg) if kf(c) is not None]
                    if not cols:
                        continue
                    ps = S.psum()
                    kaps = {c: kf(c) for c in cols}
                    qaps = {c: qsl(c) for c in cols}
                    vaps = {c: vf(c) for c in cols}

                    def qk(e, ps=ps, cols=cols, kaps=kaps, qaps=qaps, rows=rows, Q=Q):
                        ins = None
                        for c in cols:
                            ins = e.matmul(ps.ap[:rows, c * Q:(c + 1) * Q], kaps[c], qaps[c], start=True, stop=True)
                        return ins
                    S.op("pe", qk, reads=["kT12", "kT3", AR(16 + 2 * g + pair)], writes=[ps.key])
                    stmp = af[:rows, 2 + nm, 0:TOK]
                    S.op("dve", lambda e, ps=ps, rows=rows, base=base, stmp=stmp, sig=sig, ncg=ncg, Q=Q: e.scalar_tensor_tensor(
                        stmp.rearrange("p (c b) -> p c b", b=Q), base.unsqueeze(1).to_broadcast([rows, ncg, Q]), sig,
                        ps.ap[:rows, :TOK].rearrange("p (c b) -> p c b", b=Q), ALU.mult, ALU.add),
                        reads=[ps.key, "base12", "base3"], writes=[AFK(2 + nm)])
                    S.op("act", lambda e, rows=rows, stmp=stmp, nm=nm: e.activation(pT[nm][:rows, :], stmp, AF.Exp),
                         reads=[AFK(2 + nm)], writes=["pT%d" % nm])
                    chunks.append((rows, vaps, nm, cols))
                pn, pd = S.psum(), S.psum()

                def pv(e, pn=pn, chunks=chunks, ncg=ncg, Q=Q, ones=False):
                    ins = None
                    for c in range(ncg):
                        cs = [ch for ch in chunks if c in ch[3]]
                        for ci, (rows, vf, nm, cols) in enumerate(cs):
                            lhs = ones_bf[:rows, 0:64] if ones else vf[c]
                            ins = e.matmul(pn.ap[:64, c * Q:(c + 1) * Q], lhs, pT[nm][:rows, c * Q:(c + 1) * Q],
                                           start=(ci == 0), stop=(ci == len(cs) - 1))
                    return ins
                S.op("pe", pv, reads=["V1", "V2", "V3", "pT0", "pT1"], writes=[pn.key])
                S.op("pe", lambda e, pd=pd, pv=pv: pv(e, pn=pd, ones=True), reads=["ones_bf", "pT0", "pT1"], writes=[pd.key])
                for (acc, pz, ak) in ((accn, pn, AFK(0)), (accd, pd, AFK(1))):
                    if g == 0:
                        av = acc.rearrange("p (c b) -> p c b", b=128)
                    else:
                        av = acc.rearrange("p (i r) -> p r i", r=dil)
                    pzv = pz.ap[:64, :TOK].rearrange("p (c b) -> p c b", b=Q)
                    if g == 0:
                        S.op("act", lambda e, av=av, pzv=pzv: e.activation(av, pzv, AF.Copy), reads=[pz.key], writes=[ak])
                    else:
                        S.op("dve", lambda e, av=av, pzv=pzv: e.tensor_tensor(av, av, pzv, ALU.add), reads=[pz.key, ak], writes=[ak])
            S.op("dve", lambda e: e.reciprocal(accd, accd), reads=[AFK(1)], writes=[AFK(1)])
            S.op("dve", lambda e, j=j: e.tensor_tensor(oT[:, j, :], accn, accd, ALU.mult), reads=[AFK(0), AFK(1)], writes=["oT"])
        gates_to_arena(1, hrhs, ["hT"], TOK)
        dense_fm("w_atto", 0, D, 4, 64, lambda k: oT[:, k, :], ["oT"], TOK, merge_cb(False, TOK))
        if blk == 0:
            dbg("oT", oT[:, :, :], [64, 4, TOK], ["oT"], BF16)
            dbg("m2", arena[:, 8:16, :], [128, 8, TOK], [AR(8 + i) for i in range(8)], BF16)

        if stage < 5:
            continue
        for h in range(4):
            for c in range(2):
                ps = S.psum()
                mm(ps, 128, TOK, lambda k, h=h, c=c: mkT[:, h, c * 128:(c + 1) * 128], lambda k, h=h: qmT[:, h, :], 1, ["mkT", "qmT"])
                S.op("act", lambda e, ps=ps, c=c: e.activation(pTm[:, c, :], ps.ap[:, :TOK], AF.Exp, scale=SC_MEM), reads=[ps.key], writes=["pTm"])
            pn, pd = S.psum(), S.psum()
            mm(pn, 128, TOK, lambda k, h=h: mv[:, k, h * 128:(h + 1) * 128], lambda k: pTm[:, k, :], 2, ["mv", "pTm"])
            mm(pd, 128, TOK, lambda k: ones_bf[:, :], lambda k: pTm[:, k, :], 2, ["ones_bf", "pTm"])
            S.op("dve", lambda e, pd=pd: e.reciprocal(af[:, 0, 0:TOK], pd.ap[:, :TOK]), reads=[pd.key], writes=[AFK(0)])
            S.op("dve", lambda e, pn=pn, h=h: e.tensor_tensor(omT[:, h, :], pn.ap[:, :TOK], af[:, 0, 0:TOK], ALU.mult),
                 reads=[pn.key, AFK(0)], writes=["omT"])
        gates_to_arena(2, hrhs, ["hT"], TOK)
        dense_fm("w_memo", 0, D, 4, 128, lambda k: omT[:, k, :], ["omT"], TOK, merge_cb(False, TOK))
        if blk == 0:
            dbg("omT", omT[:, :, :], [128, 4, TOK], ["omT"], BF16)
            dbg("m3", arena[:, 8:16, :], [128, 8, TOK], [AR(8 + i) for i in range(8)], BF16)

        if stage < 6:
            continue
        out_proj_tm("w_out", 8, lambda tt, k: arena[:, 8 + k, tt * 128:(tt + 1) * 128], [AR(8 + k) for k in range(8)], 128)
        if blk == 0:
            dbg("xmid", xres[:, :, :], [128, 4, D], [("x", i) for i in range(4)])
        norm_to_hT("norm2_g", 4, 128, hT)

        def conv_hook(f, pa):
            at = af[:, 0, :]
            S.op("act", lambda e: e.activation(at[:, 2:TOK + 2], pa.ap[:, :TOK], AF.Copy), reads=[pa.key], writes=[AFK(0)])
            S.op("dve", lambda e, f=f: e.tensor_copy(at[:, 0:2], carry[:, f, :]), reads=["carry"], writes=[AFK(0)])
            S.op("dve", lambda e, f=f: e.tensor_copy(carry[:, f, :], at[:, TOK:TOK + 2]), reads=[AFK(0)], writes=["carry"])
            cc = af[:, 1, 0:TOK]
            S.op("dve", lambda e, f=f: e.tensor_scalar(cc, at[:, 2:TOK + 2], cwT[:, f, 2:3], cwT[:, f, 3:4], ALU.mult, ALU.add),
                 reads=[AFK(0), "cwT"], writes=[AFK(1)])
            S.op("dve", lambda e, f=f: e.scalar_tensor_tensor(cc, at[:, 1:TOK + 1], cwT[:, f, 1:2], cc, ALU.mult, ALU.add),
                 reads=[AFK(0), AFK(1), "cwT"], writes=[AFK(1)])
            S.op("dve", lambda e, f=f: e.scalar_tensor_tensor(cc, at[:, 0:TOK], cwT[:, f, 0:1], cc, ALU.mult, ALU.add),
                 reads=[AFK(0), AFK(1), "cwT"], writes=[AFK(1)])
        ffn(TOK, conv_hook)
        if blk == 0:
            dbg("gT", arena[:, :, :], [128, NF, TOK], [AR(i) for i in range(NF)], BF16)
        out_proj_tm("w_down", NF, lambda tt, k: arena[:, k, tt * 128:(tt + 1) * 128], [AR(k) for k in range(NF)], 128)
        if blk == 0:
            dbg("xfin", xres[:, :, :], [128, 4, D], [("x", i) for i in range(4)])
        final_norm_out(128, 4, lambda tt: O["y_p"][T0 + tt * 128:T0 + (tt + 1) * 128, :])
    for tcol in range(2):
        ps = S.psum()
        S.op("dve", lambda e, tcol=tcol: e.tensor_copy(stg[:, 0:NF], carry[:, :, tcol]), reads=["carry"], writes=["stg"])
        S.op("pe", lambda e, ps=ps: e.transpose(ps.ap[:NF, :128], stg[:, 0:NF], ident_f[:, :]), reads=["stg", "ident_f"], writes=[ps.key])
        S.op("dve", lambda e, ps=ps: e.tensor_copy(ytmp[:NF, 0:128], ps.ap[:NF, :128]), reads=[ps.key], writes=["ytmp"])
        S.dma("sp", O["p_conv"][tcol:tcol + 1, :].rearrange("t (f q) -> (t f) q", q=128), ytmp[:NF, 0:128], reads=["ytmp"], writes=[])

    n = NS
    S.op("dve", lambda e: e.memset(small[:, 63:64], 0.0), writes=["kT3", "V3", "kT12", "V1", "V2"])
    PA = kT3[:, :, :].bitcast(F32).rearrange("p a b -> p (a b)")
    PB = V3[:, :, :, :].bitcast(F32).rearrange("p a b c -> p (a b c)")
    z_tok = PA[:n, 0:2816]
    ZQ, ZK, ZV, ZM = 0, 768, 1536, 2304
    s0raw = PA[:n, 2816:2816 + 1024].rearrange("p (a b) -> p a b", b=512)
    PAb = PA[:, 3840:4096]
    zq_bf = PB[:n, 0:640].bitcast(BF16)
    vt_bf = PB[:n, 640:1024].bitcast(BF16)
    s0T = PB[:, 1024:1536].rearrange("p (r g b) -> p r g b", r=2, g=16)
    snw = PB[:, 1536:2048].rearrange("p (r g b) -> p r g b", r=2, g=16)
    snb = PB[:, 2048:2304].bitcast(BF16).rearrange("p (r g b) -> p r g b", r=2, g=16)
    scT = PB[:, 2304:2496].rearrange("p (b c) -> p b c", c=12)
    pTs = PB[:, 2496:2592].bitcast(BF16).rearrange("p (b c) -> p b c", c=12)
    scm = PB[:, 2592:2720].rearrange("p (c b h) -> p c b h", c=2, b=16)
    pmb = PB[:, 2720:2784].bitcast(BF16).rearrange("p (c b h) -> p c b h", c=2, b=16)
    kbuf = [PB[:, 2784 + i * 512:2784 + (i + 1) * 512] for i in range(2)]
    SK = ["skey%d" % i for i in range(12)]
    selb = sb("selb", [16, 16, 128], BF16)
    tabs_s = sb("tabs_s", [128, 12], F32)
    vbb = [sb("vbb%d" % i, [128, 512], BF16) for i in range(2)]
    prd = sb("prd", [128, 512], F32)
    cbT = sb("cbT", [128, NF, 2, NS], F32)
    anT = sb("anT", [128, NF, NS], F32)
    S.dma("pool", selb[:, :, :], C["sel"].rearrange("p (a b) -> p a b", b=128), writes=["selb"])
    S.dma("sp", tabs_s[:], C["tabs_s"], writes=["tabs_s"])

    S.dma("sp", xres[:n, 0, :], I["xs"], writes=[("x", 0)])
    norm_to_hT("norm1_g", 1, n, hT)
    hr = lambda k: hT[:, k, :n]
    def ucb(m, ps):
        evac(ubf[:, m, :n], ps, 128, n, [], ["ubf"])
    dense_fm("w_in", 0, 512, 8, 128, hr, ["hT"], n, ucb)
    for ci, c0 in enumerate(range(OFF_Q, OFF_G, 512)):
        w = min(512, OFF_G - c0)
        wt, wk = wload([(WB["w_in"], "w_in_bf", 0, c0, w, 0)], 8, 128)
        ps = S.psum()
        mm(ps, n, w, lambda k: hT[:, k, :n], lambda k, wt=wt, w=w: wt[:, k, :w], 8, [wk, "hT"])
        S.op("dve", lambda e, ps=ps, ci=ci, w=w: e.tensor_copy(z_tok[:, ci * 512:ci * 512 + w], ps.ap[:n, :w]), reads=[ps.key], writes=["z_tok"])
    for g in range(3):
        S.dma("sp", O["s_k%d" % (g + 1)], z_tok[:, ZK + g * 256:ZK + (g + 1) * 256], reads=["z_tok"], writes=[])
        S.dma("sp", O["s_v%d" % (g + 1)], z_tok[:, ZV + g * 256:ZV + (g + 1) * 256], reads=["z_tok"], writes=[])
    S.op("dve", lambda e: e.tensor_copy(zq_bf[:, 0:768], z_tok[:, ZQ:ZQ + 768]), reads=["z_tok"], writes=["zq_bf"])
    S.op("dve", lambda e: e.tensor_copy(zq_bf[:, 768:1280], z_tok[:, ZM:ZM + 512]), reads=["z_tok"], writes=["zq_bf"])
    S.op("dve", lambda e: e.tensor_copy(vt_bf[:, :], z_tok[:, ZV:ZV + 768]), reads=["z_tok"], writes=["vt_bf"])

    for ri, nm in ((0, "sre"), (1, "sim")):
        for q4 in range(4):
            S.dma("sp", s0raw[:, q4 % 2, :], I[nm][:, q4 * 512:(q4 + 1) * 512], writes=["s0raw%d" % (q4 % 2)])
            ps = S.psum()

            def trs(e, ps=ps, q4=q4):
                ins = None
                for gg in range(4):
                    ins = e.transpose(ps.ap[:, gg * 16:(gg + 1) * 16], s0raw[:, q4 % 2, gg * 128:(gg + 1) * 128], ident_f[:n, :n])
                return ins
            S.op("pe", trs, reads=["s0raw%d" % (q4 % 2), "ident_f"], writes=[ps.key])
            S.op("dve", lambda e, ps=ps, ri=ri, q4=q4: e.tensor_copy(s0T[:, ri, 4 * q4:4 * q4 + 4, :],
                                                                       ps.ap[:, 0:64].rearrange("p (g b) -> p g b", b=16)),
                 reads=[ps.key], writes=["s0T"])
    pbr, pbi = S.psum(), S.psum()
    for ri, pz in ((0, pbr), (1, pbi)):
        def bufn(e, ri=ri, pz=pz):
            ins = None
            for gp in range(16):
                ins = e.matmul(pz.ap[:, gp * 16:(gp + 1) * 16], bbT[:, gp, ri, :], ubf[:, gp // 4, :n], start=True, stop=True)
            return ins
        S.op("pe", bufn, reads=["bbT", "ubf"], writes=[pz.key])
    abr = ab[:, 0, :].unsqueeze(2).to_broadcast([128, 16, 16])
    abi = ab[:, 1, :].unsqueeze(2).to_broadcast([128, 16, 16])
    t1 = af[:, 0, 0:256].rearrange("p (g b) -> p g b", b=16)
    t2_ = af[:, 1, 0:256].rearrange("p (g b) -> p g b", b=16)
    pv3 = lambda pz: pz.ap[:, 0:256].rearrange("p (g b) -> p g b", b=16)
    dv = lambda fn, rk, wk_: S.op("dve", fn, reads=rk, writes=wk_)
    dv(lambda e: e.tensor_tensor(t1, s0T[:, 0], abr, ALU.mult), ["s0T", "ab"], [AFK(0)])
    dv(lambda e: e.tensor_tensor(t2_, s0T[:, 1], abi, ALU.mult), ["s0T", "ab"], [AFK(1)])
    dv(lambda e: e.tensor_tensor(t1, t1, t2_, ALU.subtract), [AFK(0), AFK(1)], [AFK(0)])
    dv(lambda e: e.tensor_tensor(snw[:, 0], t1, pv3(pbr), ALU.add), [AFK(0), pbr.key], ["snw"])
    dv(lambda e: e.tensor_tensor(t1, s0T[:, 1], abr, ALU.mult), ["s0T", "ab", "snw"], [AFK(0)])
    dv(lambda e: e.tensor_tensor(t2_, s0T[:, 0], abi, ALU.mult), ["s0T", "ab", "snw"], [AFK(1)])
    dv(lambda e: e.tensor_tensor(t1, t1, t2_, ALU.add), [AFK(0), AFK(1)], [AFK(0)])
    dv(lambda e: e.tensor_tensor(snw[:, 1], t1, pv3(pbi), ALU.add), [AFK(0), pbi.key], ["snw"])
    dv(lambda e: e.tensor_copy(snb[:, :, :, :], snw[:, :, :, :]), ["snw"], ["snb"])
    for ri, oname in ((0, "s_sre"), (1, "s_sim")):
        for q4 in range(4):
            ps = S.psum()

            def trb(e, ps=ps, ri=ri, q4=q4):
                ins = None
                for gg in range(4):
                    ins = e.transpose(ps.ap[:n, gg * 128:(gg + 1) * 128], snw[:, ri, 4 * q4 + gg, :], ident_f[:, :])
                return ins
            S.op("pe", trb, reads=["snw", "ident_f"], writes=[ps.key])
            S.op("dve", lambda e, ps=ps: e.tensor_copy(stg[:n, :], ps.ap[:n, :512]), reads=[ps.key], writes=["stg"])
            S.dma("sp", O[oname][:, q4 * 512:(q4 + 1) * 512], stg[:n, :], reads=["stg"], writes=[])
    for t in range(4):
        yb = YB[t % 2]

        def ysf(e, t=t, yb=yb):
            ins = None
            for gi in range(4):
                gp = 4 * t + gi
                e.matmul(yb.ap[:, :n], cT[:, gp, 0, :], snb[:, 0, gp, :], start=(gi == 0), stop=False)
                e.matmul(yb.ap[:, :n], cT[:, gp, 1, :], snb[:, 1, gp, :], start=False, stop=False)
            return e.matmul(yb.ap[:, :n], dD[:, t, :], ubf[:, t, :n], start=False, stop=True)
        S.op("pe", ysf, reads=["cT", "snb", "dD", "ubf"], writes=[yb.key])
        S.op("act", lambda e, t=t, yb=yb: e.activation(zT[:, t, :n], yb.ap[:, :n], AF.Gelu_apprx_tanh), reads=[yb.key], writes=["zT"])
    gates_to_arena(0, hr, ["hT"], n)
    glu_branch(zT, n)
    dbg("s_zT", zT[:, :, :n], [128, 4, n], ["zT"], BF16)
    dbg("s_m1", arena[:, 8:16, :n], [128, 8, n], [AR(8 + i) for i in range(8)], BF16)

    caches = [("ck1", "cv1"), ("ck2", "cv2"), ("ck3", "cv3")]
    for b in range(n):
        pq0, pq1 = S.psum(), S.psum()
        S.op("pe", lambda e, b=b, pq0=pq0: e.matmul(pq0.ap[:, :512], selb[:, b, :], zq_bf[:, 0:512], start=True, stop=True),
             reads=["selb", "zq_bf"], writes=[pq0.key])
        S.op("pe", lambda e, b=b, pq1=pq1: e.matmul(pq1.ap[:, :256], selb[:, b, :], zq_bf[:, 512:768], start=True, stop=True),
             reads=["selb", "zq_bf"], writes=[pq1.key])
        for g in range(3):
            kb, kk = kbuf[g % 2], "kbuf%d" % (g % 2)
            S.dma("sp", kb[:, 0:256], I[caches[g][0]][b].rearrange("(i d) c -> i d c", d=DILS[g])[:, 0, :], writes=[kk])
            qsrc = pq0.ap[:, g * 256:(g + 1) * 256] if g < 2 else pq1.ap[:, 0:256]
            S.op("dve", lambda e, kb=kb, qsrc=qsrc: e.tensor_tensor(prd[:, 0:256], kb[:, 0:256], qsrc, ALU.mult),
                 reads=[kk, pq0.key, pq1.key], writes=["prd"])
            S.op("dve", lambda e, b=b, g=g: e.tensor_reduce(scT[:, b, g * 4:(g + 1) * 4], prd[:, 0:256].rearrange("p (h e) -> p h e", e=64),
                                                           AX.X, ALU.add), reads=["prd"], writes=["scT"])
    S.op("dve", lambda e: e.scalar_tensor_tensor(scT[:, :, :], scT[:, :, :], 0.125, tabs_s[:, :].unsqueeze(1).to_broadcast([128, 16, 12]),
                                                  ALU.mult, ALU.add), reads=["scT", "tabs_s"], writes=["scT"])
    dbg("s_scT", scT[:, :, :], [128, 16, 12], ["scT"])
    S.op("act", lambda e: e.activation(pTs[:, :, :], scT[:, :, :], AF.Exp), reads=["scT"], writes=["pTs"])
    pnew = small[:n, 40:52]
    S.op("dve", lambda e: e.tensor_tensor(prd[:n, 0:384].rearrange("p (a b) -> p a b", b=1)[:, :, 0] if False else PAb[:n, 0:1], PAb[:n, 0:1], PAb[:n, 0:1], ALU.mult) if False else
         e.tensor_tensor(stg[:n, 0:512], z_tok[:, ZQ:ZQ + 512], z_tok[:, ZK:ZK + 512], ALU.mult), reads=["z_tok"], writes=["stg"])
    S.op("dve", lambda e: e.tensor_reduce(pnew[:, 0:8], stg[:n, 0:512].rearrange("p (h e) -> p h e", e=64), AX.X, ALU.add),
         reads=["stg"], writes=["pnew"])
    S.op("dve", lambda e: e.tensor_tensor(stg[:n, 0:256], z_tok[:, ZQ + 512:ZQ + 768], z_tok[:, ZK + 512:ZK + 768], ALU.mult),
         reads=["z_tok", "pnew"], writes=["stg"])
    S.op("dve", lambda e: e.tensor_reduce(pnew[:, 8:12], stg[:n, 0:256].rearrange("p (h e) -> p h e", e=64), AX.X, ALU.add),
         reads=["stg"], writes=["pnew"])
    S.op("act", lambda e: e.activation(pnew, pnew, AF.Exp, scale=0.125), reads=["pnew"], writes=["pnew"])
    Dm = sb("Dm", [16, 12, 16], BF16)
    for c in range(12):
        S.op("dve", lambda e, c=c: e.tensor_scalar_mul(Dm[:, c, :], ident_f[:n, :n], pnew[:, c:c + 1]), reads=["pnew", "ident_f"], writes=["Dm"])
    dbg("s_pnew", pnew, [n, 12], ["pnew"])
    pnum, pden = S.psum(), S.psum()
    S.op("pe", lambda e: e.matmul(pnum.ap[:64, 0:64], ones_bf[:, 0:64], zer_bf[:, 0:64], start=True, stop=False),
         reads=["ones_bf", "zer_bf"], writes=[pnum.key])
    for b in range(n):
        for g in range(3):
            kb, kk = kbuf[g % 2], "kbuf%d" % (g % 2)
            vb, vk = vbb[g % 2], "vbb%d" % (g % 2)
            S.dma("sp", kb[:, 0:256], I[caches[g][1]][b].rearrange("(i d) c -> i d c", d=DILS[g])[:, 0, :], writes=[kk])
            S.op("act", lambda e, kb=kb, vb=vb: e.activation(vb[:, 0:256], kb[:, 0:256], AF.Copy), reads=[kk], writes=[vk])

            def pvs(e, b=b, g=g, vb=vb):
                ins = None
                for j in range(4):
                    ins = e.matmul(pnum.ap[:64, j * 16 + b:j * 16 + b + 1], vb[:, j * 64:(j + 1) * 64], pTs[:, b, g * 4 + j:g * 4 + j + 1],
                                   start=False, stop=False)
                return ins
            S.op("pe", pvs, reads=[vk, "pTs"], writes=[pnum.key])

    def pvnew(e):
        ins = None
        for j in range(4):
            for g in range(3):
                ins = e.matmul(pnum.ap[:64, j * 16:(j + 1) * 16], vt_bf[:, g * 256 + j * 64:g * 256 + (j + 1) * 64], Dm[:, g * 4 + j, :],
                               start=False, stop=(g == 2 and j == 3))
        return ins
    S.op("pe", pvnew, reads=["vt_bf", "Dm"], writes=[pnum.key])

    def dens(e):
        ins = None
        for j in range(4):
            for g in range(3):
                e.matmul(pden.ap[:64, j * 16:(j + 1) * 16], ones_bf[:, 0:64], pTs[:, :, g * 4 + j], start=(g == 0), stop=False)
            for g in range(3):
                ins = e.matmul(pden.ap[:64, j * 16:(j + 1) * 16], ones_bf[:n, 0:64], Dm[:, g * 4 + j, :], start=False, stop=(g == 2))
        return ins
    S.op("pe", dens, reads=["pTs", "Dm", "ones_bf"], writes=[pden.key])
    S.op("dve", lambda e: e.tensor_copy(stg[:64, 0:64], pden.ap[:64, 0:64]), reads=[pden.key], writes=["stg"])
    S.op("dve", lambda e: e.tensor_copy(stg[:64, 64:128], pnum.ap[:64, 0:64]), reads=[pnum.key, "stg"], writes=["stg"])
    dbg("s_dn", stg[:64, 0:128], [64, 128], ["stg"])
    S.op("dve", lambda e: e.reciprocal(af[:64, 0, 0:64], pden.ap[:64, 0:64]), reads=[pden.key], writes=[AFK(0)])
    S.op("dve", lambda e: e.tensor_tensor(oT[:, :, :n], pnum.ap[:64, 0:64].rearrange("p (j b) -> p j b", b=16),
                                          af[:64, 0, 0:64].rearrange("p (j b) -> p j b", b=16), ALU.mult),
         reads=[pnum.key, AFK(0)], writes=["oT"])
    gates_to_arena(1, hr, ["hT"], n)
    dense_fm("w_atto", 0, D, 4, 64, lambda k: oT[:, k, :n], ["oT"], n, merge_cb(False, n))
    dbg("s_oT", oT[:, :, :n], [64, 4, n], ["oT"], BF16)
    dbg("s_m2", arena[:, 8:16, :n], [128, 8, n], [AR(8 + i) for i in range(8)], BF16)

    for b in range(n):
        pq = S.psum()
        S.op("pe", lambda e, b=b, pq=pq: e.matmul(pq.ap[:, :512], selb[:, b, :], zq_bf[:, 768:1280], start=True, stop=True),
             reads=["selb", "zq_bf"], writes=[pq.key])
        for c in range(2):
            kb, kk = kbuf[c], "kbuf%d" % c
            S.dma("sp", kb[:, :], I["cmk"][b, c * 128:(c + 1) * 128, :], writes=[kk])
            S.op("dve", lambda e, kb=kb, pq=pq: e.tensor_tensor(prd[:, :], kb[:, :], pq.ap[:, :512], ALU.mult), reads=[kk, pq.key], writes=["prd"])
            S.op("dve", lambda e, b=b, c=c: e.tensor_reduce(scm[:, c, b, :], prd[:, :].rearrange("p (h e) -> p h e", e=128), AX.X, ALU.add),
                 reads=["prd"], writes=["scm"])
    S.op("act", lambda e: e.activation(pmb[:, :, :, :], scm[:, :, :, :], AF.Exp, scale=SC_MEM), reads=["scm"], writes=["pmb"])
    pnm, pdm = S.psum(), S.psum()
    S.op("pe", lambda e: e.matmul(pnm.ap[:, 0:64], ones_bf[:, :], zer_bf[:, 0:64], start=True, stop=False),
         reads=["ones_bf", "zer_bf"], writes=[pnm.key])
    for b in range(n):
        for c in range(2):
            kb, kk = kbuf[c], "kbuf%d" % c
            vb, vk = vbb[c], "vbb%d" % c
            S.dma("sp", kb[:, :], I["cmv"][b, c * 128:(c + 1) * 128, :], writes=[kk])
            S.op("act", lambda e, kb=kb, vb=vb: e.activation(vb[:, :], kb[:, :], AF.Copy), reads=[kk], writes=[vk])

            def pvm(e, b=b, c=c, vb=vb):
                ins = None
                for h in range(4):
                    ins = e.matmul(pnm.ap[:, h * 16 + b:h * 16 + b + 1], vb[:, h * 128:(h + 1) * 128], pmb[:, c, b, h:h + 1],
                                   start=False, stop=(c == 1 and b == n - 1 and h == 3))
                return ins
            S.op("pe", pvm, reads=[vk, "pmb"], writes=[pnm.key])

    def denm(e):
        ins = None
        for h in range(4):
            for c in range(2):
                ins = e.matmul(pdm.ap[:, h * 16:(h + 1) * 16], ones_bf[:, :], pmb[:, c, :, h], start=(c == 0), stop=(c == 1))
        return ins
    S.op("pe", denm, reads=["pmb", "ones_bf"], writes=[pdm.key])
    S.op("dve", lambda e: e.reciprocal(af[:, 0, 0:64], pdm.ap[:, 0:64]), reads=[pdm.key], writes=[AFK(0)])
    S.op("dve", lambda e: e.tensor_tensor(omT[:, :, :n], pnm.ap[:, 0:64].rearrange("p (h b) -> p h b", b=16),
                                          af[:, 0, 0:64].rearrange("p (h b) -> p h b", b=16), ALU.mult),
         reads=[pnm.key, AFK(0)], writes=["omT"])
    gates_to_arena(2, hr, ["hT"], n)
    dense_fm("w_memo", 0, D, 4, 128, lambda k: omT[:, k, :n], ["omT"], n, merge_cb(False, n))
    dbg("s_omT", omT[:, :, :n], [128, 4, n], ["omT"], BF16)
    dbg("s_m3", arena[:, 8:16, :n], [128, 8, n], [AR(8 + i) for i in range(8)], BF16)

    out_proj_tm("w_out", 8, lambda tt, k: arena[:, 8 + k, 0:n], [AR(8 + k) for k in range(8)], n)
    norm_to_hT("norm2_g", 1, n, hT)
    for tcol in range(2):
        crow = PA[:n, 0:2816]
        S.dma("sp", crow, I["sconv"][:, tcol, :], reads=[], writes=["z_tok"])
        for f0 in range(0, NF, 8):
            nf = min(8, NF - f0)
            ps = S.psum()

            def trc2(e, ps=ps, f0=f0, nf=nf, crow=crow):
                ins = None
                for fi in range(nf):
                    ins = e.transpose(ps.ap[:, fi * 16:(fi + 1) * 16], crow[:, (f0 + fi) * 128:(f0 + fi + 1) * 128], ident_f[:n, :n])
                return ins
            S.op("pe", trc2, reads=["z_tok", "ident_f"], writes=[ps.key])
            S.op("dve", lambda e, ps=ps, f0=f0, nf=nf, tcol=tcol: e.tensor_copy(
                cbT[:, f0:f0 + nf, tcol, :], ps.ap[:, 0:nf * 16].rearrange("p (f b) -> p f b", b=16)), reads=[ps.key], writes=["cbT"])
        if tcol == 1:
            S.dma("sp", O["s_conv"][:, 0, :], crow, reads=["z_tok"], writes=[])

    def conv_hook_s(f, pa):
        cc = af[:, 1, 0:n]
        S.op("dve", lambda e, f=f, pa=pa: e.tensor_copy(anT[:, f, :], pa.ap[:, :n]), reads=[pa.key], writes=["anT"])
        S.op("dve", lambda e, f=f: e.tensor_scalar(cc, anT[:, f, :], cwT[:, f, 2:3], cwT[:, f, 3:4], ALU.mult, ALU.add),
             reads=["anT", "cwT"], writes=[AFK(1)])
        S.op("dve", lambda e, f=f: e.scalar_tensor_tensor(cc, cbT[:, f, 1, :], cwT[:, f, 1:2], cc, ALU.mult, ALU.add),
             reads=["cbT", AFK(1), "cwT"], writes=[AFK(1)])
        S.op("dve", lambda e, f=f: e.scalar_tensor_tensor(cc, cbT[:, f, 0, :], cwT[:, f, 0:1], cc, ALU.mult, ALU.add),
             reads=["cbT", AFK(1), "cwT"], writes=[AFK(1)])
    ffn(n, conv_hook_s)
    for f0 in range(0, NF, 4):
        nf = min(4, NF - f0)
        ps = S.psum()

        def tra(e, ps=ps, f0=f0, nf=nf):
            ins = None
            for fi in range(nf):
                ins = e.transpose(ps.ap[:n, fi * 128:(fi + 1) * 128], anT[:, f0 + fi, :], ident_f[:, :])
            return ins
        S.op("pe", tra, reads=["anT", "ident_f"], writes=[ps.key])
        S.op("dve", lambda e, ps=ps, nf=nf: e.tensor_copy(stg[:n, 0:nf * 128], ps.ap[:n, 0:nf * 128]), reads=[ps.key], writes=["stg"])
        S.dma("sp", O["s_conv"][:, 1, f0 * 128:(f0 + nf) * 128], stg[:n, 0:nf * 128], reads=["stg"], writes=[])
    out_proj_tm("w_down", NF, lambda tt, k: arena[:, k, 0:n], [AR(k) for k in range(NF)], n)
    final_norm_out(n, 1, lambda tt: O["y_s"])

    S.emit()
    return nc


def kernel(**inp):
    f = lambda a: np.ascontiguousarray(np.asarray(a, dtype=np.float32))
    consts = host_consts()
    shared = {
        "norm1_g": f(inp["norm1_g"]), "w_in": f(inp["w_in"][0]),
        "a_re": f(inp["ssm_a_re"][0]).reshape(16, 128), "a_im": f(inp["ssm_a_im"][0]).reshape(16, 128),
        "log_dt": f(inp["ssm_log_dt"][0]).reshape(16, 2),
        "b_re": f(inp["ssm_b_re"][0]).reshape(2048, 16), "b_im": f(inp["ssm_b_im"][0]).reshape(2048, 16),
        "c_re": f(inp["ssm_c_re"][0]).reshape(512, 64), "c_im": f(inp["ssm_c_im"][0]).reshape(512, 64),
        "ssm_d": f(inp["ssm_d"][0]).reshape(4, 128), "w_glu": f(inp["w_ssm_glu"][0]), "w_atto": f(inp["w_att_o"][0]),
        "mem_g": f(inp["mem_norm_g"]), "w_memkv": f(inp["w_mem_kv"][0]), "w_memo": f(inp["w_mem_o"][0]),
        "w_out": f(inp["w_out"][0]), "norm2_g": f(inp["norm2_g"]), "w_up": f(inp["w_up"][0]),
        "conv_w": f(inp["ffn_conv_w"][0]), "conv_b": f(inp["ffn_conv_b"]), "w_down": f(inp["w_down"][0]),
        "fin_g": f(inp["final_norm_g"]).reshape(1, D),
    }
    shared.update(consts)
    in_maps = []
    for c in range(8):
        s, b0 = c % 4, c * NS
        m = dict(shared)
        m["xp"] = f(inp["x_prompt"][s])
        m["memp"] = f(inp["mem_prompt"][s])
        m["xs"] = f(inp["x_sample"][b0:b0 + NS, 0])
        m["sre"] = f(inp["state_ssm_re"][0, b0:b0 + NS]).reshape(NS, 2048)
        m["sim"] = f(inp["state_ssm_im"][0, b0:b0 + NS]).reshape(NS, 2048)
        caches = {"ck1": inp["cache_w1_k"], "cv1": inp["cache_w1_v"], "ck2": inp["cache_w2_k"], "cv2": inp["cache_w2_v"],
                  "ck3": inp["cache_w3_k"], "cv3": inp["cache_w3_v"]}
        for nm, arr in caches.items():
            a = np.asarray(arr)[0, b0:b0 + NS]
            m[nm] = f(a).reshape(NS, a.shape[1], 256)
        m["cmk"] = f(inp["cache_mem_k"][0, b0:b0 + NS]).reshape(NS, 256, 512)
        m["cmv"] = f(inp["cache_mem_v"][0, b0:b0 + NS]).reshape(NS, 256, 512)
        m["sconv"] = f(inp["state_ffn_conv"][0, b0:b0 + NS])
        in_maps.append(m)
    nc = build_nc()
    res = run_bass_kernel_spmd(nc, in_maps, core_ids=list(range(8)))
    R = res.results
    cat = lambda name, cores: np.stack([np.asarray(R[c][name], dtype=np.float32) for c in cores], axis=0)
    P = range(4)
    A = range(8)
    sm = lambda name, shape: np.concatenate([np.asarray(R[c][name], np.float32) for c in A], axis=0).reshape(shape)[None]
    outs = (
        cat("y_p", P), sm("y_s", (128, 1, D)),
        cat("p_sre", P).reshape(4, 32, 64)[None], cat("p_sim", P).reshape(4, 32, 64)[None],
        cat("p_k1", P).reshape(4, 128, 4, 64)[None], cat("p_v1", P).reshape(4, 128, 4, 64)[None],
        cat("p_k2", P).reshape(4, 512, 4, 64)[None], cat("p_v2", P).reshape(4, 512, 4, 64)[None],
        cat("p_k3", P).reshape(4, 2048, 4, 64)[None], cat("p_v3", P).reshape(4, 2048, 4, 64)[None],
        cat("p_mk", P).reshape(4, 256, 4, 128)[None], cat("p_mv", P).reshape(4, 256, 4, 128)[None],
        cat("p_conv", P)[None],
        sm("s_sre", (128, 32, 64)), sm("s_sim", (128, 32, 64)),
        sm("s_k1", (128, 1, 4, 64)), sm("s_v1", (128, 1, 4, 64)), sm("s_k2", (128, 1, 4, 64)), sm("s_v2", (128, 1, 4, 64)),
        sm("s_k3", (128, 1, 4, 64)), sm("s_v3", (128, 1, 4, 64)), sm("s_conv", (128, 2, DFF)),
    )
    outs = list(outs)
    outs[1] = outs[1][0]
    return tuple(np.ascontiguousarray(o) for o in outs)
```

```python
import math
from contextlib import ExitStack

import numpy as np
import concourse.bass as bass
import concourse.mybir as mybir
from concourse.bass_utils import run_bass_kernel_spmd

F32 = mybir.dt.float32
BF16 = mybir.dt.bfloat16
AF = mybir.ActivationFunctionType
ALU = mybir.AluOpType
AX = mybir.AxisListType


class PsBank:
    def __init__(self, key, ap):
        self.key = key
        self.ap = ap


class Sched:
    ENG = ("pe", "act", "dve", "pool", "sp")
    NDS = 8

    def __init__(self, nc):
        self.nc = nc
        self.es = ExitStack()
        self.q = {e: [] for e in self.ENG}
        self.cnt = {e: 0 for e in self.ENG}
        self.dcnt = {e: 0 for e in self.ENG}
        self.known = {e: {} for e in self.ENG}
        self.lastw = {}
        self.readers = {}
        self.sem = {e: self.es.enter_context(nc.semaphore("s_" + e)) for e in self.ENG}
        self.dsem = {e: [self.es.enter_context(nc.semaphore("d_%s%d" % (e, i))) for i in range(self.NDS)]
                     for e in ("sp", "pool", "act")}
        self.banks = []
        for i in range(8):
            t = self.es.enter_context(nc.psum_tensor("psb%d" % i, [128, 512], F32))
            self.banks.append(PsBank("psb%d" % i, t))
        self.bank_i = 0
        self.nrot = 8

    def sb(self, name, shape, dtype):
        return self.es.enter_context(self.nc.sbuf_tensor("sb_" + name, shape, dtype))

    def psum(self):
        b = self.banks[self.bank_i % self.nrot]
        self.bank_i += 1
        return b

    def _deps(self, reads, writes):
        deps = []
        for b in reads:
            deps.extend(self.lastw.get(b, ()))
        for b in writes:
            deps.extend(self.lastw.get(b, ()))
            deps.extend(self.readers.get(b, ()))
        return deps

    def _waits(self, eng, deps):
        waits = []
        kn = self.known[eng]
        for ev in deps:
            if ev[0] == "c":
                _, e2, idx = ev
                if e2 == eng and idx < self.cnt[eng] - 1:
                    continue
                if kn.get(("c", e2), 0) >= idx:
                    continue
                kn[("c", e2)] = idx
                waits.append(ev)
            else:
                _, qn, j = ev
                s, c = j % self.NDS, j // self.NDS + 1
                if kn.get(("d", qn, s), 0) >= c:
                    continue
                kn[("d", qn, s)] = c
                waits.append(ev)
        return waits

    def _record(self, ev, reads, writes):
        for b in writes:
            if self.readers.get(b) or b not in self.lastw:
                self.lastw[b] = [ev]
            else:
                self.lastw[b] = self.lastw[b] + [ev]
            self.readers[b] = []
        for b in reads:
            if b not in writes:
                self.readers.setdefault(b, []).append(ev)

    def op(self, eng, fn, reads=(), writes=()):
        deps = self._deps(reads, writes)
        waits = self._waits(eng, deps)
        self.cnt[eng] += 1
        ev = ("c", eng, self.cnt[eng])
        self.q[eng].append((fn, waits, ev))
        self._record(ev, reads, writes)

    def dma(self, qn, out, in_, reads=(), writes=(), **kw):
        deps = self._deps(reads, writes)
        j = self.dcnt[qn]
        if j >= self.NDS:
            deps.append(("d", qn, j - self.NDS))
        waits = self._waits(qn, deps)
        self.dcnt[qn] += 1
        ev = ("d", qn, j)
        self.q[qn].append((lambda e: e.dma_start(out=out, in_=in_, **kw), waits, ev))
        self._record(ev, reads, writes)

    def emit(self):
        nc = self.nc
        q = self.q
        needed = set()
        for name in self.ENG:
            for fn, waits, ev in q[name]:
                for w in waits:
                    if w[0] == "c":
                        needed.add(w)
        for e in ("pe", "act", "dve", "pool"):
            if self.cnt[e] > 0:
                needed.add(("c", e, self.cnt[e]))
        import os
        if os.environ.get("DENSE"):
            for e in ("pe", "act", "dve", "pool"):
                for idx in range(1, self.cnt[e] + 1):
                    needed.add(("c", e, idx))
        cum = {}
        for e in ("pe", "act", "dve", "pool"):
            c, arr = 0, [0] * (self.cnt[e] + 1)
            for idx in range(1, self.cnt[e] + 1):
                if ("c", e, idx) in needed:
                    c += 1
                arr[idx] = c
            cum[e] = arr

        def semval(w):
            if w[0] == "c":
                return self.sem[w[1]], cum[w[1]][w[2]]
            _, qn, j = w
            return self.dsem[qn][j % self.NDS], 16 * (j // self.NDS + 1)

        fin = []
        for qn in ("sp", "pool", "act"):
            for s in range(self.NDS):
                n = (self.dcnt[qn] - s + self.NDS - 1) // self.NDS
                if n > 0:
                    fin.append((self.dsem[qn][s], 16 * n))
        for e in ("pe", "act", "dve", "pool"):
            if self.cnt[e] > 0:
                fin.append((self.sem[e], cum[e][self.cnt[e]]))
        self.n_inc = {e: cum[e][-1] for e in cum}

        def run(e, name, final=()):
            for fn, waits, ev in q[name]:
                for w in waits:
                    s, v = semval(w)
                    e.wait_ge(s, v)
                ins = fn(e)
                if ev[0] == "d":
                    ins.then_inc(self.dsem[ev[1]][ev[2] % self.NDS], 16)
                elif ev in needed:
                    ins.then_inc(self.sem[ev[1]], 1)
            for s, v in final:
                e.wait_ge(s, v)

        with nc.Block() as block:
            @block.sync
            def _(e):
                run(e, "sp", fin)

            @block.tensor
            def _(e):
                run(e, "pe")

            @block.scalar
            def _(e):
                run(e, "act")

            @block.vector
            def _(e):
                run(e, "dve")

            @block.gpsimd
            def _(e):
                run(e, "pool")
        self.es.close()


D = 1024
SEQ = 4096
TOK = 512
NBLK = SEQ // TOK
NS = 16
DFF = 2816
NF = DFF // 128
INW = 6400
OFF_U, OFF_Q, OFF_K, OFF_V, OFF_QM, OFF_G = 0, 512, 1280, 2048, 2816, 3328
DILS = (1, 4, 16)
EPS = 1e-6
NEG = -30000.0
SLOPES = [2.0 ** (-8.0 * h / 12.0) for h in range(1, 13)]
TWO_PI = 2.0 * math.pi


def host_consts():
    import ml_dtypes
    c = {}
    c["ident_bf"] = np.eye(128, dtype=np.float32).astype(ml_dtypes.bfloat16)
    c["ident_f"] = np.eye(128, dtype=np.float32)
    a = np.arange(128)[:, None].astype(np.float64)
    b3_ = np.arange(32)[None, :].astype(np.float64)
    b = np.arange(128)[None, :].astype(np.float64)
    t12 = np.zeros((128, 2, 2, 4, 128), np.float32)
    for g in range(2):
        for h in range(4):
            sl = SLOPES[g * 4 + h] * DILS[g]
            dA = 128 + b - a
            t12[:, g, 0, h, :] = np.where(dA <= 128, -sl * dA, NEG)
            dB = b - a
            t12[:, g, 1, h, :] = np.where(dB >= 0, -sl * dB, NEG)
    c["tab12"] = t12.reshape(128, 16 * 128)
    bA = 128 + b - a
    bB = b - a
    c["base12"] = np.stack([np.where(bA <= 128, -bA, -1e9), np.where(bB >= 0, -bB, -1e9)], axis=1).astype(np.float32).reshape(128, 256)
    t3b = np.zeros((128, 4, 2, 32), np.float32)
    for v in range(4):
        dA = 128 + 32 * v + b3_ - a
        t3b[:, v, 0, :] = np.where(dA <= 128, -dA, -1e9)
        dB = 32 * v + b3_ - a
        t3b[:, v, 1, :] = np.where(dB >= 0, -dB, -1e9)
    c["base3"] = t3b.reshape(128, 256)
    b3 = np.arange(32)[None, :].astype(np.float64)
    t3 = np.zeros((128, 4, 2, 4, 32), np.float32)
    for v in range(4):
        for h in range(4):
            sl = SLOPES[8 + h] * 16
            dA = 128 + 32 * v + b3 - a
            t3[:, v, 0, h, :] = np.where(dA <= 128, -sl * dA, NEG)
            dB = 32 * v + b3 - a
            t3[:, v, 1, h, :] = np.where(dB >= 0, -sl * dB, NEG)
    c["tab3"] = t3.reshape(128, 32 * 32)
    ts = np.zeros((128, 12), np.float32)
    for g in range(3):
        for h in range(4):
            ts[:, g * 4 + h] = -SLOPES[g * 4 + h] * DILS[g] * (128 - np.arange(128))
    c["tabs_s"] = ts
    sel = np.zeros((16, 16, 128), np.float32)
    for bb in range(16):
        sel[bb, bb, :] = 1.0
    c["sel"] = sel.reshape(16, 16 * 128)
    c["twopi"] = np.full((128, 1), TWO_PI, np.float32)
    c["iota"] = np.tile(np.arange(514, dtype=np.float32)[None, :], (128, 1))
    return c


CONST_SHAPES = {"ident_bf": ([128, 128], BF16), "ident_f": ([128, 128], F32), "tab12": ([128, 2048], F32),
                "tab3": ([128, 1024], F32), "base12": ([128, 256], F32), "base3": ([128, 256], F32), "tabs_s": ([128, 12], F32), "sel": ([16, 2048], F32),
                "twopi": ([128, 1], F32), "iota": ([128, 514], F32)}

IN_SHAPES = {
    "xp": [SEQ, D], "memp": [256, D], "xs": [NS, D], "sre": [NS, 2048], "sim": [NS, 2048],
    "ck1": [NS, 128, 256], "cv1": [NS, 128, 256], "ck2": [NS, 512, 256], "cv2": [NS, 512, 256],
    "ck3": [NS, 2048, 256], "cv3": [NS, 2048, 256], "cmk": [NS, 256, 512], "cmv": [NS, 256, 512],
    "sconv": [NS, 2, DFF],
    "norm1_g": [1, D], "w_in": [D, INW], "a_re": [16, 128], "a_im": [16, 128], "log_dt": [16, 2],
    "b_re": [2048, 16], "b_im": [2048, 16], "c_re": [512, 64], "c_im": [512, 64], "ssm_d": [4, 128],
    "w_glu": [512, 2048], "w_atto": [256, D], "mem_g": [1, D], "w_memkv": [D, D], "w_memo": [512, D],
    "w_out": [D, D], "norm2_g": [1, D], "w_up": [D, 2 * DFF], "conv_w": [3, DFF], "conv_b": [1, DFF],
    "w_down": [DFF, D], "fin_g": [1, D],
}
OUT_SHAPES = {
    "y_p": [SEQ, D], "y_s": [NS, D], "p_sre": [16, 128], "p_sim": [16, 128],
    "p_k1": [128, 256], "p_v1": [128, 256], "p_k2": [512, 256], "p_v2": [512, 256],
    "p_k3": [2048, 256], "p_v3": [2048, 256], "p_mk": [256, 512], "p_mv": [256, 512], "p_conv": [2, DFF],
    "s_sre": [NS, 2048], "s_sim": [NS, 2048], "s_k1": [NS, 256], "s_v1": [NS, 256], "s_k2": [NS, 256],
    "s_v2": [NS, 256], "s_k3": [NS, 256], "s_v3": [NS, 256], "s_conv": [NS, 2, DFF],
}


def build_nc(stage=99, nblk=NBLK, debug=False):
    nc = bass.Bass("TRN2", target_bir_lowering=False)
    I = {k: nc.dram_tensor(k, s, F32, kind="ExternalInput").ap() for k, s in IN_SHAPES.items()}
    C = {k: nc.dram_tensor(k, s, dt, kind="ExternalInput").ap() for k, (s, dt) in CONST_SHAPES.items()}
    O = {k: nc.dram_tensor(k, s, F32, kind="ExternalOutput").ap() for k, s in OUT_SHAPES.items()}
    WB = {k: nc.dram_tensor(k + "_bf", IN_SHAPES[k], BF16, kind="Internal").ap()
          for k in ("w_in", "w_glu", "w_atto", "w_memkv", "w_memo", "w_out", "w_up", "w_down")}
    tabs_d = nc.dram_tensor("tabs_d", [16, 128, 1024], F32, kind="Internal").ap()
    S = Sched(nc)
    sb = S.sb
    dbg_n = {"i": 0}

    def dbg(name, ap, shape, keys, dt=F32):
        if not debug:
            return
        t = nc.dram_tensor("dbg_" + name, shape, dt, kind="ExternalOutput").ap()
        S.dma("sp", t, ap, reads=keys, writes=[])

    ident_bf = sb("ident_bf", [128, 128], BF16)
    ident_f = sb("ident_f", [128, 128], F32)
    ones_bf = sb("ones_bf", [128, 128], BF16)
    twopi = sb("twopi", [128, 1], F32)
    S.dma("sp", ident_bf[:], C["ident_bf"], writes=["ident_bf"])
    S.dma("sp", ident_f[:], C["ident_f"], writes=["ident_f"])
    S.dma("sp", twopi[:], C["twopi"], writes=["twopi"])
    S.op("dve", lambda e: e.memset(ones_bf[:], 1.0), writes=["ones_bf"])
    zer_bf = sb("zer_bf", [128, 64], BF16)
    S.op("dve", lambda e: e.memset(zer_bf[:], 0.0), writes=["zer_bf"])

    for k in ("w_in", "w_memkv", "w_glu", "w_atto", "w_memo", "w_out", "w_up", "w_down"):
        rows = IN_SHAPES[k][0]
        step = 256
        for r0 in range(0, rows, step):
            r1 = min(rows, r0 + step)
            S.dma("pool", WB[k][r0:r1, :], I[k][r0:r1, :], writes=[k + "_bf"])

    wbuf = [sb("wbuf%d" % i, [128, 8, 512], BF16) for i in range(2)]
    wstate = {"i": 0}
    arena = sb("arena", [128, 22, TOK], BF16)
    af = sb("af", [128, 4, TOK + 2], F32)
    xres = sb("xres", [128, 4, D], F32)
    xnb = sb("xnb", [128, D], BF16)
    gbc = sb("gbc", [128, D], F32)
    ytmp = sb("ytmp", [128, D], F32)
    hT = sb("hT", [128, 8, TOK], BF16)
    ubf = sb("ubf", [128, 4, TOK], BF16)
    qmT = sb("qmT", [128, 4, TOK], BF16)
    zT = sb("zT", [128, 4, TOK], BF16)
    kT12 = sb("kT12", [128, 2, 2, 2, TOK], BF16)
    kT3 = sb("kT3", [128, 2, SEQ], BF16)
    V1 = sb("V1", [128, 2, 4, 256], BF16)
    V2 = sb("V2", [128, 2, 4, 256], BF16)
    V3 = sb("V3", [128, 16, 2, 256], BF16)
    vst = [sb("vst%d" % i, [128, 256], BF16) for i in range(2)]
    stg = sb("stg", [128, 512], F32)
    pT = [sb("pT%d" % i, [128, 512], BF16) for i in range(2)]
    oT = sb("oT", [64, 4, TOK], BF16)
    omT = sb("omT", [128, 4, TOK], BF16)
    pTm = sb("pTm", [128, 2, TOK], BF16)
    mkT = sb("mkT", [128, 4, 256], BF16)
    mv = sb("mv", [128, 2, 512], BF16)
    base12 = sb("base12", [128, 2, 128], F32)
    base3 = sb("base3", [128, 8, 32], F32)
    small = sb("small", [128, 64], F32)
    tabt = [sb("tabt%d" % i, [128, 2, 512], F32) for i in range(2)]
    sbf = [sb("sbf%d" % i, [128, TOK], BF16) for i in range(2)]
    bbT = sb("bbT", [128, 16, 2, 128], BF16)
    cT = sb("cT", [128, 16, 2, 128], BF16)
    dD = sb("dD", [128, 4, 128], BF16)
    mag = sb("mag", [128, 16], F32)
    e512 = sb("e512", [128, 2, 16], F32)
    e511 = sb("e511", [128, 2, 16], F32)
    ab = sb("ab", [128, 2, 16], F32)
    rlast = sb("rlast", [128, 2, 16], F32)
    rinit = sb("rinit", [128, 2, 16], F32)
    carry = sb("carry", [128, NF, 2], F32)
    cwT = sb("cwT", [128, NF, 4], F32)
    S.dma("sp", base12[:], C["base12"].rearrange("p (a b) -> p a b", b=128), writes=["base12"])
    S.dma("sp", base3[:], C["base3"].rearrange("p (a b) -> p a b", b=32), writes=["base3"])

    P32 = kT3[:, :, :].bitcast(F32).rearrange("p a b -> p (a b)")
    pro_state = {"off": 0, "keys": []}

    def pro(name, shape):
        n = int(np.prod(shape[1:]))
        o = pro_state["off"]
        pro_state["off"] = o + n
        assert pro_state["off"] <= 4096, name
        pro_state["keys"].append(name)
        v = P32[:shape[0], o:o + n]
        if len(shape) == 3:
            v = v.rearrange("p (a b) -> p a b", b=shape[2])
        elif len(shape) == 4:
            v = v.rearrange("p (a b c) -> p a b c", b=shape[2], c=shape[3])
        return v

    AR = lambda j: ("ar", j)
    AFK = lambda j: ("af", j)
    evac_rr = {"i": 0}

    def evac(out_ap, ps, rows, n, reads, writes, scale=None, func=None):
        evac_rr["i"] += 1
        if func is not None or scale is not None or evac_rr["i"] % 2 == 0:
            f = func if func is not None else AF.Copy
            if scale is None:
                S.op("act", lambda e: e.activation(out_ap, ps.ap[:rows, :n], f), reads=[ps.key] + reads, writes=writes)
            else:
                S.op("act", lambda e: e.activation(out_ap, ps.ap[:rows, :n], f, scale=scale),
                     reads=[ps.key] + reads, writes=writes)
        else:
            S.op("dve", lambda e: e.tensor_copy(out_ap, ps.ap[:rows, :n]), reads=[ps.key] + reads, writes=writes)

    def wload(pieces, kc, krows):
        i = wstate["i"] % 2
        wstate["i"] += 1
        t, key = wbuf[i], "wbuf%d" % i
        for (W, wk, r0, c0, n, dc) in pieces:
            src = W[r0:r0 + kc * krows, c0:c0 + n].rearrange("(k p) c -> p k c", p=krows)
            S.dma("sp", t[:krows, :kc, dc:dc + n], src, reads=[wk], writes=[key])
        return t, key

    def mm(ps, prow, n, lhs_fn, rhs_fn, kc, reads, start=True, stop=True):
        def fn(e):
            ins = None
            for k in range(kc):
                ins = e.matmul(ps.ap[:prow, :n], lhs_fn(k), rhs_fn(k), start=(start and k == 0),
                               stop=(stop and k == kc - 1))
            return ins
        S.op("pe", fn, reads=reads, writes=[ps.key])

    def load_gbc(name):
        S.dma("sp", gbc[:], I[name].to_broadcast([128, D]), writes=["gbc"], allow_slow_non_contiguous=True)

    def rmsnorm_rows(x_ap, rows, xkeys, out_ap, outkeys):
        S.op("act", lambda e: e.activation(xnb[:rows, :], x_ap, AF.Square, accum_out=small[:rows, 0:1]),
             reads=xkeys, writes=["xnb", "small0"])
        S.op("dve", lambda e: e.tensor_scalar(small[:rows, 1:2], small[:rows, 0:1], 1.0 / D, EPS, ALU.mult, ALU.add),
             reads=["small0"], writes=["small1"])
        S.op("act", lambda e: e.activation(small[:rows, 3:4], small[:rows, 1:2], AF.Sqrt), reads=["small1"], writes=["small3"])
        S.op("dve", lambda e: e.reciprocal(small[:rows, 2:3], small[:rows, 3:4]), reads=["small3"], writes=["small2"])
        S.op("dve", lambda e: e.scalar_tensor_tensor(out_ap, x_ap, small[:rows, 2:3], gbc[:rows, :], ALU.mult, ALU.mult),
             reads=xkeys + ["small2", "gbc"], writes=outkeys)

    def transpose_to(dst_fn, src_bf, rows, nk, skeys, dkeys, dst_all=None):
        ps = S.psum()
        pv = ps.ap[:, :].bitcast(BF16)

        def fn(e):
            ins = None
            for k in range(nk):
                ins = e.transpose(pv[:, k * 128:k * 128 + rows], src_bf[:rows, k * 128:(k + 1) * 128],
                                  ident_bf[:rows, :rows])
            return ins
        S.op("pe", fn, reads=skeys + ["ident_bf"], writes=[ps.key])
        evac_rr["t"] = evac_rr.get("t", 0) + 1
        use_dve = evac_rr["t"] % 2
        if dst_all is not None and rows == 128:
            pv3 = pv[:, 0:nk * 128].rearrange("p (k r) -> p k r", r=128)
            if use_dve:
                S.op("dve", lambda e: e.tensor_copy(dst_all, pv3), reads=[ps.key], writes=dkeys)
            else:
                S.op("act", lambda e: e.activation(dst_all, pv3, AF.Copy), reads=[ps.key], writes=dkeys)
            return
        for k in range(nk):
            S.op("dve" if use_dve else "act",
                 (lambda e, k=k: e.tensor_copy(dst_fn(k), pv[:, k * 128:k * 128 + rows])) if use_dve else
                 (lambda e, k=k: e.activation(dst_fn(k), pv[:, k * 128:k * 128 + rows], AF.Copy)),
                 reads=[ps.key], writes=dkeys)

    ld16 = pro("ld16", [16, 4, 128])
    S.dma("sp", ld16[:, 0, :], I["a_re"], writes=["ld16"])
    S.dma("sp", ld16[:, 1, :], I["a_im"], writes=["ld16"])
    ldt = sb("ldt", [16, 2], F32)
    S.dma("sp", ldt[:], I["log_dt"], writes=["ldt"])
    S.op("dve", lambda e: e.tensor_copy(ld16[:, 2, :].rearrange("g (a p) -> g a p", p=64),
                                        ldt[:, :].unsqueeze(2).to_broadcast([16, 2, 64])), reads=["ldt"], writes=["ld16"])
    cst = sb("cst", [128, 12, 16], F32)
    ps = S.psum()

    def _tr3(e):
        ins = None
        for j in range(3):
            ins = e.transpose(ps.ap[:, j * 16:(j + 1) * 16], ld16[:, j, :], ident_f[:16, :16])
        return ins
    S.op("pe", _tr3, reads=["ld16", "ident_f"], writes=[ps.key])
    S.op("dve", lambda e: e.tensor_copy(cst[:, 0:3, :], ps.ap[:, 0:48].rearrange("p (a b) -> p a b", b=16)),
         reads=[ps.key], writes=["cst"])
    S.op("act", lambda e: e.activation(cst[:, 2, :], cst[:, 2, :], AF.Exp), reads=["cst"], writes=["cst"])
    S.op("dve", lambda e: e.tensor_mul(cst[:, 3, :], cst[:, 1, :], cst[:, 2, :]), reads=["cst"], writes=["cst"])
    S.op("dve", lambda e: e.tensor_mul(cst[:, 5, :], cst[:, 0, :], cst[:, 2, :]), reads=["cst"], writes=["cst"])
    S.op("dve", lambda e: e.tensor_scalar(mag[:], cst[:, 5, :], 1.0 / 12.0, 1.0, ALU.mult, ALU.add), reads=["cst"], writes=["mag"])
    for n_ in range(11, 0, -1):
        S.op("dve", lambda e: e.tensor_mul(mag[:], mag[:], cst[:, 5, :]), reads=["cst", "mag"], writes=["mag"])
        S.op("dve", lambda e, n_=n_: e.tensor_scalar(mag[:], mag[:], 1.0 / n_, 1.0, ALU.mult, ALU.add), reads=["mag"], writes=["mag"])
    MAGIC = 12582912.0

    def reduce_pi(dst, src_ap, tmp, rk, wk, shift=0.0):
        S.op("dve", lambda e: e.tensor_scalar(tmp, src_ap, shift, 1.0 / TWO_PI, ALU.add, ALU.mult), reads=rk, writes=wk)
        S.op("dve", lambda e: e.tensor_scalar_add(tmp, tmp, MAGIC), reads=wk, writes=wk)
        S.op("dve", lambda e: e.tensor_scalar(tmp, tmp, MAGIC, -TWO_PI, ALU.subtract, ALU.mult), reads=wk, writes=wk)
        S.op("dve", lambda e: e.scalar_tensor_tensor(dst, src_ap, shift, tmp, ALU.add, ALU.add), reads=rk + wk, writes=wk)
        S.op("dve", lambda e: e.tensor_scalar(dst, dst, 3.141592, -3.141592, ALU.min, ALU.max), reads=wk, writes=wk)

    reduce_pi(cst[:, 6, :], cst[:, 3, :], cst[:, 11, :], ["cst"], ["cst"])
    ang = pro("ang", [128, 514])
    iota = pro("iota", [128, 514])
    S.dma("sp", iota[:], C["iota"], writes=["iota"])
    for gp in range(16):
        S.op("dve", lambda e, gp=gp: e.tensor_scalar_mul(ang[:, 0:513], iota[:, 0:513], cst[:, 6, gp:gp + 1]),
             reads=["cst", "iota"], writes=["ang"])
        reduce_pi(af[:, 1, 0:513], ang[:, 0:513], af[:, 0, 0:513], ["ang"], [AFK(0), AFK(1)], shift=math.pi / 2)
        reduce_pi(af[:, 2, 0:513], ang[:, 0:513], af[:, 3, 0:513], ["ang"], [AFK(2), AFK(3)])
        S.op("act", lambda e: e.activation(af[:, 0, 0:513], af[:, 1, 0:513], AF.Sin), reads=[AFK(1)], writes=[AFK(0)])
        S.op("act", lambda e: e.activation(af[:, 3, 0:513], af[:, 2, 0:513], AF.Sin), reads=[AFK(2)], writes=[AFK(3)])
        S.dma("sp", tabs_d[gp, :, 0:512], af[:, 0, 0:512], reads=[AFK(0)], writes=["tabs_d"])
        S.dma("sp", tabs_d[gp, :, 512:1024], af[:, 3, 0:512], reads=[AFK(3)], writes=["tabs_d"])
        for (dst, col) in ((e512, 512), (e511, 511)):
            S.op("dve", lambda e, gp=gp, dst=dst, col=col: e.tensor_copy(dst[:, 0, gp:gp + 1], af[:, 0, col:col + 1]),
                 reads=[AFK(0)], writes=["ecst"])
            S.op("dve", lambda e, gp=gp, dst=dst, col=col: e.tensor_copy(dst[:, 1, gp:gp + 1], af[:, 3, col:col + 1]),
                 reads=[AFK(3)], writes=["ecst"])
        S.op("dve", lambda e, gp=gp: e.tensor_copy(small[:, 16 + gp:17 + gp], af[:, 0, 1:2]), reads=[AFK(0)], writes=["small16"])
        S.op("dve", lambda e, gp=gp: e.tensor_copy(small[:, 32 + gp:33 + gp], af[:, 3, 1:2]), reads=[AFK(3)], writes=["small16"])
    S.op("dve", lambda e: e.tensor_mul(ab[:, 0, :], mag[:], small[:, 16:32]), reads=["mag", "small16"], writes=["ab"])
    S.op("dve", lambda e: e.tensor_mul(ab[:, 1, :], mag[:], small[:, 32:48]), reads=["mag", "small16"], writes=["ab"])
    c_ = lambda j: cst[:, j, :]
    S.op("dve", lambda e: e.tensor_scalar_add(c_(5), ab[:, 0, :], -1.0), reads=["ab"], writes=["cst"])
    S.op("dve", lambda e: e.tensor_mul(c_(7), c_(0), c_(0)), reads=["cst"], writes=["cst"])
    S.op("dve", lambda e: e.tensor_mul(c_(8), c_(1), c_(1)), reads=["cst"], writes=["cst"])
    S.op("dve", lambda e: e.tensor_add(c_(4), c_(7), c_(8)), reads=["cst"], writes=["cst"])
    S.op("dve", lambda e: e.reciprocal(c_(4), c_(4)), reads=["cst"], writes=["cst"])
    S.op("dve", lambda e: e.tensor_mul(c_(7), c_(5), c_(0)), reads=["cst"], writes=["cst"])
    S.op("dve", lambda e: e.tensor_mul(c_(8), ab[:, 1, :], c_(1)), reads=["cst", "ab"], writes=["cst"])
    S.op("dve", lambda e: e.tensor_add(c_(7), c_(7), c_(8)), reads=["cst"], writes=["cst"])
    S.op("dve", lambda e: e.tensor_mul(c_(9), c_(7), c_(4)), reads=["cst"], writes=["cst"])
    S.op("dve", lambda e: e.tensor_mul(c_(7), ab[:, 1, :], c_(0)), reads=["cst", "ab"], writes=["cst"])
    S.op("dve", lambda e: e.tensor_mul(c_(8), c_(5), c_(1)), reads=["cst"], writes=["cst"])
    S.op("dve", lambda e: e.tensor_sub(c_(7), c_(7), c_(8)), reads=["cst"], writes=["cst"])
    S.op("dve", lambda e: e.tensor_mul(c_(10), c_(7), c_(4)), reads=["cst"], writes=["cst"])
    braw = pro("braw", [128, 2, 16, 16])
    S.dma("sp", braw[:, 0], I["b_re"].rearrange("(gp q) c -> q gp c", q=128), writes=["braw"])
    S.dma("sp", braw[:, 1], I["b_im"].rearrange("(gp q) c -> q gp c", q=128), writes=["braw"])
    bbf = pro("bbf", [128, 2, 16, 16])
    qre_b = cst[:, 9, :].unsqueeze(2).to_broadcast([128, 16, 16])
    qim_b = cst[:, 10, :].unsqueeze(2).to_broadcast([128, 16, 16])
    tmpb = pro("tmpb", [128, 16, 16])
    S.op("dve", lambda e: e.tensor_mul(bbf[:, 0], braw[:, 0], qre_b), reads=["braw", "cst"], writes=["bbf"])
    S.op("dve", lambda e: e.tensor_mul(tmpb[:], braw[:, 1], qim_b), reads=["braw", "cst"], writes=["tmpb"])
    S.op("dve", lambda e: e.tensor_sub(bbf[:, 0], bbf[:, 0], tmpb[:]), reads=["tmpb", "bbf"], writes=["bbf"])
    S.op("dve", lambda e: e.tensor_mul(bbf[:, 1], braw[:, 1], qre_b), reads=["braw", "cst"], writes=["bbf"])
    S.op("dve", lambda e: e.tensor_mul(tmpb[:], braw[:, 0], qim_b), reads=["braw", "cst", "bbf"], writes=["tmpb"])
    S.op("dve", lambda e: e.tensor_add(bbf[:, 1], bbf[:, 1], tmpb[:]), reads=["tmpb", "bbf"], writes=["bbf"])
    S.op("dve", lambda e: e.memset(bbT[:], 0.0), writes=["bbT"])
    S.op("dve", lambda e: e.memset(cT[:], 0.0), writes=["cT"])
    S.op("dve", lambda e: e.memset(dD[:], 0.0), writes=["dD"])
    bbz = pro("bbz", [128, 32])
    for gp in range(16):
        for ri in range(2):
            S.op("dve", lambda e: e.memset(bbz[:], 0.0), writes=["bbz"])
            S.op("dve", lambda e, gp=gp, ri=ri: e.tensor_copy(bbz[0:64, 0:16], bbf[0:64, ri, gp, :]), reads=["bbf"], writes=["bbz"])
            S.op("dve", lambda e, gp=gp, ri=ri: e.tensor_copy(bbz[64:128, 16:32], bbf[64:128, ri, gp, :]), reads=["bbf"], writes=["bbz"])
            ps = S.psum()
            S.op("pe", lambda e, ps=ps: e.transpose(ps.ap[:32, :128], bbz[:, :], ident_f[:, :]), reads=["bbz", "ident_f"], writes=[ps.key])
            r0 = (gp % 4) * 32
            S.op("dve", lambda e, ps=ps: e.tensor_copy(vst[0][:32, :128], ps.ap[:32, :128]), reads=[ps.key], writes=["vst0"])
            S.dma("sp", bbT[r0:r0 + 32, gp, ri, :], vst[0][:32, :128], reads=["vst0"], writes=["bbT"])
    craw = pro("craw", [128, 4, 2, 64])
    S.dma("sp", craw[:, :, 0, :], I["c_re"].rearrange("(t q) p -> q t p", q=128), writes=["craw"])
    S.dma("sp", craw[:, :, 1, :], I["c_im"].rearrange("(t q) p -> q t p", q=128), writes=["craw"])
    for gp in range(16):
        t, r0 = gp // 4, (gp % 4) * 32
        for ri in range(2):
            ps = S.psum()
            S.op("pe", lambda e, ps=ps, t=t, ri=ri: e.transpose(ps.ap[:64, :128], craw[:, t, ri, :], ident_f[:, :]),
                 reads=["craw", "ident_f"], writes=[ps.key])
            sc = 1.0 if ri == 0 else -1.0
            S.op("act", lambda e, ps=ps, gp=gp, ri=ri, r0=r0, sc=sc: e.activation(cT[0:64, gp, ri, r0:r0 + 16], ps.ap[0:64, r0:r0 + 16], AF.Copy, scale=sc),
                 reads=[ps.key], writes=["cT"])
            S.op("act", lambda e, ps=ps, r0=r0, sc=sc: e.activation(stg[0:64, 0:16], ps.ap[0:64, r0 + 16:r0 + 32], AF.Copy, scale=sc),
                 reads=[ps.key], writes=["stg"])
            S.op("dve", lambda e: e.tensor_copy(vst[0][0:64, 0:16], stg[0:64, 0:16]), reads=["stg"], writes=["vst0"])
            S.dma("sp", cT[64:128, gp, ri, r0 + 16:r0 + 32], vst[0][0:64, 0:16], reads=["vst0"], writes=["cT"])
    dcol = sb("dcol", [128, 4], F32)
    drow = pro("drow", [4, 128])
    S.dma("sp", drow[:], I["ssm_d"], writes=["drow"])
    ps = S.psum()
    S.op("pe", lambda e, ps=ps: e.transpose(ps.ap[:, 0:4], drow[:, :], ident_f[:4, :4]), reads=["drow", "ident_f"], writes=[ps.key])
    S.op("dve", lambda e, ps=ps: e.tensor_copy(dcol[:], ps.ap[:, 0:4]), reads=[ps.key], writes=["dcol"])
    for t in range(4):
        S.op("dve", lambda e, t=t: e.tensor_scalar_mul(dD[:, t, :], ident_f[:, :], dcol[:, t:t + 1]), reads=["ident_f", "dcol"], writes=["dD"])
    cwrow = pro("cwrow", [NF, 4, 128])
    for j in range(3):
        S.dma("sp", cwrow[:, j, :], I["conv_w"][j:j + 1, :].rearrange("j (f q) -> (j f) q", q=128), writes=["cwrow"])
    S.dma("sp", cwrow[:, 3, :], I["conv_b"].rearrange("j (f q) -> (j f) q", q=128), writes=["cwrow"])
    ps = S.psum()

    def _trc(e, ps=ps):
        ins = None
        for j in range(4):
            ins = e.transpose(ps.ap[:, j * NF:(j + 1) * NF], cwrow[:, j, :], ident_f[:NF, :NF])
        return ins
    S.op("pe", _trc, reads=["cwrow", "ident_f"], writes=[ps.key])
    S.op("dve", lambda e, ps=ps: e.tensor_copy(cwT[:, :, :], ps.ap[:, 0:4 * NF].rearrange("p (j f) -> p f j", f=NF)),
         reads=[ps.key], writes=["cwT"])
    S.op("dve", lambda e: e.memset(carry[:], 0.0), writes=["carry"])
    S.op("dve", lambda e: e.memset(rinit[:], 0.0), writes=["rinit"])

    S.op("dve", lambda e: e.memset(small[:, 63:64], 0.0), writes=pro_state["keys"] + ["kT3"])

    load_gbc("mem_g")
    for mt in range(2):
        S.dma("sp", xres[:, mt, :], I["memp"][mt * 128:(mt + 1) * 128, :], writes=[("x", mt)])
        rmsnorm_rows(xres[:, mt, :], 128, [("x", mt)], xnb[:, :], ["xnb"])
        transpose_to(lambda k, mt=mt: hT[:, k, mt * 128:(mt + 1) * 128], xnb, 128, 8, ["xnb"], ["hT"])
    for cg, oname in ((0, "p_mk"), (1, "p_mv")):
        wt, wk = wload([(WB["w_memkv"], "w_memkv_bf", 0, cg * 512, 512, 0)], 8, 128)
        for mt in range(2):
            ps = S.psum()
            mm(ps, 128, 512, lambda k, mt=mt: hT[:, k, mt * 128:(mt + 1) * 128], lambda k, wt=wt: wt[:, k, :], 8, [wk, "hT"])
            evac(stg[:, :], ps, 128, 512, [], ["stg"])
            S.dma("sp", O[oname][mt * 128:(mt + 1) * 128, :], stg[:, :], reads=["stg"], writes=[])
            if cg == 1:
                S.op("dve", lambda e, mt=mt: e.tensor_copy(mv[:, mt, :], stg[:, :]), reads=["stg"], writes=["mv"])
        if cg == 0:
            for h in range(4):
                ps = S.psum()
                mm(ps, 128, 256, lambda k, wt=wt, h=h: wt[:, k, h * 128:(h + 1) * 128], lambda k: hT[:, k, 0:256], 8, [wk, "hT"])
                evac(mkT[:, h, :], ps, 128, 256, [], ["mkT"])

    def window_outputs(blk):
        T0 = blk * TOK
        for g, win in enumerate((128, 512, 2048)):
            tts = [tt for tt in range(4) if T0 + tt * 128 >= SEQ - win]
            if not tts:
                continue
            wt, wk = wload([(WB["w_in"], "w_in_bf", 0, OFF_K + g * 256, 256, 0),
                            (WB["w_in"], "w_in_bf", 0, OFF_V + g * 256, 256, 256)], 8, 128)
            for tt in tts:
                ps = S.psum()
                mm(ps, 128, 512, lambda k, tt=tt: hT[:, k, tt * 128:(tt + 1) * 128], lambda k, wt=wt: wt[:, k, :], 8, [wk, "hT"])
                evac(stg[:, :], ps, 128, 512, [], ["stg"])
                r0 = T0 + tt * 128 - (SEQ - win)
                S.dma("sp", O["p_k%d" % (g + 1)][r0:r0 + 128, :], stg[:, 0:256], reads=["stg"], writes=[])
                S.dma("sp", O["p_v%d" % (g + 1)][r0:r0 + 128, :], stg[:, 256:512], reads=["stg"], writes=[])

    YB = [S.banks[6], S.banks[7]]
    S.nrot = 6
    SC_MEM = 128.0 ** -0.5

    def dense_fm(Wk, col0, ncols, kc, krows, rhs_fn, rkeys, n, cb):
        for c0 in range(0, ncols, 512):
            w = min(512, ncols - c0)
            wt, wk = wload([(WB[Wk], Wk + "_bf", 0, col0 + c0, w, 0)], kc, krows)
            for mi in range(w // 128):
                ps = S.psum()
                mm(ps, 128, n, lambda k, wt=wt, mi=mi: wt[:krows, k, mi * 128:(mi + 1) * 128], rhs_fn, kc, [wk] + rkeys)
                cb((c0 // 128) + mi, ps)

    def gates_to_arena(b, rhs_fn, rkeys, n):
        def cb(m, ps):
            S.op("act", lambda e: e.activation(arena[:, m, :n], ps.ap[:, :n], AF.Sigmoid), reads=[ps.key], writes=[AR(m)])
        dense_fm("w_in", OFF_G + b * D, D, 8, 128, rhs_fn, rkeys, n, cb)

    def merge_cb(first, n):
        def cb(m, ps):
            if first:
                S.op("dve", lambda e: e.tensor_tensor(arena[:, 8 + m, :n], ps.ap[:, :n], arena[:, m, :n], ALU.mult),
                     reads=[ps.key, AR(m)], writes=[AR(8 + m)])
            else:
                S.op("dve", lambda e: e.tensor_tensor(stg[:, :n], ps.ap[:, :n], arena[:, m, :n], ALU.mult),
                     reads=[ps.key, AR(m)], writes=["stg"])
                S.op("dve", lambda e: e.tensor_tensor(arena[:, 8 + m, :n], arena[:, 8 + m, :n], stg[:, :n], ALU.add),
                     reads=["stg", AR(8 + m)], writes=[AR(8 + m)])
        return cb

    def glu_branch(zt, n):
        for half in range(2):
            wA, kA = wload([(WB["w_glu"], "w_glu_bf", 0, half * 512, 512, 0)], 4, 128)
            wB, kB = wload([(WB["w_glu"], "w_glu_bf", 0, D + half * 512, 512, 0)], 4, 128)
            for mi in range(4):
                m = half * 4 + mi
                pa, pb_ = S.psum(), S.psum()
                mm(pa, 128, n, lambda k, mi=mi, wA=wA: wA[:, k, mi * 128:(mi + 1) * 128], lambda k: zt[:, k, :n], 4, [kA, "zT"])
                mm(pb_, 128, n, lambda k, mi=mi, wB=wB: wB[:, k, mi * 128:(mi + 1) * 128], lambda k: zt[:, k, :n], 4, [kB, "zT"])
                S.op("act", lambda e, pb_=pb_: e.activation(stg[:, :n], pb_.ap[:, :n], AF.Sigmoid), reads=[pb_.key], writes=["stg"])
                S.op("dve", lambda e, pa=pa: e.tensor_tensor(stg[:, :n], pa.ap[:, :n], stg[:, :n], ALU.mult), reads=[pa.key, "stg"], writes=["stg"])
                S.op("dve", lambda e, m=m: e.tensor_tensor(arena[:, 8 + m, :n], stg[:, :n], arena[:, m, :n], ALU.mult),
                     reads=["stg", AR(m)], writes=[AR(8 + m)])

    def out_proj_tm(Wk, kc_total, lhs_fn, lkeys, rows):
        pieces = [(k0, min(8, kc_total - k0)) for k0 in range(0, kc_total, 8)]
        ntt = 4 if rows == 128 else 1
        for cg in range(2):
            banks = [S.psum() for _ in range(ntt)]
            for pi, (k0, kc) in enumerate(pieces):
                wt, wk = wload([(WB[Wk], Wk + "_bf", k0 * 128, cg * 512, 512, 0)], kc, 128)
                for tt in range(ntt):
                    mm(banks[tt], rows, 512, lambda k, tt=tt, k0=k0: lhs_fn(tt, k0 + k), lambda k, wt=wt: wt[:, k, :], kc,
                       [wk] + lkeys, start=(pi == 0), stop=(pi == len(pieces) - 1))
            for tt in range(ntt):
                S.op("dve", lambda e, tt=tt, cg=cg, b=banks[tt]: e.tensor_tensor(
                    xres[:rows, tt, cg * 512:(cg + 1) * 512], xres[:rows, tt, cg * 512:(cg + 1) * 512], b.ap[:rows, :512], ALU.add),
                    reads=[banks[tt].key, ("x", tt)], writes=[("x", tt)])

    def rms_stats(rows, ntt):
        for tt in range(ntt):
            S.op("act", lambda e, tt=tt: e.activation(xnb[:rows, :], xres[:rows, tt, :], AF.Square, accum_out=small[:rows, 4 + tt:5 + tt]),
                 reads=[("x", tt)], writes=["xnb", "smallS"])
        S.op("dve", lambda e: e.tensor_scalar(small[:rows, 8:8 + ntt], small[:rows, 4:4 + ntt], 1.0 / D, EPS, ALU.mult, ALU.add),
             reads=["smallS"], writes=["smallM"])
        S.op("act", lambda e: e.activation(small[:rows, 12:12 + ntt], small[:rows, 8:8 + ntt], AF.Sqrt), reads=["smallM"], writes=["smallQ"])
        S.op("dve", lambda e: e.reciprocal(small[:rows, 24:24 + ntt], small[:rows, 12:12 + ntt]), reads=["smallQ"], writes=["smallR"])

    def norm_to_hT(gname, ntt, rows, dst):
        load_gbc(gname)
        rms_stats(rows, ntt)
        for tt in range(ntt):
            S.op("dve", lambda e, tt=tt: e.scalar_tensor_tensor(xnb[:rows, :], xres[:rows, tt, :], small[:rows, 24 + tt:25 + tt], gbc[:rows, :],
                                                                ALU.mult, ALU.mult), reads=[("x", tt), "smallR", "gbc"], writes=["xnb"])
            transpose_to(lambda k, tt=tt: dst[:, k, tt * rows:(tt + 1) * rows], xnb, rows, 8, ["xnb"], ["hT"],
                         dst_all=dst[:, :, tt * rows:(tt + 1) * rows])

    def ffn(n, a_hook):
        for s in range(NF // 2):
            wt, wk = wload([(WB["w_up"], "w_up_bf", 0, 256 * s, 256, 0), (WB["w_up"], "w_up_bf", 0, DFF + 256 * s, 256, 256)], 8, 128)
            for jj in range(2):
                f = 2 * s + jj
                pa, pv = S.psum(), S.psum()
                mm(pa, 128, n, lambda k, wt=wt, jj=jj: wt[:, k, jj * 128:(jj + 1) * 128], lambda k: hT[:, k, :n], 8, [wk, "hT"])
                mm(pv, 128, n, lambda k, wt=wt, jj=jj: wt[:, k, 256 + jj * 128:256 + (jj + 1) * 128], lambda k: hT[:, k, :n], 8, [wk, "hT"])
                a_hook(f, pa)
                S.op("act", lambda e: e.activation(af[:, 2, :n], af[:, 1, :n], AF.Gelu_apprx_tanh), reads=[AFK(1)], writes=[AFK(2)])
                S.op("dve", lambda e, f=f, pv=pv: e.tensor_tensor(arena[:, f, :n], af[:, 2, :n], pv.ap[:, :n], ALU.mult),
                     reads=[AFK(2), pv.key], writes=[AR(f)])

    def final_norm_out(rows, ntt, out_fn):
        load_gbc("fin_g")
        rms_stats(rows, ntt)
        for tt in range(ntt):
            S.op("dve", lambda e, tt=tt: e.scalar_tensor_tensor(ytmp[:rows, :], xres[:rows, tt, :], small[:rows, 24 + tt:25 + tt], gbc[:rows, :],
                                                                ALU.mult, ALU.mult), reads=[("x", tt), "smallR", "gbc"], writes=["ytmp"])
            S.dma("sp", out_fn(tt), ytmp[:rows, :], reads=["ytmp"], writes=[])

    for blk in range(nblk):
        T0, par = blk * TOK, blk % 2
        for tt in range(4):
            S.dma("sp", xres[:, tt, :], I["xp"][T0 + tt * 128:T0 + (tt + 1) * 128, :], writes=[("x", tt)])
        norm_to_hT("norm1_g", 4, 128, hT)
        hrhs = lambda k: hT[:, k, :]

        def inproj_cb(base_tile):
            def cb(m, ps):
                f = base_tile + m
                if f < 4:
                    evac(ubf[:, f, :], ps, 128, TOK, [], ["ubf"])
                elif f < 10:
                    evac(arena[:, 16 + f - 4, :], ps, 128, TOK, [], [AR(16 + f - 4)], scale=0.125)
                elif f < 16:
                    kt = f - 10
                    g, pair = kt // 2, kt % 2
                    if g < 2:
                        evac(kT12[:, g, pair, par, :], ps, 128, TOK, [], ["kT12"])
                    else:
                        evac(kT3[:, pair, T0:T0 + TOK], ps, 128, TOK, [], ["kT3"])
                else:
                    evac(qmT[:, f - 22, :], ps, 128, TOK, [], ["qmT"])
            return cb
        if stage < 1:
            window_outputs(blk)
            continue
        dense_fm("w_in", 0, 2048, 8, 128, hrhs, ["hT"], TOK, inproj_cb(0))
        dense_fm("w_in", OFF_QM, 512, 8, 128, hrhs, ["hT"], TOK, inproj_cb(22))
        if blk == 0:
            dbg("hT", hT[:, :, :], [128, 8, TOK], ["hT"], BF16)
            dbg("ubf", ubf[:, :, :], [128, 4, TOK], ["ubf"], BF16)
            dbg("qT", arena[:, 16:22, :], [128, 6, TOK], [AR(16 + i) for i in range(6)], BF16)
            dbg("qmT", qmT[:, :, :], [128, 4, TOK], ["qmT"], BF16)

        if stage < 1.5:
            window_outputs(blk)
            continue
        for g in range(3 if stage >= 1.7 else (2 if stage >= 1.6 else 1)):
            wt, wk = wload([(WB["w_in"], "w_in_bf", 0, OFF_V + g * 256, 256, 0)], 8, 128)
            if g < 2:
                for c in range(4):
                    ps = S.psum()
                    if g == 0:
                        lf = lambda k, c=c: hT[:, k, c * 128:(c + 1) * 128]
                    else:
                        lf = lambda k, c=c: hT[:, k, :].rearrange("p (i r) -> p r i", r=4)[:, c, :]
                    mm(ps, 128, 256, lf, lambda k, wt=wt: wt[:, k, 0:256], 8, [wk, "hT"])
                    evac((V1 if g == 0 else V2)[:, par, c, :], ps, 128, 256, [], ["V1" if g == 0 else "V2"])
            else:
                off, kt = (32 * blk) % 128, blk // 4
                for c2 in range(8):
                    ps = S.psum()

                    def vfn(e, ps=ps, c2=c2, wt=wt):
                        ins = None
                        for rr in range(2):
                            for k in range(8):
                                ins = e.matmul(ps.ap[:32, rr * 256:(rr + 1) * 256],
                                               hT[:, k, :].rearrange("p (i r) -> p r i", r=16)[:, 2 * c2 + rr, :],
                                               wt[:, k, 0:256], start=(k == 0), stop=(k == 7))
                        return ins
                    S.op("pe", vfn, reads=[wk, "hT"], writes=[ps.key])
                    for rr in range(2):
                        vb, vk = vst[rr], "vst%d" % rr
                        S.op("dve", lambda e, ps=ps, vb=vb, rr=rr: e.tensor_copy(vb[:32, :], ps.ap[:32, rr * 256:(rr + 1) * 256]),
                             reads=[ps.key], writes=[vk])
                        if stage != 1.75:
                            S.dma("sp", V3[off:off + 32, 2 * c2 + rr, kt, :], vb[:32, :], reads=[vk], writes=["V3"])
        window_outputs(blk)
        if stage < 2:
            continue

        gates_to_arena(0, hrhs, ["hT"], TOK)
        for gp in range(16):
            t = gp // 4
            tb = tabt[gp % 2]
            tk = "tabt%d" % (gp % 2)
            S.dma("sp", tb[:, :, :], tabs_d[gp].rearrange("p (a b) -> p a b", b=512), reads=["tabs_d"], writes=[tk])
            cos, sin = tb[:, 0, :], tb[:, 1, :]
            pr, pi_ = S.psum(), S.psum()
            mm(pr, 128, TOK, lambda k, gp=gp: bbT[:, gp, 0, :], lambda k, t=t: ubf[:, t, :], 1, ["bbT", "ubf"])
            mm(pi_, 128, TOK, lambda k, gp=gp: bbT[:, gp, 1, :], lambda k, t=t: ubf[:, t, :], 1, ["bbT", "ubf"])
            T = [af[:, j, 0:TOK] for j in range(4)]
            tt_ = lambda o, a, b_, op, rk, wk_, eng="dve": S.op(eng, lambda e: e.tensor_tensor(o, a, b_, op), reads=rk, writes=wk_)
            tt_(T[0], pr.ap[:, :TOK], cos, ALU.mult, [pr.key, tk], [AFK(0)])
            tt_(T[1], pi_.ap[:, :TOK], sin, ALU.mult, [pi_.key, tk], [AFK(1)])
            tt_(T[0], T[0], T[1], ALU.add, [AFK(0), AFK(1)], [AFK(0)])
            tt_(T[1], pi_.ap[:, :TOK], cos, ALU.mult, [pi_.key, tk, AFK(0)], [AFK(1)])
            tt_(T[2], pr.ap[:, :TOK], sin, ALU.mult, [pr.key, tk], [AFK(2)])
            tt_(T[1], T[1], T[2], ALU.subtract, [AFK(1), AFK(2)], [AFK(1)])
            magb = mag[:, gp:gp + 1].to_broadcast([128, TOK])
            S.op("dve", lambda e, gp=gp, magb=magb: e.tensor_tensor_scan(T[2], magb, T[0], rinit[:, 0, gp:gp + 1], ALU.mult, ALU.add),
                 reads=["mag", AFK(0), "rinit", AFK(1)], writes=[AFK(2)])
            S.op("dve", lambda e, gp=gp, magb=magb: e.tensor_tensor_scan(T[3], magb, T[1], rinit[:, 1, gp:gp + 1], ALU.mult, ALU.add),
                 reads=["mag", AFK(1), "rinit"], writes=[AFK(3)])
            S.op("act", lambda e, gp=gp: e.activation(rlast[:, 0, gp:gp + 1], af[:, 2, TOK - 1:TOK], AF.Copy), reads=[AFK(2)], writes=["rlast"])
            S.op("act", lambda e, gp=gp: e.activation(rlast[:, 1, gp:gp + 1], af[:, 3, TOK - 1:TOK], AF.Copy), reads=[AFK(3)], writes=["rlast"])
            s0, s1 = sbf[0], sbf[1]
            tt_(T[0], T[2], cos, ALU.mult, [AFK(2), tk], [AFK(0)])
            tt_(T[1], T[3], sin, ALU.mult, [AFK(3), tk], [AFK(1)])
            tt_(s0[:, :], T[0], T[1], ALU.subtract, [AFK(0), AFK(1)], ["sbf0"])
            tt_(T[0], T[2], sin, ALU.mult, [AFK(2), tk, "sbf0"], [AFK(0)])
            tt_(T[1], T[3], cos, ALU.mult, [AFK(3), tk, "sbf0"], [AFK(1)])
            tt_(s1[:, :], T[0], T[1], ALU.add, [AFK(0), AFK(1)], ["sbf1"])
            yb = YB[t % 2]

            def yfn(e, gp=gp, t=t, yb=yb):
                e.matmul(yb.ap[:, :TOK], cT[:, gp, 0, :], sbf[0][:, :], start=(gp % 4 == 0), stop=False)
                ins = e.matmul(yb.ap[:, :TOK], cT[:, gp, 1, :], sbf[1][:, :], start=False, stop=False)
                if gp % 4 == 3:
                    ins = e.matmul(yb.ap[:, :TOK], dD[:, t, :], ubf[:, t, :], start=False, stop=True)
                return ins
            S.op("pe", yfn, reads=["cT", "sbf0", "sbf1", "dD", "ubf"], writes=[yb.key])
            if gp % 4 == 3:
                S.op("act", lambda e, t=t, yb=yb: e.activation(zT[:, t, :], yb.ap[:, :TOK], AF.Gelu_apprx_tanh), reads=[yb.key], writes=["zT"])
        def cmul(dst, E, rk, wk_):
            S.op("dve", lambda e: e.tensor_mul(cst[:, 7, :], E[:, 0, :], rlast[:, 0, :]), reads=rk + ["rlast"], writes=["cst"])
            S.op("dve", lambda e: e.tensor_mul(cst[:, 8, :], E[:, 1, :], rlast[:, 1, :]), reads=rk + ["rlast"], writes=["cst"])
            S.op("dve", lambda e: e.tensor_sub(dst[:, 0, :], cst[:, 7, :], cst[:, 8, :]), reads=["cst"], writes=wk_)
            S.op("dve", lambda e: e.tensor_mul(cst[:, 7, :], E[:, 1, :], rlast[:, 0, :]), reads=rk + ["rlast"] + wk_, writes=["cst"])
            S.op("dve", lambda e: e.tensor_mul(cst[:, 8, :], E[:, 0, :], rlast[:, 1, :]), reads=rk + ["rlast"], writes=["cst"])
            S.op("dve", lambda e: e.tensor_add(dst[:, 1, :], cst[:, 7, :], cst[:, 8, :]), reads=["cst"], writes=wk_)
        if blk < NBLK - 1:
            cmul(rinit, e512, ["ecst"], ["rinit"])
        else:
            cmul(rinit, e511, ["ecst"], ["rinit"])
            for ri, oname in ((0, "p_sre"), (1, "p_sim")):
                ps = S.psum()
                S.op("pe", lambda e, ps=ps, ri=ri: e.transpose(ps.ap[:16, :128], rinit[:, ri, :], ident_f[:, :]),
                     reads=["rinit", "ident_f"], writes=[ps.key])
                S.op("dve", lambda e, ps=ps: e.tensor_copy(stg[:16, :128], ps.ap[:16, :128]), reads=[ps.key], writes=["stg"])
                S.dma("sp", O[oname], stg[:16, :128], reads=["stg"], writes=[])

        if stage < 3:
            continue
        glu_branch(zT, TOK)
        if blk == 0:
            dbg("zT", zT[:, :, :], [128, 4, TOK], ["zT"], BF16)
            dbg("m1", arena[:, 8:16, :], [128, 8, TOK], [AR(8 + i) for i in range(8)], BF16)

        if stage < 4:
            continue
        accn, accd = af[:64, 0, 0:TOK], af[:64, 1, 0:TOK]
        for j in range(4):
            pair, pb = j // 2, (j % 2) * 64
            for g in range(3):
                dil = DILS[g]
                sig = SLOPES[g * 4 + j] * dil
                if g < 2:
                    ncg, Q = 4, 128
                else:
                    ncg, Q = 16, 32
                kt3, v3 = blk // 4, blk % 4
                qfull = arena[pb:pb + 64, 16 + 2 * g + pair, :]
                qsl = (lambda c, qfull=qfull: qfull[:, c * 128:(c + 1) * 128]) if g == 0 else \
                      (lambda c, qfull=qfull, dil=dil: qfull.rearrange("p (i r) -> p r i", r=dil)[:, c, :])
                def kcur(c, g=g, pair=pair, pb=pb):
                    if g == 0:
                        return kT12[pb:pb + 64, 0, pair, par, c * 128:(c + 1) * 128]
                    if g == 1:
                        return kT12[pb:pb + 64, 1, pair, par, :].rearrange("p (i r) -> p r i", r=4)[:, c, :]
                    return kT3[pb:pb + 64, pair, kt3 * 2048:(kt3 + 1) * 2048].rearrange("p (i r) -> p r i", r=16)[:, c, 0:32 * v3 + 32]

                def kprev(c, g=g, pair=pair, pb=pb):
                    if g == 0:
                        if c > 0:
                            return kT12[pb:pb + 64, 0, pair, par, (c - 1) * 128:c * 128]
                        return kT12[pb:pb + 64, 0, pair, 1 - par, 384:512] if blk > 0 else None
                    if g == 1:
                        return kT12[pb:pb + 64, 1, pair, 1 - par, :].rearrange("p (i r) -> p r i", r=4)[:, c, :] if blk > 0 else None
                    if kt3 == 0:
                        return None
                    return kT3[pb:pb + 64, pair, (kt3 - 1) * 2048:kt3 * 2048].rearrange("p (i r) -> p r i", r=16)[:, c, :]

                def vcur(c, g=g, j=j):
                    if g == 0:
                        return V1[:, par, c, j * 64:(j + 1) * 64]
                    if g == 1:
                        return V2[:, par, c, j * 64:(j + 1) * 64]
                    return V3[0:32 * v3 + 32, c, kt3, j * 64:(j + 1) * 64]

                def vprev(c, g=g, j=j):
                    if g == 0:
                        return V1[:, par, c - 1, j * 64:(j + 1) * 64] if c > 0 else V1[:, 1 - par, 3, j * 64:(j + 1) * 64]
                    if g == 1:
                        return V2[:, 1 - par, c, j * 64:(j + 1) * 64]
                    return V3[:, c, kt3 - 1, j * 64:(j + 1) * 64]
                rows_cur = 128 if g < 2 else 32 * v3 + 32
                if g < 2:
                    baseA, baseB = base12[:, 0, :], base12[:rows_cur, 1, :]
                else:
                    baseA, baseB = base3[:, 2 * v3, :], base3[:rows_cur, 2 * v3 + 1, :]
                chunks = []
                for (rows, kf, vf, base, nm) in ((128, kprev, vprev, baseA, 0), (rows_cur, kcur, vcur, baseB, 1)):
                    cols = [c for c in range(ncg) if kf(c) is not None]
                    if not cols:
                        continue
                    ps = S.psum()
                    kaps = {c: kf(c) for c in cols}
                    qaps = {c: qsl(c) for c in cols}
                    vaps = {c: vf(c) for c in cols}

                    def qk(e, ps=ps, cols=cols, kaps=kaps, qaps=qaps, rows=rows, Q=Q):
                        ins = None
                        for c in cols:
                            ins = e.matmul(ps.ap[:rows, c * Q:(c + 1) * Q], kaps[c], qaps[c], start=True, stop=True)
                        return ins
                    S.op("pe", qk, reads=["kT12", "kT3", AR(16 + 2 * g + pair)], writes=[ps.key])
                    stmp = af[:rows, 2 + nm, 0:TOK]
                    S.op("dve", lambda e, ps=ps, rows=rows, base=base, stmp=stmp, sig=sig, ncg=ncg, Q=Q: e.scalar_tensor_tensor(
                        stmp.rearrange("p (c b) -> p c b", b=Q), base.unsqueeze(1).to_broadcast([rows, ncg, Q]), sig,
                        ps.ap[:rows, :TOK].rearrange("p (c b) -> p c b", b=Q), ALU.mult, ALU.add),
                        reads=[ps.key, "base12", "base3"], writes=[AFK(2 + nm)])
                    S.op("act", lambda e, rows=rows, stmp=stmp, nm=nm: e.activation(pT[nm][:rows, :], stmp, AF.Exp),
                         reads=[AFK(2 + nm)], writes=["pT%d" % nm])
                    chunks.append((rows, vaps, nm, cols))
                pn, pd = S.psum(), S.psum()

                def pv(e, pn=pn, chunks=chunks, ncg=ncg, Q=Q, ones=False):
                    ins = None
                    for c in range(ncg):
                        cs = [ch for ch in chunks if c in ch[3]]
                        for ci, (rows, vf, nm, cols) in enumerate(cs):
                            lhs = ones_bf[:rows, 0:64] if ones else vf[c]
                            ins = e.matmul(pn.ap[:64, c * Q:(c + 1) * Q], lhs, pT[nm][:rows, c * Q:(c + 1) * Q],
                                           start=(ci == 0), stop=(ci == len(cs) - 1))
                    return ins
                S.op("pe", pv, reads=["V1", "V2", "V3", "pT0", "pT1"], writes=[pn.key])
                S.op("pe", lambda e, pd=pd, pv=pv: pv(e, pn=pd, ones=True), reads=["ones_bf", "pT0", "pT1"], writes=[pd.key])
                for (acc, pz, ak) in ((accn, pn, AFK(0)), (accd, pd, AFK(1))):
                    if g == 0:
                        av = acc.rearrange("p (c b) -> p c b", b=128)
                    else:
                        av = acc.rearrange("p (i r) -> p r i", r=dil)
                    pzv = pz.ap[:64, :TOK].rearrange("p (c b) -> p c b", b=Q)
                    if g == 0:
                        S.op("act", lambda e, av=av, pzv=pzv: e.activation(av, pzv, AF.Copy), reads=[pz.key], writes=[ak])
                    else:
                        S.op("dve", lambda e, av=av, pzv=pzv: e.tensor_tensor(av, av, pzv, ALU.add), reads=[pz.key, ak], writes=[ak])
            S.op("dve", lambda e: e.reciprocal(accd, accd), reads=[AFK(1)], writes=[AFK(1)])
            S.op("dve", lambda e, j=j: e.tensor_tensor(oT[:, j, :], accn, accd, ALU.mult), reads=[AFK(0), AFK(1)], writes=["oT"])
        gates_to_arena(1, hrhs, ["hT"], TOK)
        dense_fm("w_atto", 0, D, 4, 64, lambda k: oT[:, k, :], ["oT"], TOK, merge_cb(False, TOK))
        if blk == 0:
            dbg("oT", oT[:, :, :], [64, 4, TOK], ["oT"], BF16)
            dbg("m2", arena[:, 8:16, :], [128, 8, TOK], [AR(8 + i) for i in range(8)], BF16)

        if stage < 5:
            continue
        for h in range(4):
            for c in range(2):
                ps = S.psum()
                mm(ps, 128, TOK, lambda k, h=h, c=c: mkT[:, h, c * 128:(c + 1) * 128], lambda k, h=h: qmT[:, h, :], 1, ["mkT", "qmT"])
                S.op("act", lambda e, ps=ps, c=c: e.activation(pTm[:, c, :], ps.ap[:, :TOK], AF.Exp, scale=SC_MEM), reads=[ps.key], writes=["pTm"])
            pn, pd = S.psum(), S.psum()
            mm(pn, 128, TOK, lambda k, h=h: mv[:, k, h * 128:(h + 1) * 128], lambda k: pTm[:, k, :], 2, ["mv", "pTm"])
            mm(pd, 128, TOK, lambda k: ones_bf[:, :], lambda k: pTm[:, k, :], 2, ["ones_bf", "pTm"])
            S.op("dve", lambda e, pd=pd: e.reciprocal(af[:, 0, 0:TOK], pd.ap[:, :TOK]), reads=[pd.key], writes=[AFK(0)])
            S.op("dve", lambda e, pn=pn, h=h: e.tensor_tensor(omT[:, h, :], pn.ap[:, :TOK], af[:, 0, 0:TOK], ALU.mult),
                 reads=[pn.key, AFK(0)], writes=["omT"])
        gates_to_arena(2, hrhs, ["hT"], TOK)
        dense_fm("w_memo", 0, D, 4, 128, lambda k: omT[:, k, :], ["omT"], TOK, merge_cb(False, TOK))
        if blk == 0:
            dbg("omT", omT[:, :, :], [128, 4, TOK], ["omT"], BF16)
            dbg("m3", arena[:, 8:16, :], [128, 8, TOK], [AR(8 + i) for i in range(8)], BF16)

        if stage < 6:
            continue
        out_proj_tm("w_out", 8, lambda tt, k: arena[:, 8 + k, tt * 128:(tt + 1) * 128], [AR(8 + k) for k in range(8)], 128)
        if blk == 0:
            dbg("xmid", xres[:, :, :], [128, 4, D], [("x", i) for i in range(4)])
        norm_to_hT("norm2_g", 4, 128, hT)

        def conv_hook(f, pa):
            at = af[:, 0, :]
            S.op("act", lambda e: e.activation(at[:, 2:TOK + 2], pa.ap[:, :TOK], AF.Copy), reads=[pa.key], writes=[AFK(0)])
            S.op("dve", lambda e, f=f: e.tensor_copy(at[:, 0:2], carry[:, f, :]), reads=["carry"], writes=[AFK(0)])
            S.op("dve", lambda e, f=f: e.tensor_copy(carry[:, f, :], at[:, TOK:TOK + 2]), reads=[AFK(0)], writes=["carry"])
            cc = af[:, 1, 0:TOK]
            S.op("dve", lambda e, f=f: e.tensor_scalar(cc, at[:, 2:TOK + 2], cwT[:, f, 2:3], cwT[:, f, 3:4], ALU.mult, ALU.add),
                 reads=[AFK(0), "cwT"], writes=[AFK(1)])
            S.op("dve", lambda e, f=f: e.scalar_tensor_tensor(cc, at[:, 1:TOK + 1], cwT[:, f, 1:2], cc, ALU.mult, ALU.add),
                 reads=[AFK(0), AFK(1), "cwT"], writes=[AFK(1)])
            S.op("dve", lambda e, f=f: e.scalar_tensor_tensor(cc, at[:, 0:TOK], cwT[:, f, 0:1], cc, ALU.mult, ALU.add),
                 reads=[AFK(0), AFK(1), "cwT"], writes=[AFK(1)])
        ffn(TOK, conv_hook)
        if blk == 0:
            dbg("gT", arena[:, :, :], [128, NF, TOK], [AR(i) for i in range(NF)], BF16)
        out_proj_tm("w_down", NF, lambda tt, k: arena[:, k, tt * 128:(tt + 1) * 128], [AR(k) for k in range(NF)], 128)
        if blk == 0:
            dbg("xfin", xres[:, :, :], [128, 4, D], [("x", i) for i in range(4)])
        final_norm_out(128, 4, lambda tt: O["y_p"][T0 + tt * 128:T0 + (tt + 1) * 128, :])
    for tcol in range(2):
        ps = S.psum()
        S.op("dve", lambda e, tcol=tcol: e.tensor_copy(stg[:, 0:NF], carry[:, :, tcol]), reads=["carry"], writes=["stg"])
        S.op("pe", lambda e, ps=ps: e.transpose(ps.ap[:NF, :128], stg[:, 0:NF], ident_f[:, :]), reads=["stg", "ident_f"], writes=[ps.key])
        S.op("dve", lambda e, ps=ps: e.tensor_copy(ytmp[:NF, 0:128], ps.ap[:NF, :128]), reads=[ps.key], writes=["ytmp"])
        S.dma("sp", O["p_conv"][tcol:tcol + 1, :].rearrange("t (f q) -> (t f) q", q=128), ytmp[:NF, 0:128], reads=["ytmp"], writes=[])

    n = NS
    S.op("dve", lambda e: e.memset(small[:, 63:64], 0.0), writes=["kT3", "V3", "kT12", "V1", "V2"])
    PA = kT3[:, :, :].bitcast(F32).rearrange("p a b -> p (a b)")
    PB = V3[:, :, :, :].bitcast(F32).rearrange("p a b c -> p (a b c)")
    z_tok = PA[:n, 0:2816]
    ZQ, ZK, ZV, ZM = 0, 768, 1536, 2304
    s0raw = PA[:n, 2816:2816 + 1024].rearrange("p (a b) -> p a b", b=512)
    PAb = PA[:, 3840:4096]
    zq_bf = PB[:n, 0:640].bitcast(BF16)
    vt_bf = PB[:n, 640:1024].bitcast(BF16)
    s0T = PB[:, 1024:1536].rearrange("p (r g b) -> p r g b", r=2, g=16)
    snw = PB[:, 1536:2048].rearrange("p (r g b) -> p r g b", r=2, g=16)
    snb = PB[:, 2048:2304].bitcast(BF16).rearrange("p (r g b) -> p r g b", r=2, g=16)
    scT = PB[:, 2304:2496].rearrange("p (b c) -> p b c", c=12)
    pTs = PB[:, 2496:2592].bitcast(BF16).rearrange("p (b c) -> p b c", c=12)
    scm = PB[:, 2592:2720].rearrange("p (c b h) -> p c b h", c=2, b=16)
    pmb = PB[:, 2720:2784].bitcast(BF16).rearrange("p (c b h) -> p c b h", c=2, b=16)
    kbuf = [PB[:, 2784 + i * 512:2784 + (i + 1) * 512] for i in range(2)]
    SK = ["skey%d" % i for i in range(12)]
    selb = sb("selb", [16, 16, 128], BF16)
    tabs_s = sb("tabs_s", [128, 12], F32)
    vbb = [sb("vbb%d" % i, [128, 512], BF16) for i in range(2)]
    prd = sb("prd", [128, 512], F32)
    cbT = sb("cbT", [128, NF, 2, NS], F32)
    anT = sb("anT", [128, NF, NS], F32)
    S.dma("pool", selb[:, :, :], C["sel"].rearrange("p (a b) -> p a b", b=128), writes=["selb"])
    S.dma("sp", tabs_s[:], C["tabs_s"], writes=["tabs_s"])

    S.dma("sp", xres[:n, 0, :], I["xs"], writes=[("x", 0)])
    norm_to_hT("norm1_g", 1, n, hT)
    hr = lambda k: hT[:, k, :n]
    def ucb(m, ps):
        evac(ubf[:, m, :n], ps, 128, n, [], ["ubf"])
    dense_fm("w_in", 0, 512, 8, 128, hr, ["hT"], n, ucb)
    for ci, c0 in enumerate(range(OFF_Q, OFF_G, 512)):
        w = min(512, OFF_G - c0)
        wt, wk = wload([(WB["w_in"], "w_in_bf", 0, c0, w, 0)], 8, 128)
        ps = S.psum()
        mm(ps, n, w, lambda k: hT[:, k, :n], lambda k, wt=wt, w=w: wt[:, k, :w], 8, [wk, "hT"])
        S.op("dve", lambda e, ps=ps, ci=ci, w=w: e.tensor_copy(z_tok[:, ci * 512:ci * 512 + w], ps.ap[:n, :w]), reads=[ps.key], writes=["z_tok"])
    for g in range(3):
        S.dma("sp", O["s_k%d" % (g + 1)], z_tok[:, ZK + g * 256:ZK + (g + 1) * 256], reads=["z_tok"], writes=[])
        S.dma("sp", O["s_v%d" % (g + 1)], z_tok[:, ZV + g * 256:ZV + (g + 1) * 256], reads=["z_tok"], writes=[])
    S.op("dve", lambda e: e.tensor_copy(zq_bf[:, 0:768], z_tok[:, ZQ:ZQ + 768]), reads=["z_tok"], writes=["zq_bf"])
    S.op("dve", lambda e: e.tensor_copy(zq_bf[:, 768:1280], z_tok[:, ZM:ZM + 512]), reads=["z_tok"], writes=["zq_bf"])
    S.op("dve", lambda e: e.tensor_copy(vt_bf[:, :], z_tok[:, ZV:ZV + 768]), reads=["z_tok"], writes=["vt_bf"])

    for ri, nm in ((0, "sre"), (1, "sim")):
        for q4 in range(4):
            S.dma("sp", s0raw[:, q4 % 2, :], I[nm][:, q4 * 512:(q4 + 1) * 512], writes=["s0raw%d" % (q4 % 2)])
            ps = S.psum()

            def trs(e, ps=ps, q4=q4):
                ins = None
                for gg in range(4):
                    ins = e.transpose(ps.ap[:, gg * 16:(gg + 1) * 16], s0raw[:, q4 % 2, gg * 128:(gg + 1) * 128], ident_f[:n, :n])
                return ins
            S.op("pe", trs, reads=["s0raw%d" % (q4 % 2), "ident_f"], writes=[ps.key])
            S.op("dve", lambda e, ps=ps, ri=ri, q4=q4: e.tensor_copy(s0T[:, ri, 4 * q4:4 * q4 + 4, :],
                                                                       ps.ap[:, 0:64].rearrange("p (g b) -> p g b", b=16)),
                 reads=[ps.key], writes=["s0T"])
    pbr, pbi = S.psum(), S.psum()
    for ri, pz in ((0, pbr), (1, pbi)):
        def bufn(e, ri=ri, pz=pz):
            ins = None
            for gp in range(16):
                ins = e.matmul(pz.ap[:, gp * 16:(gp + 1) * 16], bbT[:, gp, ri, :], ubf[:, gp // 4, :n], start=True, stop=True)
            return ins
        S.op("pe", bufn, reads=["bbT", "ubf"], writes=[pz.key])
    abr = ab[:, 0, :].unsqueeze(2).to_broadcast([128, 16, 16])
    abi = ab[:, 1, :].unsqueeze(2).to_broadcast([128, 16, 16])
    t1 = af[:, 0, 0:256].rearrange("p (g b) -> p g b", b=16)
    t2_ = af[:, 1, 0:256].rearrange("p (g b) -> p g b", b=16)
    pv3 = lambda pz: pz.ap[:, 0:256].rearrange("p (g b) -> p g b", b=16)
    dv = lambda fn, rk, wk_: S.op("dve", fn, reads=rk, writes=wk_)
    dv(lambda e: e.tensor_tensor(t1, s0T[:, 0], abr, ALU.mult), ["s0T", "ab"], [AFK(0)])
    dv(lambda e: e.tensor_tensor(t2_, s0T[:, 1], abi, ALU.mult), ["s0T", "ab"], [AFK(1)])
    dv(lambda e: e.tensor_tensor(t1, t1, t2_, ALU.subtract), [AFK(0), AFK(1)], [AFK(0)])
    dv(lambda e: e.tensor_tensor(snw[:, 0], t1, pv3(pbr), ALU.add), [AFK(0), pbr.key], ["snw"])
    dv(lambda e: e.tensor_tensor(t1, s0T[:, 1], abr, ALU.mult), ["s0T", "ab", "snw"], [AFK(0)])
    dv(lambda e: e.tensor_tensor(t2_, s0T[:, 0], abi, ALU.mult), ["s0T", "ab", "snw"], [AFK(1)])
    dv(lambda e: e.tensor_tensor(t1, t1, t2_, ALU.add), [AFK(0), AFK(1)], [AFK(0)])
    dv(lambda e: e.tensor_tensor(snw[:, 1], t1, pv3(pbi), ALU.add), [AFK(0), pbi.key], ["snw"])
    dv(lambda e: e.tensor_copy(snb[:, :, :, :], snw[:, :, :, :]), ["snw"], ["snb"])
    for ri, oname in ((0, "s_sre"), (1, "s_sim")):
        for q4 in range(4):
            ps = S.psum()

            def trb(e, ps=ps, ri=ri, q4=q4):
                ins = None
                for gg in range(4):
                    ins = e.transpose(ps.ap[:n, gg * 128:(gg + 1) * 128], snw[:, ri, 4 * q4 + gg, :], ident_f[:, :])
                return ins
            S.op("pe", trb, reads=["snw", "ident_f"], writes=[ps.key])
            S.op("dve", lambda e, ps=ps: e.tensor_copy(stg[:n, :], ps.ap[:n, :512]), reads=[ps.key], writes=["stg"])
            S.dma("sp", O[oname][:, q4 * 512:(q4 + 1) * 512], stg[:n, :], reads=["stg"], writes=[])
    for t in range(4):
        yb = YB[t % 2]

        def ysf(e, t=t, yb=yb):
            ins = None
            for gi in range(4):
                gp = 4 * t + gi
                e.matmul(yb.ap[:, :n], cT[:, gp, 0, :], snb[:, 0, gp, :], start=(gi == 0), stop=False)
                e.matmul(yb.ap[:, :n], cT[:, gp, 1, :], snb[:, 1, gp, :], start=False, stop=False)
            return e.matmul(yb.ap[:, :n], dD[:, t, :], ubf[:, t, :n], start=False, stop=True)
        S.op("pe", ysf, reads=["cT", "snb", "dD", "ubf"], writes=[yb.key])
        S.op("act", lambda e, t=t, yb=yb: e.activation(zT[:, t, :n], yb.ap[:, :n], AF.Gelu_apprx_tanh), reads=[yb.key], writes=["zT"])
    gates_to_arena(0, hr, ["hT"], n)
    glu_branch(zT, n)
    dbg("s_zT", zT[:, :, :n], [128, 4, n], ["zT"], BF16)
    dbg("s_m1", arena[:, 8:16, :n], [128, 8, n], [AR(8 + i) for i in range(8)], BF16)

    caches = [("ck1", "cv1"), ("ck2", "cv2"), ("ck3", "cv3")]
    for b in range(n):
        pq0, pq1 = S.psum(), S.psum()
        S.op("pe", lambda e, b=b, pq0=pq0: e.matmul(pq0.ap[:, :512], selb[:, b, :], zq_bf[:, 0:512], start=True, stop=True),
             reads=["selb", "zq_bf"], writes=[pq0.key])
        S.op("pe", lambda e, b=b, pq1=pq1: e.matmul(pq1.ap[:, :256], selb[:, b, :], zq_bf[:, 512:768], start=True, stop=True),
             reads=["selb", "zq_bf"], writes=[pq1.key])
        for g in range(3):
            kb, kk = kbuf[g % 2], "kbuf%d" % (g % 2)
            S.dma("sp", kb[:, 0:256], I[caches[g][0]][b].rearrange("(i d) c -> i d c", d=DILS[g])[:, 0, :], writes=[kk])
            qsrc = pq0.ap[:, g * 256:(g + 1) * 256] if g < 2 else pq1.ap[:, 0:256]
            S.op("dve", lambda e, kb=kb, qsrc=qsrc: e.tensor_tensor(prd[:, 0:256], kb[:, 0:256], qsrc, ALU.mult),
                 reads=[kk, pq0.key, pq1.key], writes=["prd"])
            S.op("dve", lambda e, b=b, g=g: e.tensor_reduce(scT[:, b, g * 4:(g + 1) * 4], prd[:, 0:256].rearrange("p (h e) -> p h e", e=64),
                                                           AX.X, ALU.add), reads=["prd"], writes=["scT"])
    S.op("dve", lambda e: e.scalar_tensor_tensor(scT[:, :, :], scT[:, :, :], 0.125, tabs_s[:, :].unsqueeze(1).to_broadcast([128, 16, 12]),
                                                  ALU.mult, ALU.add), reads=["scT", "tabs_s"], writes=["scT"])
    dbg("s_scT", scT[:, :, :], [128, 16, 12], ["scT"])
    S.op("act", lambda e: e.activation(pTs[:, :, :], scT[:, :, :], AF.Exp), reads=["scT"], writes=["pTs"])
    pnew = small[:n, 40:52]
    S.op("dve", lambda e: e.tensor_tensor(prd[:n, 0:384].rearrange("p (a b) -> p a b", b=1)[:, :, 0] if False else PAb[:n, 0:1], PAb[:n, 0:1], PAb[:n, 0:1], ALU.mult) if False else
         e.tensor_tensor(stg[:n, 0:512], z_tok[:, ZQ:ZQ + 512], z_tok[:, ZK:ZK + 512], ALU.mult), reads=["z_tok"], writes=["stg"])
    S.op("dve", lambda e: e.tensor_reduce(pnew[:, 0:8], stg[:n, 0:512].rearrange("p (h e) -> p h e", e=64), AX.X, ALU.add),
         reads=["stg"], writes=["pnew"])
    S.op("dve", lambda e: e.tensor_tensor(stg[:n, 0:256], z_tok[:, ZQ + 512:ZQ + 768], z_tok[:, ZK + 512:ZK + 768], ALU.mult),
         reads=["z_tok", "pnew"], writes=["stg"])
    S.op("dve", lambda e: e.tensor_reduce(pnew[:, 8:12], stg[:n, 0:256].rearrange("p (h e) -> p h e", e=64), AX.X, ALU.add),
         reads=["stg"], writes=["pnew"])
    S.op("act", lambda e: e.activation(pnew, pnew, AF.Exp, scale=0.125), reads=["pnew"], writes=["pnew"])
    Dm = sb("Dm", [16, 12, 16], BF16)
    for c in range(12):
        S.op("dve", lambda e, c=c: e.tensor_scalar_mul(Dm[:, c, :], ident_f[:n, :n], pnew[:, c:c + 1]), reads=["pnew", "ident_f"], writes=["Dm"])
    dbg("s_pnew", pnew, [n, 12], ["pnew"])
    pnum, pden = S.psum(), S.psum()
    S.op("pe", lambda e: e.matmul(pnum.ap[:64, 0:64], ones_bf[:, 0:64], zer_bf[:, 0:64], start=True, stop=False),
         reads=["ones_bf", "zer_bf"], writes=[pnum.key])
    for b in range(n):
        for g in range(3):
            kb, kk = kbuf[g % 2], "kbuf%d" % (g % 2)
            vb, vk = vbb[g % 2], "vbb%d" % (g % 2)
            S.dma("sp", kb[:, 0:256], I[caches[g][1]][b].rearrange("(i d) c -> i d c", d=DILS[g])[:, 0, :], writes=[kk])
            S.op("act", lambda e, kb=kb, vb=vb: e.activation(vb[:, 0:256], kb[:, 0:256], AF.Copy), reads=[kk], writes=[vk])

            def pvs(e, b=b, g=g, vb=vb):
                ins = None
                for j in range(4):
                    ins = e.matmul(pnum.ap[:64, j * 16 + b:j * 16 + b + 1], vb[:, j * 64:(j + 1) * 64], pTs[:, b, g * 4 + j:g * 4 + j + 1],
                                   start=False, stop=False)
                return ins
            S.op("pe", pvs, reads=[vk, "pTs"], writes=[pnum.key])

    def pvnew(e):
        ins = None
        for j in range(4):
            for g in range(3):
                ins = e.matmul(pnum.ap[:64, j * 16:(j + 1) * 16], vt_bf[:, g * 256 + j * 64:g * 256 + (j + 1) * 64], Dm[:, g * 4 + j, :],
                               start=False, stop=(g == 2 and j == 3))
        return ins
    S.op("pe", pvnew, reads=["vt_bf", "Dm"], writes=[pnum.key])

    def dens(e):
        ins = None
        for j in range(4):
            for g in range(3):
                e.matmul(pden.ap[:64, j * 16:(j + 1) * 16], ones_bf[:, 0:64], pTs[:, :, g * 4 + j], start=(g == 0), stop=False)
            for g in range(3):
                ins = e.matmul(pden.ap[:64, j * 16:(j + 1) * 16], ones_bf[:n, 0:64], Dm[:, g * 4 + j, :], start=False, stop=(g == 2))
        return ins
    S.op("pe", dens, reads=["pTs", "Dm", "ones_bf"], writes=[pden.key])
    S.op("dve", lambda e: e.tensor_copy(stg[:64, 0:64], pden.ap[:64, 0:64]), reads=[pden.key], writes=["stg"])
    S.op("dve", lambda e: e.tensor_copy(stg[:64, 64:128], pnum.ap[:64, 0:64]), reads=[pnum.key, "stg"], writes=["stg"])
    dbg("s_dn", stg[:64, 0:128], [64, 128], ["stg"])
    S.op("dve", lambda e: e.reciprocal(af[:64, 0, 0:64], pden.ap[:64, 0:64]), reads=[pden.key], writes=[AFK(0)])
    S.op("dve", lambda e: e.tensor_tensor(oT[:, :, :n], pnum.ap[:64, 0:64].rearrange("p (j b) -> p j b", b=16),
                                          af[:64, 0, 0:64].rearrange("p (j b) -> p j b", b=16), ALU.mult),
         reads=[pnum.key, AFK(0)], writes=["oT"])
    gates_to_arena(1, hr, ["hT"], n)
    dense_fm("w_atto", 0, D, 4, 64, lambda k: oT[:, k, :n], ["oT"], n, merge_cb(False, n))
    dbg("s_oT", oT[:, :, :n], [64, 4, n], ["oT"], BF16)
    dbg("s_m2", arena[:, 8:16, :n], [128, 8, n], [AR(8 + i) for i in range(8)], BF16)

    for b in range(n):
        pq = S.psum()
        S.op("pe", lambda e, b=b, pq=pq: e.matmul(pq.ap[:, :512], selb[:, b, :], zq_bf[:, 768:1280], start=True, stop=True),
             reads=["selb", "zq_bf"], writes=[pq.key])
        for c in range(2):
            kb, kk = kbuf[c], "kbuf%d" % c
            S.dma("sp", kb[:, :], I["cmk"][b, c * 128:(c + 1) * 128, :], writes=[kk])
            S.op("dve", lambda e, kb=kb, pq=pq: e.tensor_tensor(prd[:, :], kb[:, :], pq.ap[:, :512], ALU.mult), reads=[kk, pq.key], writes=["prd"])
            S.op("dve", lambda e, b=b, c=c: e.tensor_reduce(scm[:, c, b, :], prd[:, :].rearrange("p (h e) -> p h e", e=128), AX.X, ALU.add),
                 reads=["prd"], writes=["scm"])
    S.op("act", lambda e: e.activation(pmb[:, :, :, :], scm[:, :, :, :], AF.Exp, scale=SC_MEM), reads=["scm"], writes=["pmb"])
    pnm, pdm = S.psum(), S.psum()
    S.op("pe", lambda e: e.matmul(pnm.ap[:, 0:64], ones_bf[:, :], zer_bf[:, 0:64], start=True, stop=False),
         reads=["ones_bf", "zer_bf"], writes=[pnm.key])
    for b in range(n):
        for c in range(2):
            kb, kk = kbuf[c], "kbuf%d" % c
            vb, vk = vbb[c], "vbb%d" % c
            S.dma("sp", kb[:, :], I["cmv"][b, c * 128:(c + 1) * 128, :], writes=[kk])
            S.op("act", lambda e, kb=kb, vb=vb: e.activation(vb[:, :], kb[:, :], AF.Copy), reads=[kk], writes=[vk])

            def pvm(e, b=b, c=c, vb=vb):
                ins = None
                for h in range(4):
                    ins = e.matmul(pnm.ap[:, h * 16 + b:h * 16 + b + 1], vb[:, h * 128:(h + 1) * 128], pmb[:, c, b, h:h + 1],
                                   start=False, stop=(c == 1 and b == n - 1 and h == 3))
                return ins
            S.op("pe", pvm, reads=[vk, "pmb"], writes=[pnm.key])

    def denm(e):
        ins = None
        for h in range(4):
            for c in range(2):
                ins = e.matmul(pdm.ap[:, h * 16:(h + 1) * 16], ones_bf[:, :], pmb[:, c, :, h], start=(c == 0), stop=(c == 1))
        return ins
    S.op("pe", denm, reads=["pmb", "ones_bf"], writes=[pdm.key])
    S.op("dve", lambda e: e.reciprocal(af[:, 0, 0:64], pdm.ap[:, 0:64]), reads=[pdm.key], writes=[AFK(0)])
    S.op("dve", lambda e: e.tensor_tensor(omT[:, :, :n], pnm.ap[:, 0:64].rearrange("p (h b) -> p h b", b=16),
                                          af[:, 0, 0:64].rearrange("p (h b) -> p h b", b=16), ALU.mult),
         reads=[pnm.key, AFK(0)], writes=["omT"])
    gates_to_arena(2, hr, ["hT"], n)
    dense_fm("w_memo", 0, D, 4, 128, lambda k: omT[:, k, :n], ["omT"], n, merge_cb(False, n))
    dbg("s_omT", omT[:, :, :n], [128, 4, n], ["omT"], BF16)
    dbg("s_m3", arena[:, 8:16, :n], [128, 8, n], [AR(8 + i) for i in range(8)], BF16)

    out_proj_tm("w_out", 8, lambda tt, k: arena[:, 8 + k, 0:n], [AR(8 + k) for k in range(8)], n)
    norm_to_hT("norm2_g", 1, n, hT)
    for tcol in range(2):
        crow = PA[:n, 0:2816]
        S.dma("sp", crow, I["sconv"][:, tcol, :], reads=[], writes=["z_tok"])
        for f0 in range(0, NF, 8):
            nf = min(8, NF - f0)
            ps = S.psum()

            def trc2(e, ps=ps, f0=f0, nf=nf, crow=crow):
                ins = None
                for fi in range(nf):
                    ins = e.transpose(ps.ap[:, fi * 16:(fi + 1) * 16], crow[:, (f0 + fi) * 128:(f0 + fi + 1) * 128], ident_f[:n, :n])
                return ins
            S.op("pe", trc2, reads=["z_tok", "ident_f"], writes=[ps.key])
            S.op("dve", lambda e, ps=ps, f0=f0, nf=nf, tcol=tcol: e.tensor_copy(
                cbT[:, f0:f0 + nf, tcol, :], ps.ap[:, 0:nf * 16].rearrange("p (f b) -> p f b", b=16)), reads=[ps.key], writes=["cbT"])
        if tcol == 1:
            S.dma("sp", O["s_conv"][:, 0, :], crow, reads=["z_tok"], writes=[])

    def conv_hook_s(f, pa):
        cc = af[:, 1, 0:n]
        S.op("dve", lambda e, f=f, pa=pa: e.tensor_copy(anT[:, f, :], pa.ap[:, :n]), reads=[pa.key], writes=["anT"])
        S.op("dve", lambda e, f=f: e.tensor_scalar(cc, anT[:, f, :], cwT[:, f, 2:3], cwT[:, f, 3:4], ALU.mult, ALU.add),
             reads=["anT", "cwT"], writes=[AFK(1)])
        S.op("dve", lambda e, f=f: e.scalar_tensor_tensor(cc, cbT[:, f, 1, :], cwT[:, f, 1:2], cc, ALU.mult, ALU.add),
             reads=["cbT", AFK(1), "cwT"], writes=[AFK(1)])
        S.op("dve", lambda e, f=f: e.scalar_tensor_tensor(cc, cbT[:, f, 0, :], cwT[:, f, 0:1], cc, ALU.mult, ALU.add),
             reads=["cbT", AFK(1), "cwT"], writes=[AFK(1)])
    ffn(n, conv_hook_s)
    for f0 in range(0, NF, 4):
        nf = min(4, NF - f0)
        ps = S.psum()

        def tra(e, ps=ps, f0=f0, nf=nf):
            ins = None
            for fi in range(nf):
                ins = e.transpose(ps.ap[:n, fi * 128:(fi + 1) * 128], anT[:, f0 + fi, :], ident_f[:, :])
            return ins
        S.op("pe", tra, reads=["anT", "ident_f"], writes=[ps.key])
        S.op("dve", lambda e, ps=ps, nf=nf: e.tensor_copy(stg[:n, 0:nf * 128], ps.ap[:n, 0:nf * 128]), reads=[ps.key], writes=["stg"])
        S.dma("sp", O["s_conv"][:, 1, f0 * 128:(f0 + nf) * 128], stg[:n, 0:nf * 128], reads=["stg"], writes=[])
    out_proj_tm("w_down", NF, lambda tt, k: arena[:, k, 0:n], [AR(k) for k in range(NF)], n)
    final_norm_out(n, 1, lambda tt: O["y_s"])

    S.emit()
    return nc


def kernel(**inp):
    f = lambda a: np.ascontiguousarray(np.asarray(a, dtype=np.float32))
    consts = host_consts()
    shared = {
        "norm1_g": f(inp["norm1_g"]), "w_in": f(inp["w_in"][0]),
        "a_re": f(inp["ssm_a_re"][0]).reshape(16, 128), "a_im": f(inp["ssm_a_im"][0]).reshape(16, 128),
        "log_dt": f(inp["ssm_log_dt"][0]).reshape(16, 2),
        "b_re": f(inp["ssm_b_re"][0]).reshape(2048, 16), "b_im": f(inp["ssm_b_im"][0]).reshape(2048, 16),
        "c_re": f(inp["ssm_c_re"][0]).reshape(512, 64), "c_im": f(inp["ssm_c_im"][0]).reshape(512, 64),
        "ssm_d": f(inp["ssm_d"][0]).reshape(4, 128), "w_glu": f(inp["w_ssm_glu"][0]), "w_atto": f(inp["w_att_o"][0]),
        "mem_g": f(inp["mem_norm_g"]), "w_memkv": f(inp["w_mem_kv"][0]), "w_memo": f(inp["w_mem_o"][0]),
        "w_out": f(inp["w_out"][0]), "norm2_g": f(inp["norm2_g"]), "w_up": f(inp["w_up"][0]),
        "conv_w": f(inp["ffn_conv_w"][0]), "conv_b": f(inp["ffn_conv_b"]), "w_down": f(inp["w_down"][0]),
        "fin_g": f(inp["final_norm_g"]).reshape(1, D),
    }
    shared.update(consts)
    in_maps = []
    for c in range(8):
        s, b0 = c % 4, c * NS
        m = dict(shared)
        m["xp"] = f(inp["x_prompt"][s])
        m["memp"] = f(inp["mem_prompt"][s])
        m["xs"] = f(inp["x_sample"][b0:b0 + NS, 0])
        m["sre"] = f(inp["state_ssm_re"][0, b0:b0 + NS]).reshape(NS, 2048)
        m["sim"] = f(inp["state_ssm_im"][0, b0:b0 + NS]).reshape(NS, 2048)
        caches = {"ck1": inp["cache_w1_k"], "cv1": inp["cache_w1_v"], "ck2": inp["cache_w2_k"], "cv2": inp["cache_w2_v"],
                  "ck3": inp["cache_w3_k"], "cv3": inp["cache_w3_v"]}
        for nm, arr in caches.items():
            a = np.asarray(arr)[0, b0:b0 + NS]
            m[nm] = f(a).reshape(NS, a.shape[1], 256)
        m["cmk"] = f(inp["cache_mem_k"][0, b0:b0 + NS]).reshape(NS, 256, 512)
        m["cmv"] = f(inp["cache_mem_v"][0, b0:b0 + NS]).reshape(NS, 256, 512)
        m["sconv"] = f(inp["state_ffn_conv"][0, b0:b0 + NS])
        in_maps.append(m)
    nc = build_nc()
    res = run_bass_kernel_spmd(nc, in_maps, core_ids=list(range(8)))
    R = res.results
    cat = lambda name, cores: np.stack([np.asarray(R[c][name], dtype=np.float32) for c in cores], axis=0)
    P = range(4)
    A = range(8)
    sm = lambda name, shape: np.concatenate([np.asarray(R[c][name], np.float32) for c in A], axis=0).reshape(shape)[None]
    outs = (
        cat("y_p", P), sm("y_s", (128, 1, D)),
        cat("p_sre", P).reshape(4, 32, 64)[None], cat("p_sim", P).reshape(4, 32, 64)[None],
        cat("p_k1", P).reshape(4, 128, 4, 64)[None], cat("p_v1", P).reshape(4, 128, 4, 64)[None],
        cat("p_k2", P).reshape(4, 512, 4, 64)[None], cat("p_v2", P).reshape(4, 512, 4, 64)[None],
        cat("p_k3", P).reshape(4, 2048, 4, 64)[None], cat("p_v3", P).reshape(4, 2048, 4, 64)[None],
        cat("p_mk", P).reshape(4, 256, 4, 128)[None], cat("p_mv", P).reshape(4, 256, 4, 128)[None],
        cat("p_conv", P)[None],
        sm("s_sre", (128, 32, 64)), sm("s_sim", (128, 32, 64)),
        sm("s_k1", (128, 1, 4, 64)), sm("s_v1", (128, 1, 4, 64)), sm("s_k2", (128, 1, 4, 64)), sm("s_v2", (128, 1, 4, 64)),
        sm("s_k3", (128, 1, 4, 64)), sm("s_v3", (128, 1, 4, 64)), sm("s_conv", (128, 2, DFF)),
    )
    outs = list(outs)
    outs[1] = outs[1][0]
    return tuple(np.ascontiguousarray(o) for o in outs)
```
